# Optimizing a Trainium2 kernel written in Bass

```python
import math
import jax, jax.numpy as jnp
from jax import lax
import numpy as np

D_MODEL = 2048
BATCH = 4
SEQ = 2048
DEPTH = 2
DEC_BATCH = 128
DEC_SEQ = 1
PAST_LEN = 16384
PAGE_SIZE = 128

S5_WIDTH = D_MODEL // 2
S5_GROUP = 16
S5_GROUPS = S5_WIDTH // S5_GROUP
S5_STATE = 64
GLA_HEADS = 4
GLA_DK = D_MODEL // 16
GLA_DV = D_MODEL // 8
GLA_RANK = 16
GLA_GATE_NORM = 16.0
RET_HEADS = 8
RET_DK = D_MODEL // 16
RET_DV = D_MODEL // 16
ROPE_BASE = 10000.0
CHUNK = 64
N_BRANCH = 3
D_FF = -(-8 * D_MODEL // (3 * 256)) * 256
EPS = 1e-6
SPLITS = (S5_WIDTH,
          GLA_HEADS * GLA_DK, GLA_HEADS * GLA_DK, GLA_HEADS * GLA_DV, GLA_HEADS * GLA_DV, GLA_RANK,
          RET_HEADS * RET_DK, RET_HEADS * RET_DK, RET_HEADS * RET_DV, RET_HEADS * RET_DV,
          N_BRANCH * D_MODEL)
W_IN = sum(SPLITS)

kernel_name = 'hybrid_s5_gla_retnet_step'

F32 = jnp.float32


def rmsnorm(x, gain):
    xf = x.astype(F32)
    y = xf * lax.rsqrt(jnp.mean(xf * xf, axis=-1, keepdims=True) + EPS)
    return (y * gain.astype(F32)).astype(x.dtype)


def rotary(x, pos):
    half = x.shape[-1] // 2
    inv = 1.0 / (ROPE_BASE ** jnp.linspace(0.0, 1.0, half, dtype=F32))
    ang = pos[:, None] * inv[None, :]
    cos = jnp.cos(ang)[None, :, None, :]
    sin = jnp.sin(ang)[None, :, None, :]
    xp = x.reshape(x.shape[:-1] + (half, 2))
    xe, xo = xp[..., 0], xp[..., 1]
    return jnp.stack([xe * cos - xo * sin, xo * cos + xe * sin], axis=-1).reshape(x.shape)


def _complex_affine_combine(e1, e2):
    a1r, a1i, b1r, b1i = e1
    a2r, a2i, b2r, b2i = e2
    return (a2r * a1r - a2i * a1i, a2r * a1i + a2i * a1r,
            a2r * b1r - a2i * b1i + b2r, a2r * b1i + a2i * b1r + b2i)


def s5_scan(u, x0_re, x0_im, a_re, a_im, log_dt, b_re, b_im, c_re, c_im, d_skip):
    bsz, length, _ = u.shape
    ug = u.reshape(bsz, length, S5_GROUPS, S5_GROUP)
    dt = jnp.exp(log_dt.astype(F32))[:, None]
    ar, ai = a_re.astype(F32), a_im.astype(F32)
    mag = jnp.exp(ar * dt)
    abr, abi = mag * jnp.cos(ai * dt), mag * jnp.sin(ai * dt)
    den = ar * ar + ai * ai
    nr, ni = abr - 1.0, abi
    fr, fi = (nr * ar + ni * ai) / den, (ni * ar - nr * ai) / den
    br, bi = b_re.astype(F32), b_im.astype(F32)
    bbr = fr[..., None] * br - fi[..., None] * bi
    bbi = fr[..., None] * bi + fi[..., None] * br
    xr_in = jnp.einsum('blgn,gpn->blgp', ug, bbr)
    xi_in = jnp.einsum('blgn,gpn->blgp', ug, bbi)
    x0r, x0i = x0_re.astype(F32), x0_im.astype(F32)
    xr_in = xr_in.at[:, 0].add(abr * x0r - abi * x0i)
    xi_in = xi_in.at[:, 0].add(abr * x0i + abi * x0r)
    a_r = jnp.broadcast_to(abr, xr_in.shape)
    a_i = jnp.broadcast_to(abi, xi_in.shape)
    _, _, xr, xi = lax.associative_scan(_complex_affine_combine, (a_r, a_i, xr_in, xi_in), axis=1)
    y = (jnp.einsum('blgp,gnp->blgn', xr, c_re.astype(F32))
         - jnp.einsum('blgp,gnp->blgn', xi, c_im.astype(F32))
         + d_skip.astype(F32).reshape(S5_GROUPS, S5_GROUP) * ug)
    return y.reshape(bsz, length, S5_WIDTH), xr[:, -1], xi[:, -1]


def gated_linear_recurrence(q, k, v, g, s0):
    bsz, length, heads, dk = q.shape
    dv = v.shape[-1]
    c = CHUNK if length % CHUNK == 0 else length
    n = length // c

    def to_chunks(a):
        return a.reshape(bsz, n, c, heads, a.shape[-1]).transpose(1, 0, 3, 2, 4)

    mask = jnp.tril(jnp.ones((c, c), dtype=bool))[:, :, None]

    def step(s, inp):
        qi, ki, vi, gi = inp
        b = jnp.cumsum(gi, axis=2)
        diff = b[:, :, :, None, :] - b[:, :, None, :, :]
        decay = jnp.exp(jnp.where(mask, diff, -jnp.inf))
        scores = jnp.einsum('bhtk,bhsk,bhtsk->bhts', qi, ki, decay)
        o = (jnp.einsum('bhts,bhsv->bhtv', scores, vi)
             + jnp.einsum('bhtk,bhkv->bhtv', qi * jnp.exp(b), s))
        b_last = b[:, :, -1:, :]
        s_new = (jnp.exp(b_last[:, :, 0, :])[..., None] * s
                 + jnp.einsum('bhsk,bhsv->bhkv', ki * jnp.exp(b_last - b), vi))
        return s_new, o

    s_fin, o = lax.scan(step, s0.astype(F32), tuple(map(to_chunks, (q, k, v, g))))
    o = o.transpose(1, 0, 3, 2, 4).reshape(bsz, length, heads, dv)
    return o, s_fin


def trunk_layer(x, pos, st_s5_re, st_s5_im, st_gla, st_ret,
                norm_mix, w_in, s5_a_re, s5_a_im, s5_log_dt, s5_b_re, s5_b_im, s5_c_re, s5_c_im,
                s5_d, s5_w_glu, s5_b_glu, s5_w_out, gla_w_gate, gla_b_gate, gla_norm, gla_w_out,
                ret_w_out, w_mix_out, norm_ffn, w_ffn_gate, w_ffn_up, w_ffn_down):
    bsz, length, _ = x.shape
    h = rmsnorm(x, norm_mix)
    proj = jnp.einsum('bld,de->ble', h, w_in).astype(F32)
    cuts = np.cumsum(SPLITS)[:-1].tolist()
    u, gq, gk, gv, gr, glr, rq, rk, rv, rg, mg = jnp.split(proj, cuts, axis=-1)

    y5, new_s5_re, new_s5_im = s5_scan(u, st_s5_re, st_s5_im, s5_a_re, s5_a_im, s5_log_dt,
                                       s5_b_re, s5_b_im, s5_c_re, s5_c_im, s5_d)
    y5 = jax.nn.gelu(y5)
    y5 = y5 * jax.nn.sigmoid(jnp.einsum('blc,ce->ble', y5, s5_w_glu.astype(F32)) + s5_b_glu.astype(F32))
    br_s5 = jnp.einsum('blc,cd->bld', y5, s5_w_out.astype(F32))

    q = gq.reshape(bsz, length, GLA_HEADS, GLA_DK) * (GLA_DK ** -0.5)
    k = gk.reshape(bsz, length, GLA_HEADS, GLA_DK)
    v = gv.reshape(bsz, length, GLA_HEADS, GLA_DV)
    logit = jnp.einsum('blr,re->ble', glr, gla_w_gate.astype(F32)) + gla_b_gate.astype(F32)
    g = (jax.nn.log_sigmoid(logit) / GLA_GATE_NORM).reshape(bsz, length, GLA_HEADS, GLA_DK)
    o, new_gla = gated_linear_recurrence(q, k, v, g, st_gla)
    o = o * lax.rsqrt(jnp.mean(o * o, axis=-1, keepdims=True) + EPS) * gla_norm.astype(F32)
    o = o.reshape(bsz, length, GLA_HEADS * GLA_DV) * jax.nn.silu(gr)
    br_gla = jnp.einsum('blc,cd->bld', o, gla_w_out.astype(F32))

    q = rotary(rq.reshape(bsz, length, RET_HEADS, RET_DK), pos)
    k = rotary(rk.reshape(bsz, length, RET_HEADS, RET_DK), pos) * (RET_DK ** -0.5)
    v = rv.reshape(bsz, length, RET_HEADS, RET_DV)
    log_gamma = jnp.log1p(-jnp.power(2.0, -5.0 - jnp.arange(RET_HEADS, dtype=F32)))
    gdec = jnp.broadcast_to(log_gamma[None, None, :, None], (bsz, length, RET_HEADS, RET_DK))
    o, new_ret = gated_linear_recurrence(q, k, v, gdec, st_ret)
    mu = jnp.mean(o, axis=-1, keepdims=True)
    o = (o - mu) * lax.rsqrt(jnp.mean(jnp.square(o - mu), axis=-1, keepdims=True) + EPS)
    o = o.reshape(bsz, length, RET_HEADS * RET_DV) * jax.nn.silu(rg)
    br_ret = jnp.einsum('blc,cd->bld', o, ret_w_out.astype(F32))

    gates = jax.nn.sigmoid(mg).reshape(bsz, length, N_BRANCH, D_MODEL)
    merged = gates[:, :, 0] * br_s5 + gates[:, :, 1] * br_gla + gates[:, :, 2] * br_ret
    x = x + jnp.einsum('bld,de->ble', merged, w_mix_out.astype(F32)).astype(x.dtype)

    h2 = rmsnorm(x, norm_ffn)
    ff = jax.nn.silu(jnp.einsum('bld,df->blf', h2, w_ffn_gate)) * jnp.einsum('bld,df->blf', h2, w_ffn_up)
    x = x + jnp.einsum('blf,fd->bld', ff, w_ffn_down).astype(x.dtype)
    dt = x.dtype
    return (x, new_s5_re.astype(dt), new_s5_im.astype(dt), new_gla.astype(dt), new_ret.astype(dt))


def setup_inputs(seed: int = 0) -> dict:
    key = jax.random.key(seed)
    ks = jax.random.split(key, 30)

    def nrm(i, shape, scale):
        return jax.random.normal(ks[i], shape, F32) * scale

    glu_cols = GLA_HEADS * GLA_DK
    return {
        'x_prompt': nrm(0, (BATCH, SEQ, D_MODEL), 1.0),
        'x_sample': nrm(1, (DEC_BATCH, DEC_SEQ, D_MODEL), 1.0),
        'state_s5_re': nrm(2, (DEPTH, DEC_BATCH, S5_GROUPS, S5_STATE), 0.5),
        'state_s5_im': nrm(3, (DEPTH, DEC_BATCH, S5_GROUPS, S5_STATE), 0.5),
        'state_gla': nrm(4, (DEPTH, DEC_BATCH, GLA_HEADS, GLA_DK, GLA_DV), 1.0),
        'state_ret': nrm(5, (DEPTH, DEC_BATCH, RET_HEADS, RET_DK, RET_DV), 1.0),
        'norm_mix': 1.0 + nrm(6, (DEPTH, D_MODEL), 0.02),
        'w_in': nrm(7, (DEPTH, D_MODEL, W_IN), D_MODEL ** -0.5),
        's5_a_re': -0.5 + nrm(8, (DEPTH, S5_GROUPS, S5_STATE), 0.01),
        's5_a_im': jnp.pi * jnp.arange(S5_STATE, dtype=F32) + nrm(9, (DEPTH, S5_GROUPS, S5_STATE), 0.01),
        's5_log_dt': jax.random.uniform(ks[10], (DEPTH, S5_GROUPS), F32, minval=math.log(1e-3), maxval=math.log(1e-1)),
        's5_b_re': nrm(11, (DEPTH, S5_GROUPS, S5_STATE, S5_GROUP), (2 * S5_GROUP) ** -0.5),
        's5_b_im': nrm(12, (DEPTH, S5_GROUPS, S5_STATE, S5_GROUP), (2 * S5_GROUP) ** -0.5),
        's5_c_re': nrm(13, (DEPTH, S5_GROUPS, S5_GROUP, S5_STATE), (2 * S5_STATE) ** -0.5),
        's5_c_im': nrm(14, (DEPTH, S5_GROUPS, S5_GROUP, S5_STATE), (2 * S5_STATE) ** -0.5),
        's5_d': nrm(15, (DEPTH, S5_WIDTH), 1.0),
        's5_w_glu': nrm(16, (DEPTH, S5_WIDTH, S5_WIDTH), S5_WIDTH ** -0.5),
        's5_b_glu': nrm(17, (DEPTH, S5_WIDTH), 0.01),
        's5_w_out': nrm(18, (DEPTH, S5_WIDTH, D_MODEL), S5_WIDTH ** -0.5),
        'gla_w_gate': nrm(19, (DEPTH, GLA_RANK, glu_cols), GLA_RANK ** -0.5),
        'gla_b_gate': nrm(20, (DEPTH, glu_cols), 0.01),
        'gla_norm': 1.0 + nrm(21, (DEPTH, GLA_DV), 0.02),
        'gla_w_out': nrm(22, (DEPTH, GLA_HEADS * GLA_DV, D_MODEL), (GLA_HEADS * GLA_DV) ** -0.5),
        'ret_w_out': nrm(23, (DEPTH, RET_HEADS * RET_DV, D_MODEL), (RET_HEADS * RET_DV) ** -0.5),
        'w_mix_out': nrm(24, (DEPTH, D_MODEL, D_MODEL), D_MODEL ** -0.5),
        'norm_ffn': 1.0 + nrm(25, (DEPTH, D_MODEL), 0.02),
        'w_ffn_gate': nrm(26, (DEPTH, D_MODEL, D_FF), D_MODEL ** -0.5),
        'w_ffn_up': nrm(27, (DEPTH, D_MODEL, D_FF), D_MODEL ** -0.5),
        'w_ffn_down': nrm(28, (DEPTH, D_FF, D_MODEL), D_FF ** -0.5),
        'norm_final': 1.0 + nrm(29, (D_MODEL,), 0.02),
    }


def reference(x_prompt, x_sample, state_s5_re, state_s5_im, state_gla, state_ret,
              norm_mix, w_in, s5_a_re, s5_a_im, s5_log_dt, s5_b_re, s5_b_im, s5_c_re, s5_c_im,
              s5_d, s5_w_glu, s5_b_glu, s5_w_out, gla_w_gate, gla_b_gate, gla_norm, gla_w_out,
              ret_w_out, w_mix_out, norm_ffn, w_ffn_gate, w_ffn_up, w_ffn_down, norm_final):
    bp, lp, _ = x_prompt.shape
    pos_prompt = jnp.arange(lp, dtype=F32)
    pos_sample = PAST_LEN + jnp.arange(x_sample.shape[1], dtype=F32)
    zero_s5 = jnp.zeros((bp, S5_GROUPS, S5_STATE), F32)
    zero_gla = jnp.zeros((bp, GLA_HEADS, GLA_DK, GLA_DV), F32)
    zero_ret = jnp.zeros((bp, RET_HEADS, RET_DK, RET_DV), F32)
    hp, hs = x_prompt, x_sample
    p_s5r, p_s5i, p_gla, p_ret = [], [], [], []
    s_s5r, s_s5i, s_gla, s_ret = [], [], [], []
    for l in range(DEPTH):
        lw = (norm_mix[l], w_in[l], s5_a_re[l], s5_a_im[l], s5_log_dt[l], s5_b_re[l], s5_b_im[l],
              s5_c_re[l], s5_c_im[l], s5_d[l], s5_w_glu[l], s5_b_glu[l], s5_w_out[l], gla_w_gate[l],
              gla_b_gate[l], gla_norm[l], gla_w_out[l], ret_w_out[l], w_mix_out[l], norm_ffn[l],
              w_ffn_gate[l], w_ffn_up[l], w_ffn_down[l])
        hp, a, b, c, d = trunk_layer(hp, pos_prompt, zero_s5, zero_s5, zero_gla, zero_ret, *lw)
        p_s5r.append(a); p_s5i.append(b); p_gla.append(c); p_ret.append(d)
        hs, a, b, c, d = trunk_layer(hs, pos_sample, state_s5_re[l], state_s5_im[l], state_gla[l], state_ret[l], *lw)
        s_s5r.append(a); s_s5i.append(b); s_gla.append(c); s_ret.append(d)
    y_prompt = rmsnorm(hp, norm_final)
    y_sample = rmsnorm(hs, norm_final)
    return (y_prompt, y_sample,
            jnp.stack(p_s5r), jnp.stack(p_s5i), jnp.stack(p_gla), jnp.stack(p_ret),
            jnp.stack(s_s5r), jnp.stack(s_s5i), jnp.stack(s_gla), jnp.stack(s_ret))
```

```python
from contextlib import ExitStack
import numpy as np
import concourse.bass as bass
import concourse.mybir as mybir
from concourse.bass_utils import run_bass_kernel_spmd

F32 = mybir.dt.float32
BF16 = mybir.dt.bfloat16
AF = mybir.ActivationFunctionType
ALU = mybir.AluOpType
AX = mybir.AxisListType


class Buf:
    __slots__ = ("name", "w", "r")

    def __init__(self, name=""):
        self.name = name
        self.w = None
        self.r = {}


class Sched:
    ENG = ("pe", "act", "dve", "pool", "sp")
    NCHAN = 90

    def __init__(self, nc, es):
        self.nc = nc
        self.es = es
        self.eng = {"pe": nc.tensor, "act": nc.scalar, "dve": nc.vector, "pool": nc.gpsimd, "sp": nc.sync}
        self.sems = {}
        self.cnt = {}
        self.seen = {e: {} for e in self.ENG}
        self.pend = {e: ([], []) for e in self.ENG}
        self.pool = []
        for e in self.ENG:
            h = self.es.enter_context(self.nc.semaphore("E_" + e))
            self.sems["E_" + e] = h
            self.cnt["E_" + e] = 0
            nc.sync.sem_clear(h)
        for i in range(self.NCHAN):
            h = self.es.enter_context(self.nc.semaphore(f"DS{i}"))
            nc.sync.sem_clear(h)
            self.pool.append(h)
        nc.all_engine_barrier()

    def chan(self, name):
        key = "D_" + name
        self.sems[key] = self.pool.pop()
        self.cnt[key] = 0
        return key

    def _wait(self, e, tok):
        key, val = tok
        if key == "E_pe" and e == "pe":
            return
        if key.startswith("D_"):
            val = max(val, self.cnt[key])
        if self.seen[e].get(key, 0) >= val:
            return
        self.eng[e].wait_ge(self.sems[key], val)
        self.seen[e][key] = val

    def _deps(self, e, R, W):
        for b in R:
            if b.w is not None:
                self._wait(e, b.w)
        for b in W:
            if b.w is not None:
                self._wait(e, b.w)
            for k, v in b.r.items():
                self._wait(e, (k, v))

    def _commit(self, tok, R, W):
        k, v = tok
        for b in R:
            if b.r.get(k, 0) < v:
                b.r[k] = v
        for b in W:
            b.w = tok
            b.r = {}

    def op(self, e, fn, R=(), W=(), inc=True):
        self._deps(e, R, W)
        inst = fn(self.eng[e])
        pr, pw = self.pend[e]
        pr.extend(R)
        pw.extend(W)
        if not inc:
            return None
        own = "E_" + e
        self.cnt[own] += 1
        inst.then_inc(self.sems[own], 1)
        tok = (own, self.cnt[own])
        self._commit(tok, pr, pw)
        self.pend[e] = ([], [])
        return tok

    def dma(self, q, out, in_, ch, R=(), W=()):
        self._deps(q, R, W)
        inst = self.eng[q].dma_start(out=out, in_=in_)
        self.cnt[ch] += 16
        inst.then_inc(self.sems[ch], 16)
        tok = (ch, self.cnt[ch])
        self._commit(tok, R, W)
        return tok

    def barrier(self):
        toks = [(k, v) for k, v in self.cnt.items() if v > 0]
        for e in self.ENG:
            for t in toks:
                self._wait(e, t)


D = 2048
WIN = 14352
DFF = 5632
NS = 16
EPS = 1e-6
DBG = False
C_ID, C_J, C_TRI, C_GM, C_SGN, C_E16, C_D0, C_DH, C_GPOW, C_KDEC, C_G128, C_G1, C_N = (
    0, 128, 256, 384, 392, 400, 656, 1168, 2192, 2200, 2208, 2216, 2224)
V_NM, V_NF, V_SD, V_BG, V_GB, V_N = 0, 16, 32, 40, 48, 52


class Tl:
    __slots__ = ("t", "b")

    def __init__(self, t, name):
        self.t = t
        self.b = Buf(name)


def build(NPT, DEPTH):
    NP = NPT * 128
    NT = NP + NS
    tiles = [(i * 128, 128) for i in range(NPT)] + [(NP, NS)]
    blocks = [(b0, min(512, NP - b0)) for b0 in range(0, NP, 512)] + [(NP, NS)]
    nc = bass.Bass("TRN2", target_bir_lowering=False)

    def din(name, shape, dt=F32):
        return nc.dram_tensor(name, list(shape), dt, kind="ExternalInput").ap()

    def dout(name, shape, dt=F32):
        return nc.dram_tensor(name, list(shape), dt, kind="ExternalOutput").ap()

    def dscr(name, shape, dt=F32):
        if DBG:
            return nc.dram_tensor(name, list(shape), dt, kind="ExternalOutput").ap()
        return nc.dram_tensor(name, list(shape), dt).ap()

    xp = din("xp", [NP, D]); xs = din("xs", [NS, D])
    s5r0 = din("s5r0", [DEPTH, NS, 4096]); s5i0 = din("s5i0", [DEPTH, NS, 4096])
    gla0 = din("gla0", [DEPTH, NS, 4, 128, 256]); ret0 = din("ret0", [DEPTH, NS, 8, 128, 128])
    w_in = din("w_in", [DEPTH, D, WIN])
    s5dup = din("s5dup", [DEPTH, 3, 128, 64]); s5row = din("s5row", [DEPTH, 3, NS, 4096])
    s5b_re = din("s5b_re", [DEPTH, 64, 64, 16]); s5b_im = din("s5b_im", [DEPTH, 64, 64, 16])
    s5c_re = din("s5c_re", [DEPTH, 1024, 64]); s5c_im = din("s5c_im", [DEPTH, 1024, 64])
    w_glu = din("w_glu", [DEPTH, 1024, 1024]); w_s5o = din("w_s5o", [DEPTH, 1024, D])
    w_gate = din("w_gate", [DEPTH, 16, 512]); gnorm = din("gnorm", [DEPTH, 128, 256])
    w_glao = din("w_glao", [DEPTH, 1024, D]); w_reto = din("w_reto", [DEPTH, 1024, D])
    w_mix = din("w_mix", [DEPTH, D, D])
    w_fg = din("w_fg", [DEPTH, D, DFF]); w_fu = din("w_fu", [DEPTH, D, DFF]); w_fd = din("w_fd", [DEPTH, DFF, D])
    vecs = din("vecs", [DEPTH, V_N, 128]); nfin = din("nfin", [128, D])
    consts = din("consts", [128, C_N]); rot = din("rot", [NT, 128])

    yp = dout("yp", [NP, D]); ys = dout("ys", [NS, D])
    ps5 = dout("ps5", [DEPTH, 64, 128]); pgla = dout("pgla", [DEPTH, 4, 128, 256]); pret = dout("pret", [DEPTH, 8, 128, 128])
    ss5 = dout("ss5", [DEPTH, NS, 64, 128]); sgla = dout("sgla", [DEPTH, NS, 4, 128, 256]); sret = dout("sret", [DEPTH, NS, 8, 128, 128])

    UT = dscr("UT", [1024, NT], BF16); GQT = dscr("GQT", [512, NT]); GKT = dscr("GKT", [512, NT]); GLRT = dscr("GLRT", [16, NT])
    GV = dscr("GV", [NT, 1024], BF16); GR = dscr("GR", [NT, 1024]); RQ = dscr("RQ", [NT, 1024]); RK = dscr("RK", [NT, 1024])
    RV = dscr("RV", [NT, 1024], BF16); RG = dscr("RG", [NT, 1024]); MGT = dscr("MGT", [6144, NT], BF16)
    Y5A = dscr("Y5A", [1024, NT], BF16); A5T = dscr("A5T", [1024, NT], BF16); AGT = dscr("AGT", [1024, NT], BF16); ART = dscr("ART", [1024, NT], BF16)
    XMID = dscr("XMID", [NT, D]); X1 = dscr("X1", [NT, D]); FFTL = dscr("FFTL", [NPT + 1, 128, DFF // 128, 128], BF16)

    def xrows(src, ti):
        r0, n = tiles[ti]
        if src is None:
            return xp[r0:r0 + n, :] if ti < NPT else xs[:, :]
        return src[r0:r0 + n, :]

    with ExitStack() as es:
        S = Sched(nc, es)
        chans = {}

        def ch(name):
            if name not in chans:
                chans[name] = S.chan(name)
            return chans[name]

        def V(fn, R=(), W=(), inc=True): return S.op("dve", fn, [x.b if isinstance(x, Tl) else x for x in R], [x.b if isinstance(x, Tl) else x for x in W], inc)
        def A(fn, R=(), W=(), inc=True): return S.op("act", fn, [x.b if isinstance(x, Tl) else x for x in R], [x.b if isinstance(x, Tl) else x for x in W], inc)
        def G(fn, R=(), W=(), inc=True): return S.op("pool", fn, [x.b if isinstance(x, Tl) else x for x in R], [x.b if isinstance(x, Tl) else x for x in W], inc)
        def M(fn, R=(), W=(), inc=True): return S.op("pe", fn, [x.b if isinstance(x, Tl) else x for x in R], [x.b if isinstance(x, Tl) else x for x in W], inc)

        def DMA(q, out, in_, chn, R=(), W=()):
            return S.dma(q, out, in_, ch(chn), [x.b if isinstance(x, Tl) else x for x in R], [x.b if isinstance(x, Tl) else x for x in W])

        uid = [0]

        def mk(stack, kind):
            def f(name, shape, dt=F32):
                alloc = nc.sbuf_tensor if kind == "s" else nc.psum_tensor
                uid[0] += 1
                name = f"{name}_{uid[0]}"
                return Tl(stack.enter_context(alloc(name, list(shape), dt)), name)
            return f

        T0 = mk(es, "s")
        cst = T0("cst", [128, C_N]); idb = T0("idb", [128, 128], BF16)
        DMA("sp", cst.t[:], consts, "cst", W=[cst])
        V(lambda e: e.tensor_copy(out=idb.t[:], in_=cst.t[:, C_ID:C_ID + 128]), R=[cst], W=[idb])
        idf = cst.t[:, C_ID:C_ID + 128]

        def rms_rstd(TT, src_ap, src_tl, n, width, name):
            junk = TT(name + "_j", [128, width]); ss = TT(name + "_s", [128, 1]); rs = TT(name + "_r", [128, 1])
            return junk, ss, rs

        def norm_T(TT, PP, src, gcol, hT, hTb, vecT, pfx):
            xt = [TT(pfx + f"xt{i}", [128, D]) for i in range(2)]
            junk = TT(pfx + "junk", [128, D], BF16)
            ss = [TT(pfx + f"ss{i}", [128, 16]) for i in range(2)]
            rs = [TT(pfx + f"rs{i}", [128, 16]) for i in range(2)]
            xn = [TT(pfx + f"xn{i}", [128, D], BF16) for i in range(2)]
            ptr = [PP(pfx + f"ptr{i}", [128, 4, 128], BF16) for i in range(2)]
            for ti, (r0, n) in enumerate(tiles):
                s = ti % 2
                DMA("sp", xt[s].t[:n, :], xrows(src, ti), pfx + f"xt{s}", W=[xt[s]])
                V(lambda e: e.memset(ss[s].t[:], 0.0), W=[ss[s]])
                A(lambda e: e.activation(out=junk.t[:n, :], in_=xt[s].t[:n, :], func=AF.Square, accum_out=ss[s].t[:n, 0:1]), R=[xt[s]], W=[junk, ss[s]])
                A(lambda e: e.activation(out=rs[s].t[:n, 0:1], in_=ss[s].t[:n, 0:1], func=AF.Sqrt, scale=1.0 / D, bias=EPS), R=[ss[s]], W=[rs[s]])
                V(lambda e: e.reciprocal(out=rs[s].t[:n, 0:1], in_=rs[s].t[:n, 0:1]), R=[rs[s]], W=[rs[s]])
                A(lambda e: e.activation(out=xn[s].t[:n, :], in_=xt[s].t[:n, :], func=AF.Copy, scale=rs[s].t[:n, 0:1]), R=[xt[s], rs[s]], W=[xn[s]])
                for q in range(4):
                    p = ptr[q % 2]
                    for j in range(4):
                        kc = q * 4 + j
                        M(lambda e: e.transpose(out=p.t[:, j, :n], in_=xn[s].t[:n, kc * 128:(kc + 1) * 128], identity=idb.t[:n, :n]),
                          R=[xn[s], idb], W=[p], inc=(j == 3))
                    V(lambda e: e.tensor_tensor(out=hT.t[:, q * 4:q * 4 + 4, r0:r0 + n], in0=p.t[:, :, :n],
                                                in1=vecT.t[:, gcol + q * 4:gcol + q * 4 + 4].unsqueeze(2).to_broadcast([128, 4, n]), op=ALU.mult),
                      R=[p, vecT], W=[hTb[ti]])
            if DBG and pfx == "n1" and not hasattr(nc, "_dbg1"):
                nc._dbg1 = 1
                d1 = dscr("DBG_ss1", [128, 16]); d2 = dscr("DBG_rs1", [128, 16]); d3 = dscr("DBG_xt1", [128, D]); d4 = dscr("DBG_xn1", [128, D], BF16)
                DMA("sp", d1, ss[1].t[:], "dbg", R=[ss[1]]); DMA("sp", d2, rs[1].t[:], "dbg", R=[rs[1]])
                DMA("sp", d3, xt[1].t[:], "dbg", R=[xt[1]]); DMA("sp", d4, xn[1].t[:], "dbg", R=[xn[1]])

        def sincos(th, sn, cs, t1, t2, RW):
            A(lambda e: e.activation(out=t1, in_=th, func=AF.Sin, scale=1.0 / 16), **RW)
            V(lambda e: e.tensor_tensor(out=t1, in0=t1, in1=t1, op=ALU.mult), **RW)
            V(lambda e: e.tensor_scalar(out=cs, in0=t1, scalar1=-2.0, scalar2=1.0, op0=ALU.mult, op1=ALU.add), **RW)
            A(lambda e: e.activation(out=sn, in_=th, func=AF.Sin, scale=1.0 / 8), **RW)
            for _ in range(3):
                V(lambda e: e.tensor_tensor(out=t1, in0=sn, in1=cs, op=ALU.mult), **RW)
                V(lambda e: e.tensor_tensor(out=t2, in0=sn, in1=sn, op=ALU.mult), **RW)
                V(lambda e: e.tensor_tensor(out=cs, in0=cs, in1=cs, op=ALU.mult), **RW)
                V(lambda e: e.tensor_tensor(out=cs, in0=cs, in1=t2, op=ALU.subtract), **RW)
                V(lambda e: e.tensor_scalar_mul(out=sn, in0=t1, scalar1=2.0), **RW)

        def blk_tiles(b0, bn):
            return [ti for ti, (r0, n) in enumerate(tiles) if r0 < b0 + bn and r0 + n > b0]

        for l in range(DEPTH):
            xsrc = None if l == 0 else X1
            with ExitStack() as ls:
                TL = mk(ls, "s")
                vecT = TL("vecT", [128, V_N])
                with ExitStack() as ps:
                    TT = mk(ps, "s"); PP = mk(ps, "p")
                    vr = TT("vr", [V_N, 128]); pv = PP("pv", [128, V_N])
                    DMA("sp", vr.t[:], vecs[l], "vr", W=[vr])
                    M(lambda e: e.transpose(out=pv.t[:], in_=vr.t[:], identity=idf[:V_N, :V_N]), R=[vr, cst], W=[pv])
                    V(lambda e: e.tensor_copy(out=vecT.t[:], in_=pv.t[:]), R=[pv], W=[vecT])
                    S.barrier()

                with ExitStack() as ps:
                    TT = mk(ps, "s"); PP = mk(ps, "p")
                    hT = TT("hT", [128, 16, NT], BF16)
                    hTb = [Buf(f"hT{ti}") for ti in range(len(tiles))]
                    with ExitStack() as ps1:
                        norm_T(mk(ps1, "s"), mk(ps1, "p"), xsrc, V_NM, hT, hTb, vecT, "n1")
                        S.barrier()
                    if DBG and l == 0:
                        dbg_hT = dscr("DBG_hT", [128, 16, NT], BF16)
                        DMA("sp", dbg_hT, hT.t[:], "dbg", R=hTb)
                    wsl = [TT(f"w{i}", [128, 16, 512], BF16) for i in range(3)]
                    pm = [PP(f"pm{i}", [128, 512]) for i in range(4)]
                    stf = [TT(f"stf{i}", [128, 512]) for i in range(4)]
                    stb = [TT(f"stb{i}", [128, 512], BF16) for i in range(4)]
                    segs = [(0, 1024, "F", AF.Copy, UT, True), (1024, 512, "F", AF.Copy, GQT, False), (1536, 512, "F", AF.Copy, GKT, False),
                            (2048, 1024, "T", AF.Copy, GV, True), (3072, 1024, "T", AF.Silu, GR, False), (4096, 16, "F", AF.Copy, GLRT, False),
                            (4112, 1024, "T", AF.Copy, RQ, False), (5136, 1024, "T", AF.Copy, RK, False), (6160, 1024, "T", AF.Copy, RV, True),
                            (7184, 1024, "T", AF.Silu, RG, False), (8208, 6144, "F", AF.Sigmoid, MGT, True)]
                    wi = 0; ui = 0
                    for (c0, ncol, form, func, dst, isb) in segs:
                        for cb in range(0, ncol, 512):
                            ncb = min(512, ncol - cb)
                            w = wsl[wi % 3]
                            DMA("pool", w.t[:, :, :ncb], w_in[l][:, c0 + cb:c0 + cb + ncb].rearrange("(k p) e -> p k e", p=128), f"w{wi % 3}", W=[w])
                            wi += 1
                            if form == "T":
                                for ti, (r0, n) in enumerate(tiles):
                                    p = pm[ui % 4]; st = (stb if isb else stf)[ui % 4]; ui += 1
                                    for kc in range(16):
                                        M(lambda e: e.matmul(p.t[:n, :ncb], lhsT=hT.t[:, kc, r0:r0 + n], rhs=w.t[:, kc, :ncb], start=(kc == 0), stop=(kc == 15)),
                                          R=[hTb[ti], w], W=[p], inc=(kc == 15))
                                    A(lambda e: e.activation(out=st.t[:n, :ncb], in_=p.t[:n, :ncb], func=func), R=[p], W=[st])
                                    DMA("sp", dst[r0:r0 + n, cb:cb + ncb], st.t[:n, :ncb], ("sb" if isb else "sf") + str((ui - 1) % 4), R=[st])
                            else:
                                for j0 in range(0, ncb, 128):
                                    m = min(128, ncb - j0)
                                    for (b0, bn) in blocks:
                                        p = pm[ui % 4]; st = (stb if isb else stf)[ui % 4]; ui += 1
                                        for kc in range(16):
                                            M(lambda e: e.matmul(p.t[:m, :bn], lhsT=w.t[:, kc, j0:j0 + m], rhs=hT.t[:, kc, b0:b0 + bn], start=(kc == 0), stop=(kc == 15)),
                                              R=[hTb[t_] for t_ in blk_tiles(b0, bn)] + [w], W=[p], inc=(kc == 15))
                                        A(lambda e: e.activation(out=st.t[:m, :bn], in_=p.t[:m, :bn], func=func), R=[p], W=[st])
                                        DMA("sp", dst[cb + j0:cb + j0 + m, b0:b0 + bn], st.t[:m, :bn], ("sb" if isb else "sf") + str((ui - 1) % 4), R=[st])
                    S.barrier()

                PI = float(np.pi)
                with ExitStack() as ps:
                    TT = mk(ps, "s"); PP = mk(ps, "p")
                    EC = TT("EC", [128, 64, 128]); ES = TT("ES", [128, 64, 128])
                    sp_ = TT("s5p", [128, 16, 64]); prm = TT("prm", [128, 3, 64])
                    X1_ = TT("X1_", [128, 64, 16]); X2_ = TT("X2_", [128, 64, 16])
                    CA = TT("CA", [128, 8, 128]); CB = TT("CB", [128, 8, 128])
                    xl = TT("xl", [128, 64])
                    DMA("sp", prm.t[:], s5dup[l].rearrange("a p g -> p a g"), "prm", W=[prm])
                    ar = prm.t[:, 0, :]; ai = prm.t[:, 1, :]; ldt = prm.t[:, 2, :]
                    (dtb, mag, th, a1, sn, cs, abr, abi, den, nr, fr, fi, tq, tr) = [sp_.t[:, i_, :] for i_ in range(14)]
                    RW = dict(R=[sp_, prm], W=[sp_])
                    A(lambda e: e.activation(out=dtb, in_=ldt, func=AF.Exp), **RW)
                    V(lambda e: e.tensor_tensor(out=tq, in0=ar, in1=dtb, op=ALU.mult), **RW)
                    A(lambda e: e.activation(out=mag, in_=tq, func=AF.Exp), **RW)
                    V(lambda e: e.tensor_tensor(out=th, in0=ai, in1=dtb, op=ALU.mult), **RW)
                    sincos(th, sn, cs, a1, tq, RW)
                    V(lambda e: e.tensor_tensor(out=abr, in0=mag, in1=cs, op=ALU.mult), **RW)
                    V(lambda e: e.tensor_tensor(out=abi, in0=mag, in1=sn, op=ALU.mult), **RW)
                    V(lambda e: e.tensor_tensor(out=den, in0=ar, in1=ar, op=ALU.mult), **RW)
                    V(lambda e: e.tensor_tensor(out=tq, in0=ai, in1=ai, op=ALU.mult), **RW)
                    V(lambda e: e.tensor_tensor(out=den, in0=den, in1=tq, op=ALU.add), **RW)
                    V(lambda e: e.reciprocal(out=den, in_=den), **RW)
                    V(lambda e: e.tensor_scalar_add(out=nr, in0=abr, scalar1=-1.0), **RW)
                    V(lambda e: e.tensor_tensor(out=tq, in0=nr, in1=ar, op=ALU.mult), **RW)
                    V(lambda e: e.tensor_tensor(out=tr, in0=abi, in1=ai, op=ALU.mult), **RW)
                    V(lambda e: e.tensor_tensor(out=tq, in0=tq, in1=tr, op=ALU.add), **RW)
                    V(lambda e: e.tensor_tensor(out=fr, in0=tq, in1=den, op=ALU.mult), **RW)
                    V(lambda e: e.tensor_tensor(out=tq, in0=abi, in1=ar, op=ALU.mult), **RW)
                    V(lambda e: e.tensor_tensor(out=tr, in0=nr, in1=ai, op=ALU.mult), **RW)
                    V(lambda e: e.tensor_tensor(out=tq, in0=tq, in1=tr, op=ALU.subtract), **RW)
                    V(lambda e: e.tensor_tensor(out=fi, in0=tq, in1=den, op=ALU.mult), **RW)
                    xlb = [Buf("xl0"), Buf("xl1")]
                    V(lambda e: e.memset(xl.t[:], 0.0), W=xlb)
                    with ExitStack() as ps1:
                        T1 = mk(ps1, "s")
                        t1 = T1("et1", [128, 64, 64]); t2 = T1("et2", [128, 64, 64])
                        A_ = T1("A_", [128, 64, 16]); Bm = T1("Bm", [128, 64, 16]); tb = T1("tb", [128, 64, 16])
                        V(lambda e: e.tensor_copy(out=EC.t[:, :, 0:1], in_=cs.unsqueeze(2)), R=[sp_], W=[EC])
                        V(lambda e: e.tensor_copy(out=ES.t[:, :, 0:1], in_=sn.unsqueeze(2)), R=[sp_], W=[ES])
                        for s_ in range(7):
                            n_ = 2 ** s_
                            cb_ = EC.t[:, :, n_ - 1:n_].to_broadcast([128, 64, n_]); sb_ = ES.t[:, :, n_ - 1:n_].to_broadcast([128, 64, n_])
                            V(lambda e: e.tensor_tensor(out=t1.t[:, :, :n_], in0=EC.t[:, :, 0:n_], in1=cb_, op=ALU.mult), R=[EC], W=[t1])
                            V(lambda e: e.tensor_tensor(out=t2.t[:, :, :n_], in0=ES.t[:, :, 0:n_], in1=sb_, op=ALU.mult), R=[ES], W=[t2])
                            V(lambda e: e.tensor_tensor(out=t1.t[:, :, :n_], in0=t1.t[:, :, :n_], in1=t2.t[:, :, :n_], op=ALU.subtract), R=[t1, t2], W=[t1])
                            V(lambda e: e.tensor_tensor(out=t2.t[:, :, :n_], in0=EC.t[:, :, 0:n_], in1=sb_, op=ALU.mult), R=[EC, ES], W=[t2])
                            V(lambda e: e.tensor_copy(out=EC.t[:, :, n_:2 * n_], in_=t1.t[:, :, :n_]), R=[t1], W=[EC])
                            V(lambda e: e.tensor_tensor(out=t1.t[:, :, :n_], in0=ES.t[:, :, 0:n_], in1=cb_, op=ALU.mult), R=[EC, ES], W=[t1])
                            V(lambda e: e.tensor_tensor(out=ES.t[:, :, n_:2 * n_], in0=t1.t[:, :, :n_], in1=t2.t[:, :, :n_], op=ALU.add), R=[t1, t2], W=[ES])
                        DMA("sp", A_.t[0:64], s5b_re[l].rearrange("g p n -> p g n"), "bb", W=[A_]); DMA("sp", A_.t[64:128], s5b_im[l].rearrange("g p n -> p g n"), "bb", W=[A_])
                        DMA("sp", Bm.t[0:64], s5b_im[l].rearrange("g p n -> p g n"), "bb", W=[Bm]); DMA("sp", Bm.t[64:128], s5b_re[l].rearrange("g p n -> p g n"), "bb", W=[Bm])
                        V(lambda e: e.tensor_scalar_mul(out=Bm.t[0:64], in0=Bm.t[0:64], scalar1=-1.0), R=[Bm], W=[Bm])
                        frb = fr.unsqueeze(2).to_broadcast([128, 64, 16]); fib = fi.unsqueeze(2).to_broadcast([128, 64, 16])
                        V(lambda e: e.tensor_tensor(out=X1_.t[:], in0=A_.t[:], in1=frb, op=ALU.mult), R=[A_, sp_], W=[X1_])
                        V(lambda e: e.tensor_tensor(out=tb.t[:], in0=Bm.t[:], in1=fib, op=ALU.mult), R=[Bm, sp_], W=[tb])
                        V(lambda e: e.tensor_tensor(out=X1_.t[:], in0=X1_.t[:], in1=tb.t[:], op=ALU.add), R=[X1_, tb], W=[X1_])
                        V(lambda e: e.tensor_tensor(out=X2_.t[:], in0=A_.t[:], in1=fib, op=ALU.mult), R=[A_, sp_], W=[X2_])
                        V(lambda e: e.tensor_tensor(out=tb.t[:], in0=Bm.t[:], in1=frb, op=ALU.mult), R=[Bm, sp_], W=[tb])
                        V(lambda e: e.tensor_tensor(out=X2_.t[:], in0=X2_.t[:], in1=tb.t[:], op=ALU.subtract), R=[X2_, tb], W=[X2_])
                        DMA("sp", CA.t[:, :, 0:64], s5c_re[l].rearrange("(t q) p -> q t p", q=128), "cc", W=[CA]); DMA("sp", CA.t[:, :, 64:128], s5c_im[l].rearrange("(t q) p -> q t p", q=128), "cc", W=[CA])
                        DMA("sp", CB.t[:, :, 0:64], s5c_im[l].rearrange("(t q) p -> q t p", q=128), "cc", W=[CB]); DMA("sp", CB.t[:, :, 64:128], s5c_re[l].rearrange("(t q) p -> q t p", q=128), "cc", W=[CB])
                        V(lambda e: e.tensor_scalar_mul(out=CA.t[:, :, 64:128], in0=CA.t[:, :, 64:128], scalar1=-1.0), R=[CA], W=[CA])
                        V(lambda e: e.tensor_scalar_mul(out=CB.t[:], in0=CB.t[:], scalar1=-1.0), R=[CB], W=[CB])
                        S.barrier()
                    cT = EC.t[:, :, 127]; sT = ES.t[:, :, 127]
                    with ExitStack() as ps1:
                        T1 = mk(ps1, "s")
                        LB = [T1(f"LB{v}", [128, 8, 128], BF16) for v in range(2)]; LC = [T1(f"LC{v}", [128, 8, 128], BF16) for v in range(2)]
                        for v in range(2):
                            G(lambda e: e.memset(LC[v].t[:], 0.0), W=[LC[v]])
                        uT = T1("uT", [128, NT], BF16); ypre = T1("ypre", [128, NT]); y5s = T1("y5s", [128, NT], BF16)
                        m1 = [T1(f"m1{i}", [128, 4, 128]) for i in range(2)]; m2 = [T1(f"m2{i}", [128, 4, 128]) for i in range(2)]
                        win = [T1(f"win{i}", [128, 4, 128]) for i in range(2)]; wv = [T1(f"wv{i}", [128, 4, 128]) for i in range(2)]
                        a1b = [T1(f"a1b{i}", [128, 4, 128], BF16) for i in range(2)]; a2b = [T1(f"a2b{i}", [128, 4, 128], BF16) for i in range(2)]
                        c1 = [T1(f"c1{i}", [128, 16]) for i in range(2)]; c2 = [T1(f"c2{i}", [128, 16]) for i in range(2)]
                        rp = T1("rp", [16, 3, 512]); rw = T1("rw", [16, 8, 512]); x0r = T1("x0r", [16, 512]); x0i = T1("x0i", [16, 512])
                        ax0 = T1("ax0", [16, 8, 128]); xnew = T1("xnew", [16, 8, 128]); xnb = T1("xnb", [16, 8, 128], BF16); xnT = T1("xnT", [128, 8, 16], BF16)
                        gz = T1("gz", [128, 512]); gs = T1("gs", [128, 512])
                        P1 = mk(ps1, "p")
                        PS1 = [P1(f"PS1{i}", [128, 4, 128]) for i in range(2)]; PS2 = [P1(f"PS2{i}", [128, 4, 128]) for i in range(2)]
                        PY = [P1(f"PY{i}", [128, 128]) for i in range(2)]; ptw = PY[0]
                        psw = P1("psw", [128, 16]); ptx = P1("ptx", [128, 8, 16], BF16)
                        for gt in range(8):
                            DMA("sp", uT.t[:], UT[gt * 128:(gt + 1) * 128, :], "uT", W=[uT])
                            for v, X_ in enumerate((X1_, X2_)):
                                M(lambda e: e.transpose(out=ptw.t[:], in_=X_.t[:, gt * 8:(gt + 1) * 8, :], identity=idf), R=[X_, cst], W=[ptw])
                                for gg in range(8):
                                    V(lambda e: e.tensor_scalar_mul(out=LB[v].t[:, gg, :], in0=ptw.t[:], scalar1=cst.t[:, C_GM + gg:C_GM + gg + 1]), R=[ptw, cst], W=[LB[v]])
                            for v, C_ in enumerate((CA, CB)):
                                M(lambda e: e.transpose(out=ptw.t[:], in_=C_.t[:, gt, :], identity=idf), R=[C_, cst], W=[ptw])
                                for gg in range(8):
                                    V(lambda e: e.tensor_copy(out=LC[v].t[:, gg, 16 * gg:16 * gg + 16], in_=ptw.t[:, 16 * gg:16 * gg + 16]), R=[ptw], W=[LC[v]])
                            dcol = vecT.t[:, V_SD + gt:V_SD + gt + 1]
                            units = [(bi, hh) for bi in range(NPT) for hh in range(2)]

                            def st1a(u):
                                bi, hh = u; t0 = bi * 128; g0 = gt * 8 + hh * 4; i = hh
                                for q in range(4):
                                    M(lambda e: e.matmul(PS1[i].t[:, q, :], lhsT=LB[0].t[:, hh * 4 + q, :], rhs=uT.t[:, t0:t0 + 128], start=True, stop=True), R=[LB[0], uT], W=[PS1[i]], inc=False)
                                    M(lambda e: e.matmul(PS2[i].t[:, q, :], lhsT=LB[1].t[:, hh * 4 + q, :], rhs=uT.t[:, t0:t0 + 128], start=True, stop=True), R=[LB[1], uT], W=[PS2[i]], inc=(q == 3))
                                V(lambda e: e.tensor_tensor(out=m1[i].t[:], in0=PS1[i].t[:], in1=EC.t[:, g0:g0 + 4, :], op=ALU.mult), R=[PS1[i], EC], W=[m1[i]])
                                V(lambda e: e.tensor_tensor(out=m2[i].t[:], in0=PS2[i].t[:], in1=ES.t[:, g0:g0 + 4, :], op=ALU.mult), R=[PS2[i], ES], W=[m2[i]])
                                G(lambda e: e.tensor_tensor(out=win[i].t[:], in0=m1[i].t[:], in1=m2[i].t[:], op=ALU.add), R=[m1[i], m2[i]], W=[win[i]])
                                for q in range(4):
                                    V(lambda e: e.tensor_tensor_scan(out=wv[i].t[:, q, :], data0=mag[:, g0 + q:g0 + q + 1].to_broadcast([128, 128]), data1=win[i].t[:, q, :],
                                                                     initial=xl.t[:, g0 + q:g0 + q + 1], op0=ALU.mult, op1=ALU.add), R=[win[i], xlb[hh], sp_], W=[wv[i]])
                                G(lambda e: e.tensor_tensor(out=a1b[i].t[:], in0=wv[i].t[:], in1=EC.t[:, g0:g0 + 4, :], op=ALU.mult), R=[wv[i], EC], W=[a1b[i]])
                                V(lambda e: e.tensor_tensor(out=a2b[i].t[:], in0=wv[i].t[:], in1=ES.t[:, g0:g0 + 4, :], op=ALU.mult), R=[wv[i], ES], W=[a2b[i]])

                            def st1b(u):
                                bi, hh = u; g0 = gt * 8 + hh * 4; i = hh
                                M(lambda e: e.matmul(psw.t[:, 0:4], lhsT=cst.t[:, C_J:C_J + 128], rhs=wv[i].t[:, :, 127], start=True, stop=True), R=[cst, wv[i]], W=[psw])
                                V(lambda e: e.tensor_tensor(out=c1[i].t[:, 0:4], in0=wv[i].t[:, :, 127], in1=cT[:, g0:g0 + 4], op=ALU.mult), R=[wv[i], EC], W=[c1[i]])
                                V(lambda e: e.tensor_tensor(out=c2[i].t[:, 0:4], in0=psw.t[:, 0:4], in1=sT[:, g0:g0 + 4], op=ALU.mult), R=[psw, ES], W=[c2[i]])
                                V(lambda e: e.tensor_tensor(out=xl.t[:, g0:g0 + 4], in0=c1[i].t[:, 0:4], in1=c2[i].t[:, 0:4], op=ALU.add), R=[c1[i], c2[i]], W=[xlb[hh]])

                            def st2(u):
                                bi, hh = u; t0 = bi * 128; i = hh; py = PY[bi % 2]
                                for q in range(4):
                                    gg = hh * 4 + q
                                    M(lambda e: e.matmul(py.t[:], lhsT=LC[0].t[:, gg, :], rhs=a1b[i].t[:, q, :], start=(gg == 0), stop=False), R=[LC[0], a1b[i]], W=[py], inc=False)
                                    M(lambda e: e.matmul(py.t[:], lhsT=LC[1].t[:, gg, :], rhs=a2b[i].t[:, q, :], start=False, stop=(gg == 7)), R=[LC[1], a2b[i]], W=[py], inc=(q == 3))
                                if hh == 1:
                                    V(lambda e: e.scalar_tensor_tensor(out=ypre.t[:, t0:t0 + 128], in0=uT.t[:, t0:t0 + 128], scalar=dcol, in1=py.t[:], op0=ALU.mult, op1=ALU.add), R=[uT, vecT, py], W=[ypre])

                            st1a(units[0])
                            for ui_ in range(1, len(units)):
                                st1a(units[ui_]); st1b(units[ui_ - 1]); st2(units[ui_ - 1])
                            st1b(units[-1]); st2(units[-1])
                            DMA("sp", rp.t[:], s5row[l][:, :, gt * 512:(gt + 1) * 512].rearrange("a b x -> b a x"), "rp", W=[rp])
                            DMA("sp", x0r.t[:], s5r0[l][:, gt * 512:(gt + 1) * 512], "rp", W=[x0r]); DMA("sp", x0i.t[:], s5i0[l][:, gt * 512:(gt + 1) * 512], "rp", W=[x0i])
                            (dtr, mgr, thr, a1r, snr, csr, q1, q2) = [rw.t[:, i_, :] for i_ in range(8)]; abrr = thr; abir = dtr
                            RWs = dict(R=[rp, rw], W=[rw])
                            A(lambda e: e.activation(out=dtr, in_=rp.t[:, 2, :], func=AF.Exp), **RWs)
                            V(lambda e: e.tensor_tensor(out=q1, in0=rp.t[:, 0, :], in1=dtr, op=ALU.mult), **RWs)
                            A(lambda e: e.activation(out=mgr, in_=q1, func=AF.Exp), **RWs)
                            V(lambda e: e.tensor_tensor(out=thr, in0=rp.t[:, 1, :], in1=dtr, op=ALU.mult), **RWs)
                            sincos(thr, snr, csr, a1r, q1, RWs)
                            V(lambda e: e.tensor_tensor(out=abrr, in0=mgr, in1=csr, op=ALU.mult), **RWs)
                            V(lambda e: e.tensor_tensor(out=abir, in0=mgr, in1=snr, op=ALU.mult), **RWs)
                            v3 = lambda ap: ap.rearrange("b (g p) -> b g p", p=64)
                            V(lambda e: e.tensor_tensor(out=q1, in0=abrr, in1=x0r.t[:], op=ALU.mult), R=[rw, x0r], W=[rw])
                            V(lambda e: e.tensor_tensor(out=q2, in0=abir, in1=x0i.t[:], op=ALU.mult), R=[rw, x0i], W=[rw])
                            V(lambda e: e.tensor_tensor(out=ax0.t[:, :, 0:64], in0=v3(q1), in1=v3(q2), op=ALU.subtract), R=[rw], W=[ax0])
                            V(lambda e: e.tensor_tensor(out=q1, in0=abrr, in1=x0i.t[:], op=ALU.mult), R=[rw, x0i], W=[rw])
                            V(lambda e: e.tensor_tensor(out=q2, in0=abir, in1=x0r.t[:], op=ALU.mult), R=[rw, x0r], W=[rw])
                            V(lambda e: e.tensor_tensor(out=ax0.t[:, :, 64:128], in0=v3(q1), in1=v3(q2), op=ALU.add), R=[rw], W=[ax0])
                            for hh in range(2):
                                for q in range(4):
                                    M(lambda e: e.matmul(PS1[hh].t[:16, q, :], lhsT=uT.t[:, NP:NT], rhs=LB[0].t[:, hh * 4 + q, :], start=True, stop=True), R=[uT, LB[0]], W=[PS1[hh]], inc=(q == 3))
                                V(lambda e: e.tensor_tensor(out=xnew.t[:, hh * 4:hh * 4 + 4, :], in0=PS1[hh].t[:16], in1=ax0.t[:, hh * 4:hh * 4 + 4, :], op=ALU.add), R=[PS1[hh], ax0], W=[xnew])
                            DMA("sp", ss5[l][:, gt * 8:(gt + 1) * 8, :], xnew.t[:], "xnw", R=[xnew])
                            V(lambda e: e.tensor_copy(out=xnb.t[:], in_=xnew.t[:]), R=[xnew], W=[xnb])
                            for gg in range(8):
                                M(lambda e: e.transpose(out=ptx.t[:, gg, :], in_=xnb.t[:, gg, :], identity=idb.t[:16, :16]), R=[xnb, idb], W=[ptx], inc=(gg == 7))
                            A(lambda e: e.activation(out=xnT.t[:], in_=ptx.t[:], func=AF.Copy), R=[ptx], W=[xnT])
                            for gg in range(8):
                                M(lambda e: e.matmul(PY[1].t[:, 0:16], lhsT=LC[0].t[:, gg, :], rhs=xnT.t[:, gg, :], start=(gg == 0), stop=(gg == 7)), R=[LC[0], xnT], W=[PY[1]], inc=(gg == 7))
                            V(lambda e: e.scalar_tensor_tensor(out=ypre.t[:, NP:NT], in0=uT.t[:, NP:NT], scalar=dcol, in1=PY[1].t[:, 0:16], op0=ALU.mult, op1=ALU.add), R=[uT, vecT, PY[1]], W=[ypre])
                            for (b0, bn) in blocks:
                                yv = ypre.t[:, b0:b0 + bn]
                                A(lambda e: e.activation(out=gz.t[:, :bn], in_=yv, func=AF.Square), R=[ypre], W=[gz])
                                V(lambda e: e.tensor_scalar(out=gz.t[:, :bn], in0=gz.t[:, :bn], scalar1=0.044715, scalar2=1.0, op0=ALU.mult, op1=ALU.add), R=[gz], W=[gz])
                                V(lambda e: e.tensor_tensor(out=gz.t[:, :bn], in0=gz.t[:, :bn], in1=yv, op=ALU.mult), R=[gz, ypre], W=[gz])
                                A(lambda e: e.activation(out=gs.t[:, :bn], in_=gz.t[:, :bn], func=AF.Sigmoid, scale=1.5957691216057308), R=[gz], W=[gs])
                                V(lambda e: e.tensor_tensor(out=y5s.t[:, b0:b0 + bn], in0=gs.t[:, :bn], in1=yv, op=ALU.mult), R=[gs, ypre], W=[y5s])
                            DMA("sp", Y5A[gt * 128:(gt + 1) * 128, :], y5s.t[:], "y5s", R=[y5s])
                        M(lambda e: e.transpose(out=ptw.t[:64, :], in_=xl.t[:], identity=idf), R=xlb + [cst], W=[ptw])
                        A(lambda e: e.activation(out=gz.t[:64, 0:128], in_=ptw.t[:64, :], func=AF.Copy), R=[ptw], W=[gz])
                        DMA("sp", ps5[l], gz.t[:64, 0:128], "xnw", R=[gz])
                        S.barrier()
                    with ExitStack() as ps1:
                        T1 = mk(ps1, "s")
                        y5a = T1("y5a", [128, 8, NT], BF16)
                        DMA("sp", y5a.t[:], Y5A.rearrange("(k p) t -> p k t", p=128), "y5s", W=[y5a])
                        wgl = T1("wgl", [128, 8, 1024], BF16); sgz = T1("sgz", [128, 512]); stb = [T1(f"gstb{i}", [128, 512], BF16) for i in range(2)]
                        pz = [mk(ps1, "p")(f"pz{i}", [128, 512]) for i in range(2)]
                        DMA("pool", wgl.t[:], w_glu[l].rearrange("(k p) e -> p k e", p=128), "w0", W=[wgl])
                        ui = 0
                        for et in range(8):
                            for (b0, bn) in blocks:
                                i = ui % 2; ui += 1
                                for kc in range(8):
                                    M(lambda e: e.matmul(pz[i].t[:, :bn], lhsT=wgl.t[:, kc, et * 128:(et + 1) * 128], rhs=y5a.t[:, kc, b0:b0 + bn], start=(kc == 0), stop=(kc == 7)),
                                      R=[wgl, y5a], W=[pz[i]], inc=(kc == 7))
                                A(lambda e: e.activation(out=sgz.t[:, :bn], in_=pz[i].t[:, :bn], func=AF.Sigmoid, bias=vecT.t[:, V_BG + et:V_BG + et + 1]), R=[pz[i], vecT], W=[sgz])
                                V(lambda e: e.tensor_tensor(out=stb[i].t[:, :bn], in0=sgz.t[:, :bn], in1=y5a.t[:, et, b0:b0 + bn], op=ALU.mult), R=[sgz, y5a], W=[stb[i]])
                                DMA("sp", A5T[et * 128:(et + 1) * 128, b0:b0 + bn], stb[i].t[:, :bn], f"sb{i}", R=[stb[i]])
                        S.barrier()

                SC = 128.0 ** -0.5
                with ExitStack() as ps:
                    TT = mk(ps, "s"); PP = mk(ps, "p")
                    wgt = TT("wgt", [16, 512]); gnr = TT("gnr", [128, 256]); nb = TT("nb", [128, 16]); glr = TT("glr", [16, NT])
                    DMA("sp", wgt.t[:], w_gate[l], "g0", W=[wgt]); DMA("sp", gnr.t[:], gnorm[l], "g0", W=[gnr]); DMA("sp", glr.t[:], GLRT, "g0", W=[glr])
                    V(lambda e: e.tensor_scalar_mul(out=nb.t[:, 0:4], in0=vecT.t[:, V_GB:V_GB + 4], scalar1=-1.0), R=[vecT], W=[nb])
                    qT = TT("gqT", [128, NT]); kT = TT("gkT", [128, NT]); l1 = TT("gl1", [128, NT]); bc = TT("gbc", [128, NT]); enb = TT("genb", [128, NT])
                    eb = [TT(f"geb{h}", [128, NT]) for h in range(4)]
                    qp = [TT(f"gqp{h}", [128, NT], BF16) for h in range(4)]; kp = [TT(f"gkp{h}", [128, NT], BF16) for h in range(4)]
                    ksT = [TT(f"gksT{h}", [16, 128]) for h in range(4)]; qs = [TT(f"gqs{h}", [128, 16]) for h in range(4)]
                    St = [TT(f"gS{h}", [128, 256]) for h in range(4)]; Sb = [TT(f"gSb{h}", [128, 256], BF16) for h in range(4)]
                    kpt = [TT(f"gkpt{h}", [128, 128], BF16) for h in range(4)]; scm = [TT(f"gscm{h}", [128, 128], BF16) for h in range(4)]
                    ss = [TT(f"gss{h}", [128, 16]) for h in range(4)]; rs = [TT(f"grs{h}", [128, 16]) for h in range(4)]
                    on = [TT(f"gon{h}", [128, 256]) for h in range(4)]; og = [TT(f"gog{h}", [128, 256], BF16) for h in range(4)]; ost = [TT(f"gost{h}", [128, 2, 128], BF16) for h in range(4)]
                    vt = [TT(f"gv{i}", [128, 1024], BF16) for i in range(2)]; grt = [TT(f"ggr{i}", [128, 1024]) for i in range(2)]
                    Vbd = TT("gVbd", [16, 16, 256]); Qbd = TT("gQbd", [128, 16, 16]); S0 = [TT(f"gS0{i}", [128, 256]) for i in range(2)]; Sn = [TT(f"gSn{i}", [128, 256]) for i in range(2)]
                    pgt = PP("pgt", [128, 512]); ptw = PP("gptw", [16, 128])
                    pab = [PP(f"gpab{i}", [128, 2, 256]) for i in range(2)]; psc = [PP(f"gpsc{i}", [128, 256]) for i in range(2)]; pbf = [PP(f"gpbf{i}", [128, 3, 128], BF16) for i in range(2)]
                    pob = [pab[i].b for i in range(2)]; pkb = pob; ptkb = [pbf[i].b for i in range(2)]; ptob = ptkb

                    def gpost(n, t0, h, po_ap, po_buf, grt_ap, grt_tl, si):
                        V(lambda e: e.memset(ss[h].t[:], 0.0), W=[ss[h]])
                        A(lambda e: e.activation(out=on[h].t[:n, :], in_=po_ap, func=AF.Square, accum_out=ss[h].t[:n, 0:1]), R=[po_buf], W=[on[h], ss[h]])
                        A(lambda e: e.activation(out=rs[h].t[:n, 0:1], in_=ss[h].t[:n, 0:1], func=AF.Sqrt, scale=1.0 / 256, bias=EPS), R=[ss[h]], W=[rs[h]])
                        V(lambda e: e.reciprocal(out=rs[h].t[:n, 0:1], in_=rs[h].t[:n, 0:1]), R=[rs[h]], W=[rs[h]])
                        V(lambda e: e.scalar_tensor_tensor(out=on[h].t[:n, :], in0=po_ap, scalar=rs[h].t[:n, 0:1], in1=gnr.t[:n, :], op0=ALU.mult, op1=ALU.mult), R=[po_buf, rs[h], gnr], W=[on[h]])
                        G(lambda e: e.tensor_tensor(out=og[h].t[:n, :], in0=on[h].t[:n, :], in1=grt_ap, op=ALU.mult), R=[on[h], grt_tl], W=[og[h]])
                        for j in range(2):
                            M(lambda e: e.transpose(out=pbf[si].t[:, 1 + j, :n], in_=og[h].t[:n, j * 128:(j + 1) * 128], identity=idb.t[:n, :n]), R=[og[h], idb], W=[ptob[si]], inc=(j == 1))
                        A(lambda e: e.activation(out=ost[h].t[:, :, :n], in_=pbf[si].t[:, 1:3, :n], func=AF.Copy), R=[ptob[si]], W=[ost[h]])
                        for j in range(2):
                            DMA("sp", AGT[h * 256 + j * 128:h * 256 + (j + 1) * 128, t0:t0 + n], ost[h].t[:, j, :n], f"g1{h}", R=[ost[h]])

                    for h in range(4):
                        DMA("sp", qT.t[:], GQT[h * 128:(h + 1) * 128, :], "g2", W=[qT]); DMA("sp", kT.t[:], GKT[h * 128:(h + 1) * 128, :], "g2", W=[kT])
                        for (b0, bn) in blocks:
                            M(lambda e: e.matmul(pgt.t[:, :bn], lhsT=wgt.t[0:16, h * 128:(h + 1) * 128], rhs=glr.t[0:16, b0:b0 + bn], start=True, stop=True), R=[wgt, glr], W=[pgt])
                            A(lambda e: e.activation(out=l1.t[:, b0:b0 + bn], in_=pgt.t[:, :bn], func=AF.Exp, scale=-1.0, bias=nb.t[:, h:h + 1]), R=[pgt, nb], W=[l1])
                            A(lambda e: e.activation(out=l1.t[:, b0:b0 + bn], in_=l1.t[:, b0:b0 + bn], func=AF.Ln, bias=1.0), R=[l1], W=[l1])
                            if b0 < NP:
                                V(lambda e: e.tensor_tensor_scan(out=bc.t[:, b0:b0 + bn], data0=cst.t[:, C_D0:C_D0 + bn], data1=l1.t[:, b0:b0 + bn], initial=0.0, op0=ALU.mult, op1=ALU.add), R=[cst, l1], W=[bc])
                            else:
                                V(lambda e: e.tensor_copy(out=bc.t[:, b0:b0 + bn], in_=l1.t[:, b0:b0 + bn]), R=[l1], W=[bc])
                        A(lambda e: e.activation(out=eb[h].t[:], in_=bc.t[:], func=AF.Exp, scale=-1.0 / 16), R=[bc], W=[eb[h]])
                        A(lambda e: e.activation(out=enb.t[:], in_=bc.t[:], func=AF.Exp, scale=1.0 / 16), R=[bc], W=[enb])
                        V(lambda e: e.scalar_tensor_tensor(out=qp[h].t[:], in0=qT.t[:], scalar=SC, in1=eb[h].t[:], op0=ALU.mult, op1=ALU.mult), R=[qT, eb[h]], W=[qp[h]])
                        G(lambda e: e.tensor_tensor(out=kp[h].t[:], in0=kT.t[:], in1=enb.t[:], op=ALU.mult), R=[kT, enb], W=[kp[h]])
                        V(lambda e: e.memset(St[h].t[:], 0.0), W=[St[h]]); V(lambda e: e.memset(Sb[h].t[:], 0.0), W=[Sb[h]])
                        M(lambda e: e.transpose(out=ptw.t[:], in_=kT.t[:, NP:NT], identity=idf), R=[kT, cst], W=[ptw])
                        A(lambda e: e.activation(out=ksT[h].t[:], in_=ptw.t[:], func=AF.Copy), R=[ptw], W=[ksT[h]])
                        V(lambda e: e.tensor_scalar_mul(out=qs[h].t[:], in0=qT.t[:, NP:NT], scalar1=SC), R=[qT], W=[qs[h]])
                    for c in range(NPT):
                        t0 = c * 128; vi = c % 2
                        DMA("sp", vt[vi].t[:], GV[t0:t0 + 128, :], f"g3{vi}", W=[vt[vi]]); DMA("sp", grt[vi].t[:], GR[t0:t0 + 128, :], f"g3{vi}", W=[grt[vi]])
                        for pr in range(2):
                            hs_ = (2 * pr, 2 * pr + 1)
                            for h in hs_:
                                si = h % 2
                                M(lambda e: e.transpose(out=pbf[si].t[:, 0, :], in_=kp[h].t[:, t0:t0 + 128], identity=idb.t[:]), R=[kp[h], idb], W=[ptkb[si]])
                                M(lambda e: e.matmul(psc[si].t[:, 0:128], lhsT=kp[h].t[:, t0:t0 + 128], rhs=qp[h].t[:, t0:t0 + 128], start=True, stop=True), R=[kp[h], qp[h]], W=[psc[si]])
                            for h in hs_:
                                si = h % 2
                                A(lambda e: e.activation(out=kpt[h].t[:], in_=pbf[si].t[:, 0, :], func=AF.Copy), R=[ptkb[si]], W=[kpt[h]])
                                V(lambda e: e.tensor_tensor(out=scm[h].t[:], in0=psc[si].t[:, 0:128], in1=cst.t[:, C_TRI:C_TRI + 128], op=ALU.mult), R=[psc[si], cst], W=[scm[h]])
                            for h in hs_:
                                si = h % 2; vh = vt[vi].t[:, h * 256:(h + 1) * 256]
                                M(lambda e: e.matmul(pab[si].t[:, 0, :], lhsT=scm[h].t[:], rhs=vh, start=True, stop=False), R=[scm[h], vt[vi]], W=[pob[si]], inc=False)
                                M(lambda e: e.matmul(pab[si].t[:, 0, :], lhsT=qp[h].t[:, t0:t0 + 128], rhs=Sb[h].t[:], start=False, stop=True), R=[qp[h], Sb[h]], W=[pob[si]])
                                M(lambda e: e.matmul(pab[si].t[:, 1, :], lhsT=kpt[h].t[:], rhs=vh, start=True, stop=True), R=[kpt[h], vt[vi]], W=[pkb[si]])
                            for h in hs_:
                                si = h % 2
                                ecol = eb[h].t[:, t0 + 127:t0 + 128]
                                V(lambda e: e.tensor_scalar_mul(out=St[h].t[:], in0=St[h].t[:], scalar1=ecol), R=[St[h], eb[h]], W=[St[h]])
                                V(lambda e: e.scalar_tensor_tensor(out=St[h].t[:], in0=pab[si].t[:, 1, :], scalar=ecol, in1=St[h].t[:], op0=ALU.mult, op1=ALU.add), R=[pkb[si], eb[h], St[h]], W=[St[h]])
                                A(lambda e: e.activation(out=Sb[h].t[:], in_=St[h].t[:], func=AF.Copy), R=[St[h]], W=[Sb[h]])
                            for h in hs_:
                                si = h % 2
                                gpost(128, t0, h, pab[si].t[:, 0, :], pob[si], grt[vi].t[:, h * 256:(h + 1) * 256], grt[vi], si)
                    for h in range(4):
                        DMA("sp", pgla[l, h], St[h].t[:], "g4", R=[St[h]])
                    DMA("sp", vt[0].t[:NS, :], GV[NP:NT, :], "g30", W=[vt[0]]); DMA("sp", grt[0].t[:NS, :], GR[NP:NT, :], "g30", W=[grt[0]])
                    for h in range(4):
                        si = h % 2
                        V(lambda e: e.tensor_tensor(out=Vbd.t[:], in0=vt[0].t[:NS, h * 256:(h + 1) * 256].unsqueeze(1).to_broadcast([NS, NS, 256]),
                                                    in1=cst.t[:NS, C_ID:C_ID + NS].unsqueeze(2).to_broadcast([NS, NS, 256]), op=ALU.mult), R=[vt[0], cst], W=[Vbd])
                        V(lambda e: e.tensor_tensor(out=Qbd.t[:], in0=qs[h].t[:].unsqueeze(1).to_broadcast([128, NS, NS]),
                                                    in1=cst.t[:, C_E16:C_E16 + 256].rearrange("p (a b) -> p a b", a=NS), op=ALU.mult), R=[qs[h], cst], W=[Qbd])
                        for b_ in range(NS):
                            bi_ = b_ % 2
                            DMA("sp", S0[bi_].t[:], gla0[l, b_, h], f"g5{bi_}", W=[S0[bi_]])
                            M(lambda e: e.matmul(psc[bi_].t[:], lhsT=ksT[h].t[:], rhs=Vbd.t[:, b_, :], start=True, stop=True), R=[ksT[h], Vbd], W=[psc[bi_]])
                            V(lambda e: e.scalar_tensor_tensor(out=Sn[bi_].t[:], in0=S0[bi_].t[:], scalar=eb[h].t[:, NP + b_:NP + b_ + 1], in1=psc[bi_].t[:], op0=ALU.mult, op1=ALU.add),
                              R=[S0[bi_], eb[h], psc[bi_]], W=[Sn[bi_]])
                            DMA("sp", sgla[l, b_, h], Sn[bi_].t[:], f"g6{bi_}", R=[Sn[bi_]])
                            M(lambda e: e.matmul(pgt.t[:NS, 0:256], lhsT=Qbd.t[:, b_, :], rhs=Sn[bi_].t[:], start=(b_ == 0), stop=(b_ == NS - 1)), R=[Qbd, Sn[bi_]], W=[pgt])
                        gpost(NS, NP, h, pgt.t[:NS, 0:256], pgt, grt[0].t[:NS, h * 256:(h + 1) * 256], grt[0], si)
                    S.barrier()

                with ExitStack() as ps:
                    TT = mk(ps, "s"); PP = mk(ps, "p")
                    rq = TT("rq", [128, 1024]); rk = TT("rk", [128, 1024]); rv = TT("rv", [128, 1024], BF16); rg = TT("rg", [128, 1024]); rt = TT("rt", [128, 128])
                    qr = TT("qr", [128, 1024]); kr = TT("kr", [128, 1024]); ta = TT("rta", [128, 8, 64]); tb = TT("rtb", [128, 8, 64])
                    qb = TT("rqb", [128, 1024], BF16); kb = TT("rkb", [128, 1024], BF16); k2 = TT("rk2", [128, 1024], BF16)
                    qTt = TT("rqT", [128, 8, 128], BF16); kTt = TT("rkT", [128, 8, 128], BF16)
                    S4 = [TT(f"rS{g_}", [128, 4, 128]) for g_ in range(2)]; Sb4 = [TT(f"rSb{g_}", [128, 4, 128], BF16) for g_ in range(2)]
                    scm4 = TT("rscm", [128, 4, 128], BF16); pos4 = TT("rpos", [128, 4, 128]); o_ = TT("ro", [128, 8, 128]); sq = TT("rsq", [128, 8, 128])
                    st1 = TT("rst1", [128, 16]); st2 = TT("rst2", [128, 16]); og = TT("rog", [128, 1024], BF16); ost = TT("rost", [128, 8, 128], BF16)
                    Vbd = TT("rVbd", [16, 16, 128]); qs = TT("rqs", [128, 16]); Qbd = TT("rQbd", [128, 16, 16]); S0 = TT("rS0", [128, 128]); Sn = TT("rSn", [128, 128])
                    ptq = [PP(f"rptq{i}", [128, 4, 128], BF16) for i in range(2)]; psc = PP("rpsc", [128, 4, 128]); po = PP("rpo", [128, 4, 128]); pi_ = PP("rpi", [128, 4, 128]); pkv = PP("rpkv", [128, 4, 128])
                    ptw = PP("rptw", [128, 16])
                    for g_ in range(2):
                        V(lambda e: e.memset(S4[g_].t[:], 0.0), W=[S4[g_]]); V(lambda e: e.memset(Sb4[g_].t[:], 0.0), W=[Sb4[g_]])

                    def rotary(src, dst, n, scale):
                        s4 = src.t[:n, :].rearrange("p (h i two) -> p h i two", h=8, two=2); d4 = dst.t[:n, :].rearrange("p (h i two) -> p h i two", h=8, two=2)
                        cb_ = rt.t[:n, 0:64].unsqueeze(1).to_broadcast([n, 8, 64]); sb_ = rt.t[:n, 64:128].unsqueeze(1).to_broadcast([n, 8, 64])
                        V(lambda e: e.tensor_tensor(out=ta.t[:n], in0=s4[:, :, :, 0], in1=cb_, op=ALU.mult), R=[src, rt], W=[ta])
                        V(lambda e: e.tensor_tensor(out=tb.t[:n], in0=s4[:, :, :, 1], in1=sb_, op=ALU.mult), R=[src, rt], W=[tb])
                        V(lambda e: e.tensor_tensor(out=d4[:, :, :, 0], in0=ta.t[:n], in1=tb.t[:n], op=ALU.subtract), R=[ta, tb], W=[dst])
                        V(lambda e: e.tensor_tensor(out=ta.t[:n], in0=s4[:, :, :, 1], in1=cb_, op=ALU.mult), R=[src, rt], W=[ta])
                        V(lambda e: e.tensor_tensor(out=tb.t[:n], in0=s4[:, :, :, 0], in1=sb_, op=ALU.mult), R=[src, rt], W=[tb])
                        V(lambda e: e.tensor_tensor(out=d4[:, :, :, 1], in0=ta.t[:n], in1=tb.t[:n], op=ALU.add), R=[ta, tb], W=[dst])
                        if scale != 1.0:
                            V(lambda e: e.tensor_scalar_mul(out=dst.t[:n, :], in0=dst.t[:n, :], scalar1=scale), R=[dst], W=[dst])

                    for ti, (r0, n) in enumerate(tiles):
                        DMA("sp", rq.t[:n, :], RQ[r0:r0 + n, :], "r0", W=[rq]); DMA("sp", rk.t[:n, :], RK[r0:r0 + n, :], "r0", W=[rk])
                        DMA("sp", rv.t[:n, :], RV[r0:r0 + n, :], "r0", W=[rv]); DMA("sp", rg.t[:n, :], RG[r0:r0 + n, :], "r0", W=[rg]); DMA("sp", rt.t[:n, :], rot[r0:r0 + n, :], "r0", W=[rt])
                        rotary(rq, qr, n, 1.0); rotary(rk, kr, n, SC)
                        if ti < NPT:
                            G(lambda e: e.tensor_copy(out=qb.t[:], in_=qr.t[:]), R=[qr], W=[qb]); G(lambda e: e.tensor_copy(out=kb.t[:], in_=kr.t[:]), R=[kr], W=[kb])
                            V(lambda e: e.tensor_tensor(out=k2.t[:].rearrange("p (h k) -> p h k", h=8), in0=kr.t[:].rearrange("p (h k) -> p h k", h=8),
                                                        in1=cst.t[:, C_KDEC:C_KDEC + 8].unsqueeze(2).to_broadcast([128, 8, 128]), op=ALU.mult), R=[kr, cst], W=[k2])
                            for (srcb, dstT) in ((qb, qTt), (kb, kTt)):
                                for hq in range(2):
                                    for j in range(4):
                                        hh_ = hq * 4 + j
                                        M(lambda e: e.transpose(out=ptq[hq].t[:, j, :], in_=srcb.t[:, hh_ * 128:(hh_ + 1) * 128], identity=idb.t[:]), R=[srcb, idb], W=[ptq[hq]], inc=(j == 3))
                                    A(lambda e: e.activation(out=dstT.t[:, hq * 4:hq * 4 + 4, :], in_=ptq[hq].t[:], func=AF.Copy), R=[ptq[hq]], W=[dstT])
                            for g_ in range(2):
                                hsl = [(j, g_ * 4 + j, slice((g_ * 4 + j) * 128, (g_ * 4 + j + 1) * 128)) for j in range(4)]
                                for (j, h, hs) in hsl:
                                    M(lambda e: e.matmul(psc.t[:, j, :], lhsT=kTt.t[:, h, :], rhs=qTt.t[:, h, :], start=True, stop=True), R=[kTt, qTt], W=[psc], inc=(j == 3))
                                V(lambda e: e.tensor_tensor(out=scm4.t[:], in0=psc.t[:], in1=cst.t[:, C_DH + g_ * 512:C_DH + (g_ + 1) * 512].rearrange("p (a b) -> p a b", a=4), op=ALU.mult), R=[psc, cst], W=[scm4])
                                for (j, h, hs) in hsl:
                                    M(lambda e: e.matmul(po.t[:, j, :], lhsT=scm4.t[:, j, :], rhs=rv.t[:, hs], start=True, stop=True), R=[scm4, rv], W=[po], inc=(j == 3))
                                for (j, h, hs) in hsl:
                                    M(lambda e: e.matmul(pi_.t[:, j, :], lhsT=qTt.t[:, h, :], rhs=Sb4[g_].t[:, j, :], start=True, stop=True), R=[qTt, Sb4[g_]], W=[pi_], inc=(j == 3))
                                for (j, h, hs) in hsl:
                                    M(lambda e: e.matmul(pkv.t[:, j, :], lhsT=k2.t[:, hs], rhs=rv.t[:, hs], start=True, stop=True), R=[k2, rv], W=[pkv], inc=(j == 3))
                                V(lambda e: e.tensor_tensor(out=S4[g_].t[:], in0=S4[g_].t[:], in1=cst.t[:, C_G128 + g_ * 4:C_G128 + g_ * 4 + 4].unsqueeze(2).to_broadcast([128, 4, 128]), op=ALU.mult), R=[S4[g_], cst], W=[S4[g_]])
                                V(lambda e: e.tensor_tensor(out=S4[g_].t[:], in0=S4[g_].t[:], in1=pkv.t[:], op=ALU.add), R=[S4[g_], pkv], W=[S4[g_]])
                                A(lambda e: e.activation(out=Sb4[g_].t[:], in_=S4[g_].t[:], func=AF.Copy), R=[S4[g_]], W=[Sb4[g_]])
                                A(lambda e: e.activation(out=pos4.t[:], in_=po.t[:], func=AF.Copy), R=[po], W=[pos4])
                                V(lambda e: e.tensor_tensor(out=o_.t[:, g_ * 4:g_ * 4 + 4, :], in0=pi_.t[:], in1=cst.t[:, C_GPOW + g_ * 4:C_GPOW + g_ * 4 + 4].unsqueeze(2).to_broadcast([128, 4, 128]), op=ALU.mult), R=[pi_, cst], W=[o_])
                                V(lambda e: e.tensor_tensor(out=o_.t[:, g_ * 4:g_ * 4 + 4, :], in0=o_.t[:, g_ * 4:g_ * 4 + 4, :], in1=pos4.t[:], op=ALU.add), R=[o_, pos4], W=[o_])
                        else:
                            for h in range(8):
                                hs = slice(h * 128, (h + 1) * 128)
                                M(lambda e: e.transpose(out=ptw.t[:], in_=qr.t[:NS, hs], identity=idf[:NS, :NS]), R=[qr, cst], W=[ptw])
                                V(lambda e: e.tensor_copy(out=qs.t[:], in_=ptw.t[:]), R=[ptw], W=[qs])
                                V(lambda e: e.tensor_tensor(out=Qbd.t[:], in0=qs.t[:].unsqueeze(1).to_broadcast([128, NS, NS]),
                                                            in1=cst.t[:, C_E16:C_E16 + 256].rearrange("p (a b) -> p a b", a=NS), op=ALU.mult), R=[qs, cst], W=[Qbd])
                                V(lambda e: e.tensor_tensor(out=Vbd.t[:], in0=rv.t[:NS, hs].unsqueeze(1).to_broadcast([NS, NS, 128]),
                                                            in1=cst.t[:NS, C_ID:C_ID + NS].unsqueeze(2).to_broadcast([NS, NS, 128]), op=ALU.mult), R=[rv, cst], W=[Vbd])
                                for b in range(NS):
                                    DMA("sp", S0.t[:], ret0[l, b, h], "r1", W=[S0])
                                    M(lambda e: e.matmul(pkv.t[:, 0, :], lhsT=kr.t[:NS, hs], rhs=Vbd.t[:, b, :], start=True, stop=True), R=[kr, Vbd], W=[pkv])
                                    V(lambda e: e.scalar_tensor_tensor(out=Sn.t[:], in0=S0.t[:], scalar=cst.t[:, C_G1 + h:C_G1 + h + 1], in1=pkv.t[:, 0, :], op0=ALU.mult, op1=ALU.add), R=[S0, cst, pkv], W=[Sn])
                                    DMA("sp", sret[l, b, h], Sn.t[:], "r2", R=[Sn])
                                    M(lambda e: e.matmul(po.t[:NS, 0, :], lhsT=Qbd.t[:, b, :], rhs=Sn.t[:], start=(b == 0), stop=(b == NS - 1)), R=[Qbd, Sn], W=[po])
                                V(lambda e: e.tensor_copy(out=o_.t[:NS, h, :], in_=po.t[:NS, 0, :]), R=[po], W=[o_])
                        V(lambda e: e.reduce_sum(out=st1.t[:n, 0:8], in_=o_.t[:n], axis=AX.X), R=[o_], W=[st1])
                        G(lambda e: e.tensor_tensor(out=sq.t[:n], in0=o_.t[:n], in1=o_.t[:n], op=ALU.mult), R=[o_], W=[sq])
                        V(lambda e: e.reduce_sum(out=st2.t[:n, 0:8], in_=sq.t[:n], axis=AX.X), R=[sq], W=[st2])
                        V(lambda e: e.tensor_scalar_mul(out=st1.t[:n, 0:8], in0=st1.t[:n, 0:8], scalar1=1.0 / 128), R=[st1], W=[st1])
                        V(lambda e: e.tensor_tensor(out=st1.t[:n, 8:16], in0=st1.t[:n, 0:8], in1=st1.t[:n, 0:8], op=ALU.mult), R=[st1], W=[st1])
                        V(lambda e: e.scalar_tensor_tensor(out=st2.t[:n, 0:8], in0=st2.t[:n, 0:8], scalar=1.0 / 128, in1=st1.t[:n, 8:16], op0=ALU.mult, op1=ALU.subtract), R=[st2, st1], W=[st2])
                        A(lambda e: e.activation(out=st2.t[:n, 0:8], in_=st2.t[:n, 0:8], func=AF.Sqrt, bias=EPS), R=[st2], W=[st2])
                        V(lambda e: e.reciprocal(out=st2.t[:n, 0:8], in_=st2.t[:n, 0:8]), R=[st2], W=[st2])
                        V(lambda e: e.tensor_tensor(out=o_.t[:n], in0=o_.t[:n], in1=st1.t[:n, 0:8].unsqueeze(2).to_broadcast([n, 8, 128]), op=ALU.subtract), R=[o_, st1], W=[o_])
                        V(lambda e: e.tensor_tensor(out=o_.t[:n], in0=o_.t[:n], in1=st2.t[:n, 0:8].unsqueeze(2).to_broadcast([n, 8, 128]), op=ALU.mult), R=[o_, st2], W=[o_])
                        G(lambda e: e.tensor_tensor(out=og.t[:n, :], in0=o_.t[:n].rearrange("p h k -> p (h k)"), in1=rg.t[:n, :], op=ALU.mult), R=[o_, rg], W=[og])
                        for hq in range(2):
                            for j in range(4):
                                hh_ = hq * 4 + j
                                M(lambda e: e.transpose(out=ptq[hq].t[:, j, :n], in_=og.t[:n, hh_ * 128:(hh_ + 1) * 128], identity=idb.t[:n, :n]), R=[og, idb], W=[ptq[hq]], inc=(j == 3))
                            A(lambda e: e.activation(out=ost.t[:, hq * 4:hq * 4 + 4, :n], in_=ptq[hq].t[:, :, :n], func=AF.Copy), R=[ptq[hq]], W=[ost])
                        DMA("sp", ART[:, r0:r0 + n].rearrange("(k p) t -> p k t", p=128), ost.t[:, :, :n], "r3", R=[ost])
                    for h in range(8):
                        DMA("sp", pret[l, h], S4[h // 4].t[:, h % 4, :], "r4", R=[S4[h // 4]])
                    S.barrier()

                with ExitStack() as ps:
                    TT = mk(ps, "s"); PP = mk(ps, "p")
                    mT = TT("mT", [128, 16, NT], BF16)
                    mTb = [Buf(f"mT{d_}") for d_ in range(16)]
                    with ExitStack() as ps1:
                        T1 = mk(ps1, "s"); P1 = mk(ps1, "p")
                        at = [T1(f"at{b}", [128, 8, NT], BF16) for b in range(3)]
                        for b, src in enumerate((A5T, AGT, ART)):
                            DMA("sp", at[b].t[:], src.rearrange("(k p) t -> p k t", p=128), f"at{b}", W=[at[b]])
                        wb = [[T1(f"wb{i}_{b}", [128, 8, 128], BF16) for b in range(3)] for i in range(2)]
                        gt = [T1(f"gt{i}", [128, 3, 512], BF16) for i in range(2)]
                        pmb = [[P1(f"pb{i}_{b}", [128, 512]) for b in range(3)] for i in range(2)]
                        t1 = [T1(f"t1_{i}", [128, 512]) for i in range(2)]; t2 = [T1(f"t2_{i}", [128, 512]) for i in range(2)]
                        ui = 0
                        for dt_ in range(16):
                            w = wb[dt_ % 2]
                            for b, wsrc in enumerate((w_s5o, w_glao, w_reto)):
                                DMA("pool", w[b].t[:], wsrc[l][:, dt_ * 128:(dt_ + 1) * 128].rearrange("(k p) e -> p k e", p=128), f"wb{dt_ % 2}", W=[w[b]])
                            for (b0, bn) in blocks:
                                i = ui % 2; ui += 1
                                for b in range(3):
                                    DMA("sp", gt[i].t[:, b, :bn], MGT[b * 2048 + dt_ * 128:b * 2048 + (dt_ + 1) * 128, b0:b0 + bn], f"gt{i}", W=[gt[i]])
                                    for kc in range(8):
                                        M(lambda e: e.matmul(pmb[i][b].t[:, :bn], lhsT=w[b].t[:, kc, :], rhs=at[b].t[:, kc, b0:b0 + bn], start=(kc == 0), stop=(kc == 7)),
                                          R=[w[b], at[b]], W=[pmb[i][b]], inc=(kc == 7))
                                V(lambda e: e.tensor_tensor(out=t1[i].t[:, :bn], in0=pmb[i][0].t[:, :bn], in1=gt[i].t[:, 0, :bn], op=ALU.mult), R=[pmb[i][0], gt[i]], W=[t1[i]])
                                V(lambda e: e.tensor_tensor(out=t2[i].t[:, :bn], in0=pmb[i][1].t[:, :bn], in1=gt[i].t[:, 1, :bn], op=ALU.mult), R=[pmb[i][1], gt[i]], W=[t2[i]])
                                G(lambda e: e.tensor_tensor(out=t1[i].t[:, :bn], in0=t1[i].t[:, :bn], in1=t2[i].t[:, :bn], op=ALU.add), R=[t1[i], t2[i]], W=[t1[i]])
                                V(lambda e: e.tensor_tensor(out=t2[i].t[:, :bn], in0=pmb[i][2].t[:, :bn], in1=gt[i].t[:, 2, :bn], op=ALU.mult), R=[pmb[i][2], gt[i]], W=[t2[i]])
                                V(lambda e: e.tensor_tensor(out=mT.t[:, dt_, b0:b0 + bn], in0=t1[i].t[:, :bn], in1=t2[i].t[:, :bn], op=ALU.add), R=[t1[i], t2[i]], W=[mTb[dt_]])
                        S.barrier()
                    wsl = [TT(f"w{i}", [128, 16, 512], BF16) for i in range(2)]
                    pm = [PP(f"pm{i}", [128, 512]) for i in range(4)]
                    xr = [TT(f"xr{i}", [128, 512]) for i in range(3)]
                    stf = [TT(f"stf{i}", [128, 512]) for i in range(3)]
                    ui = 0
                    for eb in range(4):
                        w = wsl[eb % 2]
                        DMA("pool", w.t[:], w_mix[l][:, eb * 512:(eb + 1) * 512].rearrange("(k p) e -> p k e", p=128), f"w{eb % 2}", W=[w])
                        for ti, (r0, n) in enumerate(tiles):
                            p = pm[ui % 4]; x_ = xr[ui % 3]; st = stf[ui % 3]; ci = ui % 3; ui += 1
                            DMA("sp", x_.t[:n, :], xrows(xsrc, ti)[:, eb * 512:(eb + 1) * 512], f"xr{ci}", W=[x_])
                            for kc in range(16):
                                M(lambda e: e.matmul(p.t[:n, :], lhsT=mT.t[:, kc, r0:r0 + n], rhs=w.t[:, kc, :], start=(kc == 0), stop=(kc == 15)),
                                  R=mTb + [w], W=[p], inc=(kc == 15))
                            V(lambda e: e.tensor_tensor(out=st.t[:n, :], in0=p.t[:n, :], in1=x_.t[:n, :], op=ALU.add), R=[p, x_], W=[st])
                            DMA("sp", XMID[r0:r0 + n, eb * 512:(eb + 1) * 512], st.t[:n, :], f"sf{ci}", R=[st])
                    S.barrier()

                with ExitStack() as ps:
                    TT = mk(ps, "s"); PP = mk(ps, "p")
                    hT = TT("hT2", [128, 16, NT], BF16)
                    hTb = [Buf(f"hT2{ti}") for ti in range(len(tiles))]
                    with ExitStack() as ps1:
                        norm_T(mk(ps1, "s"), mk(ps1, "p"), XMID, V_NF, hT, hTb, vecT, "n2")
                        S.barrier()
                    wg = [TT(f"wg{i}", [128, 16, 512], BF16) for i in range(2)]; wu = [TT(f"wu{i}", [128, 16, 512], BF16) for i in range(2)]
                    pg = [PP(f"pg{i}", [128, 512]) for i in range(4)]; pu = [PP(f"pu{i}", [128, 512]) for i in range(4)]
                    sg = [TT(f"sg{i}", [128, 512]) for i in range(4)]; stb = [TT(f"stb{i}", [128, 512], BF16) for i in range(6)]
                    ui = 0
                    for fb in range(DFF // 512):
                        i2 = fb % 2
                        DMA("pool", wg[i2].t[:], w_fg[l][:, fb * 512:(fb + 1) * 512].rearrange("(k p) e -> p k e", p=128), f"wg{i2}", W=[wg[i2]])
                        DMA("pool", wu[i2].t[:], w_fu[l][:, fb * 512:(fb + 1) * 512].rearrange("(k p) e -> p k e", p=128), f"wu{i2}", W=[wu[i2]])
                        for j in range(4):
                            ft = fb * 4 + j
                            for (b0, bn) in blocks:
                                i = ui % 4; si = ui % 6; ui += 1
                                bt = [hTb[t_] for t_ in blk_tiles(b0, bn)]
                                for kc in range(16):
                                    M(lambda e: e.matmul(pg[i].t[:, :bn], lhsT=wg[i2].t[:, kc, j * 128:(j + 1) * 128], rhs=hT.t[:, kc, b0:b0 + bn], start=(kc == 0), stop=(kc == 15)),
                                      R=bt + [wg[i2]], W=[pg[i]], inc=(kc == 15))
                                for kc in range(16):
                                    M(lambda e: e.matmul(pu[i].t[:, :bn], lhsT=wu[i2].t[:, kc, j * 128:(j + 1) * 128], rhs=hT.t[:, kc, b0:b0 + bn], start=(kc == 0), stop=(kc == 15)),
                                      R=bt + [wu[i2]], W=[pu[i]], inc=(kc == 15))
                                A(lambda e: e.activation(out=sg[i].t[:, :bn], in_=pg[i].t[:, :bn], func=AF.Silu), R=[pg[i]], W=[sg[i]])
                                V(lambda e: e.tensor_tensor(out=stb[si].t[:, :bn], in0=pu[i].t[:, :bn], in1=sg[i].t[:, :bn], op=ALU.mult), R=[pu[i], sg[i]], W=[stb[si]])
                                bts = blk_tiles(b0, bn)
                                if bn % 128 == 0:
                                    DMA("sp" if ui % 2 else "act", FFTL.rearrange("n p k t -> p n k t")[:, bts[0]:bts[0] + len(bts), ft, :],
                                        stb[si].t[:, :bn].rearrange("p (n t) -> p n t", t=128), f"sb{si}", R=[stb[si]])
                                else:
                                    DMA("sp", FFTL[bts[0], :, ft, :bn], stb[si].t[:, :bn], f"sb{si}", R=[stb[si]])
                    S.barrier()
                with ExitStack() as ps:
                    TT = mk(ps, "s"); PP = mk(ps, "p")
                    NF = DFF // 128
                    wd = TT("wd", [128, NF, 1024], BF16)
                    ff = [TT(f"ff{i}", [128, NF, 128], BF16) for i in range(3)]
                    pm = [PP(f"pm{i}", [128, 1024]) for i in range(3)]
                    xr = [TT(f"xr{i}", [128, 1024]) for i in range(2)]; stf = [TT(f"stf{i}", [128, 1024]) for i in range(2)]
                    ui = 0
                    for db in range(2):
                        for hf in range(2):
                            DMA("pool", wd.t[:, :, hf * 512:(hf + 1) * 512], w_fd[l][:, db * 1024 + hf * 512:db * 1024 + (hf + 1) * 512].rearrange("(k p) e -> p k e", p=128), "w0", W=[wd])
                        for ti, (r0, n) in enumerate(tiles):
                            p = pm[ui % 3]; x_ = xr[ui % 2]; st = stf[ui % 2]; ci = ui % 2; f_ = ff[ui % 3]; fi_ = ui % 3; ui += 1
                            DMA("sp", f_.t[:, :, :n], FFTL[ti][:, :, :n], f"ff{fi_}", W=[f_])
                            DMA("act", x_.t[:n, :], XMID[r0:r0 + n, db * 1024:(db + 1) * 1024], f"xr{ci}", W=[x_])
                            for hf in range(2):
                                for kc in range(NF):
                                    M(lambda e: e.matmul(p.t[:n, hf * 512:(hf + 1) * 512], lhsT=f_.t[:, kc, :n], rhs=wd.t[:, kc, hf * 512:(hf + 1) * 512], start=(kc == 0), stop=(kc == NF - 1)),
                                      R=[f_, wd], W=[p], inc=(kc == NF - 1 and hf == 1))
                            V(lambda e: e.tensor_tensor(out=st.t[:n, :], in0=p.t[:n, :], in1=x_.t[:n, :], op=ALU.add), R=[p, x_], W=[st])
                            DMA("sp", X1[r0:r0 + n, db * 1024:(db + 1) * 1024], st.t[:n, :], f"sf{ci}", R=[st])
                    S.barrier()
        with ExitStack() as ps:
            TT = mk(ps, "s")
            nf = TT("nf", [128, D]); junk = TT("fjunk", [128, D], BF16)
            DMA("sp", nf.t[:], nfin, "nf", W=[nf])
            xt = [TT(f"fxt{i}", [128, D]) for i in range(2)]; yo = [TT(f"fyo{i}", [128, D]) for i in range(2)]
            ss = [TT(f"fss{i}", [128, 16]) for i in range(2)]; rs = [TT(f"frs{i}", [128, 16]) for i in range(2)]
            for ti, (r0, n) in enumerate(tiles):
                s_ = ti % 2
                DMA("sp", xt[s_].t[:n, :], X1[r0:r0 + n, :], f"xr{s_}", W=[xt[s_]])
                V(lambda e: e.memset(ss[s_].t[:], 0.0), W=[ss[s_]])
                A(lambda e: e.activation(out=junk.t[:n, :], in_=xt[s_].t[:n, :], func=AF.Square, accum_out=ss[s_].t[:n, 0:1]), R=[xt[s_]], W=[junk, ss[s_]])
                A(lambda e: e.activation(out=rs[s_].t[:n, 0:1], in_=ss[s_].t[:n, 0:1], func=AF.Sqrt, scale=1.0 / D, bias=EPS), R=[ss[s_]], W=[rs[s_]])
                V(lambda e: e.reciprocal(out=rs[s_].t[:n, 0:1], in_=rs[s_].t[:n, 0:1]), R=[rs[s_]], W=[rs[s_]])
                V(lambda e: e.scalar_tensor_tensor(out=yo[s_].t[:n, :], in0=xt[s_].t[:n, :], scalar=rs[s_].t[:n, 0:1], in1=nf.t[:n, :], op0=ALU.mult, op1=ALU.mult),
                  R=[xt[s_], rs[s_], nf], W=[yo[s_]])
                DMA("sp", (yp[r0:r0 + n, :] if ti < NPT else ys[:, :]), yo[s_].t[:n, :], f"sf{s_}", R=[yo[s_]])
            S.barrier()
        S.barrier()
    return nc


def make_consts():
    c = np.zeros((128, C_N), np.float32)
    c[:, C_ID:C_ID + 128] = np.eye(128)
    for p in range(64):
        c[64 + p, C_J + p] = -1.0
        c[p, C_J + 64 + p] = 1.0
    s = np.arange(128)[:, None]; t = np.arange(128)[None, :]
    c[:, C_TRI:C_TRI + 128] = (t >= s)
    for gg in range(8):
        c[gg * 16:(gg + 1) * 16, C_GM + gg] = 1.0
    c[:64, C_SGN] = -1.0; c[64:, C_SGN] = 1.0
    for b in range(16):
        c[:, C_E16 + b * 16 + b] = 1.0
    c[:, C_D0:C_D0 + 512] = 1.0
    c[:, C_D0:C_D0 + 512:128] = 0.0
    lg = np.log1p(-np.power(2.0, -5.0 - np.arange(8, dtype=np.float64)))
    for h in range(8):
        c[:, C_DH + h * 128:C_DH + (h + 1) * 128] = np.where(t >= s, np.exp(lg[h] * (t - s)), 0.0)
        c[:, C_GPOW + h] = np.exp(lg[h] * (np.arange(128) + 1))
        c[:, C_KDEC + h] = np.exp(lg[h] * (127 - np.arange(128)))
        c[:, C_G128 + h] = np.exp(lg[h] * 128)
        c[:, C_G1 + h] = np.exp(lg[h])
    return c


def make_rot(NP):
    inv = (np.float32(1.0) / (np.float32(10000.0) ** np.linspace(0.0, 1.0, 64, dtype=np.float32))).astype(np.float32)
    pos = np.concatenate([np.arange(NP, dtype=np.float32), np.full(NS, 16384.0, np.float32)])
    ang = (pos[:, None] * inv[None, :]).astype(np.float32)
    return np.concatenate([np.cos(ang), np.sin(ang)], axis=1).astype(np.float32)


def prep_shared(inp, DEPTH):
    f = lambda a: np.ascontiguousarray(np.asarray(a, dtype=np.float32))
    a_re = f(inp["s5_a_re"]); a_im = f(inp["s5_a_im"]); ldt = f(inp["s5_log_dt"])
    s5dup = np.zeros((DEPTH, 3, 128, 64), np.float32); s5row = np.zeros((DEPTH, 3, NS, 4096), np.float32)
    vecs = np.zeros((DEPTH, V_N, 128), np.float32)
    for l in range(DEPTH):
        s5dup[l, 0] = np.concatenate([a_re[l].T, a_re[l].T], 0)
        s5dup[l, 1] = np.concatenate([a_im[l].T, a_im[l].T], 0)
        s5dup[l, 2] = np.broadcast_to(ldt[l][None, :], (128, 64))
        s5row[l, 0] = np.broadcast_to(a_re[l].reshape(1, 4096), (NS, 4096))
        s5row[l, 1] = np.broadcast_to(a_im[l].reshape(1, 4096), (NS, 4096))
        s5row[l, 2] = np.broadcast_to(np.repeat(ldt[l], 64)[None, :], (NS, 4096))
        vecs[l, V_NM:V_NM + 16] = f(inp["norm_mix"])[l].reshape(16, 128)
        vecs[l, V_NF:V_NF + 16] = f(inp["norm_ffn"])[l].reshape(16, 128)
        vecs[l, V_SD:V_SD + 8] = f(inp["s5_d"])[l].reshape(8, 128)
        vecs[l, V_BG:V_BG + 8] = f(inp["s5_b_glu"])[l].reshape(8, 128)
        vecs[l, V_GB:V_GB + 4] = f(inp["gla_b_gate"])[l].reshape(4, 128)
    sh = {
        "w_in": f(inp["w_in"])[:DEPTH], "s5dup": s5dup, "s5row": s5row,
        "s5b_re": f(inp["s5_b_re"])[:DEPTH], "s5b_im": f(inp["s5_b_im"])[:DEPTH],
        "s5c_re": f(inp["s5_c_re"])[:DEPTH].reshape(DEPTH, 1024, 64), "s5c_im": f(inp["s5_c_im"])[:DEPTH].reshape(DEPTH, 1024, 64),
        "w_glu": f(inp["s5_w_glu"])[:DEPTH], "w_s5o": f(inp["s5_w_out"])[:DEPTH], "w_gate": f(inp["gla_w_gate"])[:DEPTH],
        "gnorm": np.ascontiguousarray(np.broadcast_to(f(inp["gla_norm"])[:DEPTH, None, :], (DEPTH, 128, 256))),
        "w_glao": f(inp["gla_w_out"])[:DEPTH], "w_reto": f(inp["ret_w_out"])[:DEPTH], "w_mix": f(inp["w_mix_out"])[:DEPTH],
        "w_fg": f(inp["w_ffn_gate"])[:DEPTH], "w_fu": f(inp["w_ffn_up"])[:DEPTH], "w_fd": f(inp["w_ffn_down"])[:DEPTH],
        "vecs": vecs, "nfin": np.ascontiguousarray(np.broadcast_to(f(inp["norm_final"])[None, :], (128, D))),
        "consts": make_consts(),
    }
    return sh


def prep_core(inp, sh, seq, srow0, NPT, DEPTH):
    f = lambda a: np.ascontiguousarray(np.asarray(a, dtype=np.float32))
    NP = NPT * 128
    m = dict(sh)
    m["xp"] = f(inp["x_prompt"][seq, :NP])
    m["xs"] = f(inp["x_sample"][srow0:srow0 + NS, 0])
    m["s5r0"] = f(inp["state_s5_re"][:DEPTH, srow0:srow0 + NS]).reshape(DEPTH, NS, 4096)
    m["s5i0"] = f(inp["state_s5_im"][:DEPTH, srow0:srow0 + NS]).reshape(DEPTH, NS, 4096)
    m["gla0"] = f(inp["state_gla"][:DEPTH, srow0:srow0 + NS])
    m["ret0"] = f(inp["state_ret"][:DEPTH, srow0:srow0 + NS])
    m["rot"] = make_rot(NP)
    return m


_NC_CACHE = {}


def kernel(**inputs):
    NPT, DEPTH, NCORES = 16, 2, 8
    if "nc" not in _NC_CACHE:
        _NC_CACHE["nc"] = build(NPT, DEPTH)
    nc = _NC_CACHE["nc"]
    sh = prep_shared(inputs, DEPTH)
    in_maps = [prep_core(inputs, sh, c % 4, c * NS, NPT, DEPTH) for c in range(NCORES)]
    res = run_bass_kernel_spmd(nc, in_maps, core_ids=list(range(NCORES)))
    r = res.results
    g = lambda c, k: np.asarray(r[c][k], dtype=np.float32)
    y_prompt = np.stack([g(c, "yp") for c in range(4)], 0)
    y_sample = np.concatenate([g(c, "ys") for c in range(NCORES)], 0)[:, None, :]
    ps5 = np.stack([g(c, "ps5") for c in range(4)], 1)
    pgla = np.stack([g(c, "pgla") for c in range(4)], 1)
    pret = np.stack([g(c, "pret") for c in range(4)], 1)
    ss5 = np.concatenate([g(c, "ss5") for c in range(NCORES)], 1)
    sgla = np.concatenate([g(c, "sgla") for c in range(NCORES)], 1)
    sret = np.concatenate([g(c, "sret") for c in range(NCORES)], 1)
    return (y_prompt, y_sample,
            np.ascontiguousarray(ps5[..., :64]), np.ascontiguousarray(ps5[..., 64:]), pgla, pret,
            np.ascontiguousarray(ss5[..., :64]), np.ascontiguousarray(ss5[..., 64:]), sgla, sret)
```

```python
from contextlib import ExitStack
import numpy as np
import concourse.bass as bass
import concourse.mybir as mybir
from concourse.bass_utils import run_bass_kernel_spmd

F32 = mybir.dt.float32
BF16 = mybir.dt.bfloat16
AF = mybir.ActivationFunctionType
ALU = mybir.AluOpType
AX = mybir.AxisListType


class Buf:
    __slots__ = ("name", "w", "r")

    def __init__(self, name=""):
        self.name = name
        self.w = None
        self.r = {}


class Sched:
    ENG = ("pe", "act", "dve", "pool", "sp")
    NCHAN = 90

    def __init__(self, nc, es):
        self.nc = nc
        self.es = es
        self.eng = {"pe": nc.tensor, "act": nc.scalar, "dve": nc.vector, "pool": nc.gpsimd, "sp": nc.sync}
        self.sems = {}
        self.cnt = {}
        self.seen = {e: {} for e in self.ENG}
        self.pend = {e: ([], []) for e in self.ENG}
        self.pool = []
        for e in self.ENG:
            h = self.es.enter_context(self.nc.semaphore("E_" + e))
            self.sems["E_" + e] = h
            self.cnt["E_" + e] = 0
            nc.sync.sem_clear(h)
        for i in range(self.NCHAN):
            h = self.es.enter_context(self.nc.semaphore(f"DS{i}"))
            nc.sync.sem_clear(h)
            self.pool.append(h)
        nc.all_engine_barrier()

    def chan(self, name):
        key = "D_" + name
        self.sems[key] = self.pool.pop()
        self.cnt[key] = 0
        return key

    def _wait(self, e, tok):
        key, val = tok
        if key == "E_pe" and e == "pe":
            return
        if key.startswith("D_"):
            val = max(val, self.cnt[key])
        if self.seen[e].get(key, 0) >= val:
            return
        self.eng[e].wait_ge(self.sems[key], val)
        self.seen[e][key] = val

    def _deps(self, e, R, W):
        for b in R:
            if b.w is not None:
                self._wait(e, b.w)
        for b in W:
            if b.w is not None:
                self._wait(e, b.w)
            for k, v in b.r.items():
                self._wait(e, (k, v))

    def _commit(self, tok, R, W):
        k, v = tok
        for b in R:
            if b.r.get(k, 0) < v:
                b.r[k] = v
        for b in W:
            b.w = tok
            b.r = {}

    def op(self, e, fn, R=(), W=(), inc=True):
        self._deps(e, R, W)
        inst = fn(self.eng[e])
        pr, pw = self.pend[e]
        pr.extend(R)
        pw.extend(W)
        if not inc:
            return None
        own = "E_" + e
        self.cnt[own] += 1
        inst.then_inc(self.sems[own], 1)
        tok = (own, self.cnt[own])
        self._commit(tok, pr, pw)
        self.pend[e] = ([], [])
        return tok

    def dma(self, q, out, in_, ch, R=(), W=()):
        self._deps(q, R, W)
        inst = self.eng[q].dma_start(out=out, in_=in_)
        self.cnt[ch] += 16
        inst.then_inc(self.sems[ch], 16)
        tok = (ch, self.cnt[ch])
        self._commit(tok, R, W)
        return tok

    def barrier(self):
        toks = [(k, v) for k, v in self.cnt.items() if v > 0]
        for e in self.ENG:
            for t in toks:
                self._wait(e, t)


D = 2048
WIN = 14352
DFF = 5632
NS = 16
EPS = 1e-6
DBG = False
C_ID, C_J, C_TRI, C_GM, C_SGN, C_E16, C_D0, C_DH, C_GPOW, C_KDEC, C_G128, C_G1, C_N = (
    0, 128, 256, 384, 392, 400, 656, 1168, 2192, 2200, 2208, 2216, 2224)
V_NM, V_NF, V_SD, V_BG, V_GB, V_N = 0, 16, 32, 40, 48, 52


class Tl:
    __slots__ = ("t", "b")

    def __init__(self, t, name):
        self.t = t
        self.b = Buf(name)


def build(NPT, DEPTH):
    NP = NPT * 128
    NT = NP + NS
    tiles = [(i * 128, 128) for i in range(NPT)] + [(NP, NS)]
    blocks = [(b0, min(512, NP - b0)) for b0 in range(0, NP, 512)] + [(NP, NS)]
    nc = bass.Bass("TRN2", target_bir_lowering=False)

    def din(name, shape, dt=F32):
        return nc.dram_tensor(name, list(shape), dt, kind="ExternalInput").ap()

    def dout(name, shape, dt=F32):
        return nc.dram_tensor(name, list(shape), dt, kind="ExternalOutput").ap()

    def dscr(name, shape, dt=F32):
        if DBG:
            return nc.dram_tensor(name, list(shape), dt, kind="ExternalOutput").ap()
        return nc.dram_tensor(name, list(shape), dt).ap()

    xp = din("xp", [NP, D]); xs = din("xs", [NS, D])
    s5r0 = din("s5r0", [DEPTH, NS, 4096]); s5i0 = din("s5i0", [DEPTH, NS, 4096])
    gla0 = din("gla0", [DEPTH, NS, 4, 128, 256]); ret0 = din("ret0", [DEPTH, NS, 8, 128, 128])
    w_in = din("w_in", [DEPTH, D, WIN])
    s5dup = din("s5dup", [DEPTH, 3, 128, 64]); s5row = din("s5row", [DEPTH, 3, NS, 4096])
    s5b_re = din("s5b_re", [DEPTH, 64, 64, 16]); s5b_im = din("s5b_im", [DEPTH, 64, 64, 16])
    s5c_re = din("s5c_re", [DEPTH, 1024, 64]); s5c_im = din("s5c_im", [DEPTH, 1024, 64])
    w_glu = din("w_glu", [DEPTH, 1024, 1024]); w_s5o = din("w_s5o", [DEPTH, 1024, D])
    w_gate = din("w_gate", [DEPTH, 16, 512]); gnorm = din("gnorm", [DEPTH, 128, 256])
    w_glao = din("w_glao", [DEPTH, 1024, D]); w_reto = din("w_reto", [DEPTH, 1024, D])
    w_mix = din("w_mix", [DEPTH, D, D])
    w_fg = din("w_fg", [DEPTH, D, DFF]); w_fu = din("w_fu", [DEPTH, D, DFF]); w_fd = din("w_fd", [DEPTH, DFF, D])
    vecs = din("vecs", [DEPTH, V_N, 128]); nfin = din("nfin", [128, D])
    consts = din("consts", [128, C_N]); rot = din("rot", [NT, 128])

    yp = dout("yp", [NP, D]); ys = dout("ys", [NS, D])
    ps5 = dout("ps5", [DEPTH, 64, 128]); pgla = dout("pgla", [DEPTH, 4, 128, 256]); pret = dout("pret", [DEPTH, 8, 128, 128])
    ss5 = dout("ss5", [DEPTH, NS, 64, 128]); sgla = dout("sgla", [DEPTH, NS, 4, 128, 256]); sret = dout("sret", [DEPTH, NS, 8, 128, 128])

    UT = dscr("UT", [1024, NT], BF16); GQT = dscr("GQT", [512, NT]); GKT = dscr("GKT", [512, NT]); GLRT = dscr("GLRT", [16, NT])
    GV = dscr("GV", [NT, 1024], BF16); GR = dscr("GR", [NT, 1024]); RQ = dscr("RQ", [NT, 1024]); RK = dscr("RK", [NT, 1024])
    RV = dscr("RV", [NT, 1024], BF16); RG = dscr("RG", [NT, 1024]); MGT = dscr("MGT", [6144, NT], BF16)
    Y5A = dscr("Y5A", [1024, NT], BF16); A5T = dscr("A5T", [1024, NT], BF16); AGT = dscr("AGT", [1024, NT], BF16); ART = dscr("ART", [1024, NT], BF16)
    XMID = dscr("XMID", [NT, D]); X1 = dscr("X1", [NT, D]); FFTL = dscr("FFTL", [NPT + 1, 128, DFF // 128, 128], BF16)

    def xrows(src, ti):
        r0, n = tiles[ti]
        if src is None:
            return xp[r0:r0 + n, :] if ti < NPT else xs[:, :]
        return src[r0:r0 + n, :]

    with ExitStack() as es:
        S = Sched(nc, es)
        chans = {}

        def ch(name):
            if name not in chans:
                chans[name] = S.chan(name)
            return chans[name]

        def V(fn, R=(), W=(), inc=True): return S.op("dve", fn, [x.b if isinstance(x, Tl) else x for x in R], [x.b if isinstance(x, Tl) else x for x in W], inc)
        def A(fn, R=(), W=(), inc=True): return S.op("act", fn, [x.b if isinstance(x, Tl) else x for x in R], [x.b if isinstance(x, Tl) else x for x in W], inc)
        def G(fn, R=(), W=(), inc=True): return S.op("pool", fn, [x.b if isinstance(x, Tl) else x for x in R], [x.b if isinstance(x, Tl) else x for x in W], inc)
        def M(fn, R=(), W=(), inc=True): return S.op("pe", fn, [x.b if isinstance(x, Tl) else x for x in R], [x.b if isinstance(x, Tl) else x for x in W], inc)

        def DMA(q, out, in_, chn, R=(), W=()):
            if len(W) == 0 and q == "sp":
                q = "act"
            elif len(W) > 0 and q == "act":
                q = "sp"
            return S.dma(q, out, in_, ch(chn), [x.b if isinstance(x, Tl) else x for x in R], [x.b if isinstance(x, Tl) else x for x in W])

        uid = [0]

        def mk(stack, kind):
            def f(name, shape, dt=F32):
                alloc = nc.sbuf_tensor if kind == "s" else nc.psum_tensor
                uid[0] += 1
                name = f"{name}_{uid[0]}"
                return Tl(stack.enter_context(alloc(name, list(shape), dt)), name)
            return f

        T0 = mk(es, "s")
        cst = T0("cst", [128, C_N]); idb = T0("idb", [128, 128], BF16)
        DMA("sp", cst.t[:], consts, "cst", W=[cst])
        V(lambda e: e.tensor_copy(out=idb.t[:], in_=cst.t[:, C_ID:C_ID + 128]), R=[cst], W=[idb])
        idf = cst.t[:, C_ID:C_ID + 128]

        def rms_rstd(TT, src_ap, src_tl, n, width, name):
            junk = TT(name + "_j", [128, width]); ss = TT(name + "_s", [128, 1]); rs = TT(name + "_r", [128, 1])
            return junk, ss, rs

        def norm_T(TT, PP, src, gcol, hT, hTb, vecT, pfx):
            xt = [TT(pfx + f"xt{i}", [128, D]) for i in range(2)]
            junk = TT(pfx + "junk", [128, D], BF16)
            ss = [TT(pfx + f"ss{i}", [128, 16]) for i in range(2)]
            rs = [TT(pfx + f"rs{i}", [128, 16]) for i in range(2)]
            xn = [TT(pfx + f"xn{i}", [128, D], BF16) for i in range(2)]
            ptr = [PP(pfx + f"ptr{i}", [128, 4, 128], BF16) for i in range(2)]
            for ti, (r0, n) in enumerate(tiles):
                s = ti % 2
                DMA("sp", xt[s].t[:n, :], xrows(src, ti), pfx + f"xt{s}", W=[xt[s]])
                V(lambda e: e.memset(ss[s].t[:], 0.0), W=[ss[s]])
                A(lambda e: e.activation(out=junk.t[:n, :], in_=xt[s].t[:n, :], func=AF.Square, accum_out=ss[s].t[:n, 0:1]), R=[xt[s]], W=[junk, ss[s]])
                A(lambda e: e.activation(out=rs[s].t[:n, 0:1], in_=ss[s].t[:n, 0:1], func=AF.Sqrt, scale=1.0 / D, bias=EPS), R=[ss[s]], W=[rs[s]])
                V(lambda e: e.reciprocal(out=rs[s].t[:n, 0:1], in_=rs[s].t[:n, 0:1]), R=[rs[s]], W=[rs[s]])
                A(lambda e: e.activation(out=xn[s].t[:n, :], in_=xt[s].t[:n, :], func=AF.Copy, scale=rs[s].t[:n, 0:1]), R=[xt[s], rs[s]], W=[xn[s]])
                for q in range(4):
                    p = ptr[q % 2]
                    for j in range(4):
                        kc = q * 4 + j
                        M(lambda e: e.transpose(out=p.t[:, j, :n], in_=xn[s].t[:n, kc * 128:(kc + 1) * 128], identity=idb.t[:n, :n]),
                          R=[xn[s], idb], W=[p], inc=(j == 3))
                    V(lambda e: e.tensor_tensor(out=hT.t[:, q * 4:q * 4 + 4, r0:r0 + n], in0=p.t[:, :, :n],
                                                in1=vecT.t[:, gcol + q * 4:gcol + q * 4 + 4].unsqueeze(2).to_broadcast([128, 4, n]), op=ALU.mult),
                      R=[p, vecT], W=[hTb[ti]])
            if DBG and pfx == "n1" and not hasattr(nc, "_dbg1"):
                nc._dbg1 = 1
                d1 = dscr("DBG_ss1", [128, 16]); d2 = dscr("DBG_rs1", [128, 16]); d3 = dscr("DBG_xt1", [128, D]); d4 = dscr("DBG_xn1", [128, D], BF16)
                DMA("sp", d1, ss[1].t[:], "dbg", R=[ss[1]]); DMA("sp", d2, rs[1].t[:], "dbg", R=[rs[1]])
                DMA("sp", d3, xt[1].t[:], "dbg", R=[xt[1]]); DMA("sp", d4, xn[1].t[:], "dbg", R=[xn[1]])

        def sincos(th, sn, cs, t1, t2, RW):
            A(lambda e: e.activation(out=t1, in_=th, func=AF.Sin, scale=1.0 / 16), **RW)
            V(lambda e: e.tensor_tensor(out=t1, in0=t1, in1=t1, op=ALU.mult), **RW)
            V(lambda e: e.tensor_scalar(out=cs, in0=t1, scalar1=-2.0, scalar2=1.0, op0=ALU.mult, op1=ALU.add), **RW)
            A(lambda e: e.activation(out=sn, in_=th, func=AF.Sin, scale=1.0 / 8), **RW)
            for _ in range(3):
                V(lambda e: e.tensor_tensor(out=t1, in0=sn, in1=cs, op=ALU.mult), **RW)
                V(lambda e: e.tensor_tensor(out=t2, in0=sn, in1=sn, op=ALU.mult), **RW)
                V(lambda e: e.tensor_tensor(out=cs, in0=cs, in1=cs, op=ALU.mult), **RW)
                V(lambda e: e.tensor_tensor(out=cs, in0=cs, in1=t2, op=ALU.subtract), **RW)
                V(lambda e: e.tensor_scalar_mul(out=sn, in0=t1, scalar1=2.0), **RW)

        def blk_tiles(b0, bn):
            return [ti for ti, (r0, n) in enumerate(tiles) if r0 < b0 + bn and r0 + n > b0]

        for l in range(DEPTH):
            xsrc = None if l == 0 else X1
            with ExitStack() as ls:
                TL = mk(ls, "s")
                vecT = TL("vecT", [128, V_N])
                with ExitStack() as ps:
                    TT = mk(ps, "s"); PP = mk(ps, "p")
                    vr = TT("vr", [V_N, 128]); pv = PP("pv", [128, V_N])
                    DMA("sp", vr.t[:], vecs[l], "vr", W=[vr])
                    M(lambda e: e.transpose(out=pv.t[:], in_=vr.t[:], identity=idf[:V_N, :V_N]), R=[vr, cst], W=[pv])
                    V(lambda e: e.tensor_copy(out=vecT.t[:], in_=pv.t[:]), R=[pv], W=[vecT])
                    S.barrier()

                with ExitStack() as ps:
                    TT = mk(ps, "s"); PP = mk(ps, "p")
                    hT = TT("hT", [128, 16, NT], BF16)
                    hTb = [Buf(f"hT{ti}") for ti in range(len(tiles))]
                    with ExitStack() as ps1:
                        norm_T(mk(ps1, "s"), mk(ps1, "p"), xsrc, V_NM, hT, hTb, vecT, "n1")
                        S.barrier()
                    if DBG and l == 0:
                        dbg_hT = dscr("DBG_hT", [128, 16, NT], BF16)
                        DMA("sp", dbg_hT, hT.t[:], "dbg", R=hTb)
                    wsl = [TT(f"w{i}", [128, 16, 512], BF16) for i in range(3)]
                    pm = [PP(f"pm{i}", [128, 512]) for i in range(4)]
                    stf = [TT(f"stf{i}", [128, 512]) for i in range(4)]
                    stb = [TT(f"stb{i}", [128, 512], BF16) for i in range(4)]
                    segs = [(0, 1024, "F", AF.Copy, UT, True), (1024, 512, "F", AF.Copy, GQT, False), (1536, 512, "F", AF.Copy, GKT, False),
                            (2048, 1024, "T", AF.Copy, GV, True), (3072, 1024, "T", AF.Silu, GR, False), (4096, 16, "F", AF.Copy, GLRT, False),
                            (4112, 1024, "T", AF.Copy, RQ, False), (5136, 1024, "T", AF.Copy, RK, False), (6160, 1024, "T", AF.Copy, RV, True),
                            (7184, 1024, "T", AF.Silu, RG, False), (8208, 6144, "F", AF.Sigmoid, MGT, True)]
                    wi = 0; ui = 0
                    for (c0, ncol, form, func, dst, isb) in segs:
                        for cb in range(0, ncol, 512):
                            ncb = min(512, ncol - cb)
                            w = wsl[wi % 3]
                            DMA("pool", w.t[:, :, :ncb], w_in[l][:, c0 + cb:c0 + cb + ncb].rearrange("(k p) e -> p k e", p=128), f"w{wi % 3}", W=[w])
                            wi += 1
                            if form == "T":
                                for ti, (r0, n) in enumerate(tiles):
                                    p = pm[ui % 4]; st = (stb if isb else stf)[ui % 4]; ui += 1
                                    for kc in range(16):
                                        M(lambda e: e.matmul(p.t[:n, :ncb], lhsT=hT.t[:, kc, r0:r0 + n], rhs=w.t[:, kc, :ncb], start=(kc == 0), stop=(kc == 15)),
                                          R=[hTb[ti], w], W=[p], inc=(kc == 15))
                                    A(lambda e: e.activation(out=st.t[:n, :ncb], in_=p.t[:n, :ncb], func=func), R=[p], W=[st])
                                    DMA("sp", dst[r0:r0 + n, cb:cb + ncb], st.t[:n, :ncb], ("sb" if isb else "sf") + str((ui - 1) % 4), R=[st])
                            else:
                                for j0 in range(0, ncb, 128):
                                    m = min(128, ncb - j0)
                                    for (b0, bn) in blocks:
                                        p = pm[ui % 4]; st = (stb if isb else stf)[ui % 4]; ui += 1
                                        for kc in range(16):
                                            M(lambda e: e.matmul(p.t[:m, :bn], lhsT=w.t[:, kc, j0:j0 + m], rhs=hT.t[:, kc, b0:b0 + bn], start=(kc == 0), stop=(kc == 15)),
                                              R=[hTb[t_] for t_ in blk_tiles(b0, bn)] + [w], W=[p], inc=(kc == 15))
                                        A(lambda e: e.activation(out=st.t[:m, :bn], in_=p.t[:m, :bn], func=func), R=[p], W=[st])
                                        DMA("sp", dst[cb + j0:cb + j0 + m, b0:b0 + bn], st.t[:m, :bn], ("sb" if isb else "sf") + str((ui - 1) % 4), R=[st])
                    S.barrier()

                PI = float(np.pi)
                with ExitStack() as ps:
                    TT = mk(ps, "s"); PP = mk(ps, "p")
                    EC = TT("EC", [128, 64, 128]); ES = TT("ES", [128, 64, 128])
                    sp_ = TT("s5p", [128, 16, 64]); prm = TT("prm", [128, 3, 64])
                    X1_ = TT("X1_", [128, 64, 16]); X2_ = TT("X2_", [128, 64, 16])
                    CA = TT("CA", [128, 8, 128]); CB = TT("CB", [128, 8, 128])
                    xl = TT("xl", [128, 64])
                    DMA("sp", prm.t[:], s5dup[l].rearrange("a p g -> p a g"), "prm", W=[prm])
                    ar = prm.t[:, 0, :]; ai = prm.t[:, 1, :]; ldt = prm.t[:, 2, :]
                    (dtb, mag, th, a1, sn, cs, abr, abi, den, nr, fr, fi, tq, tr) = [sp_.t[:, i_, :] for i_ in range(14)]
                    RW = dict(R=[sp_, prm], W=[sp_])
                    A(lambda e: e.activation(out=dtb, in_=ldt, func=AF.Exp), **RW)
                    V(lambda e: e.tensor_tensor(out=tq, in0=ar, in1=dtb, op=ALU.mult), **RW)
                    A(lambda e: e.activation(out=mag, in_=tq, func=AF.Exp), **RW)
                    V(lambda e: e.tensor_tensor(out=th, in0=ai, in1=dtb, op=ALU.mult), **RW)
                    sincos(th, sn, cs, a1, tq, RW)
                    V(lambda e: e.tensor_tensor(out=abr, in0=mag, in1=cs, op=ALU.mult), **RW)
                    V(lambda e: e.tensor_tensor(out=abi, in0=mag, in1=sn, op=ALU.mult), **RW)
                    V(lambda e: e.tensor_tensor(out=den, in0=ar, in1=ar, op=ALU.mult), **RW)
                    V(lambda e: e.tensor_tensor(out=tq, in0=ai, in1=ai, op=ALU.mult), **RW)
                    V(lambda e: e.tensor_tensor(out=den, in0=den, in1=tq, op=ALU.add), **RW)
                    V(lambda e: e.reciprocal(out=den, in_=den), **RW)
                    V(lambda e: e.tensor_scalar_add(out=nr, in0=abr, scalar1=-1.0), **RW)
                    V(lambda e: e.tensor_tensor(out=tq, in0=nr, in1=ar, op=ALU.mult), **RW)
                    V(lambda e: e.tensor_tensor(out=tr, in0=abi, in1=ai, op=ALU.mult), **RW)
                    V(lambda e: e.tensor_tensor(out=tq, in0=tq, in1=tr, op=ALU.add), **RW)
                    V(lambda e: e.tensor_tensor(out=fr, in0=tq, in1=den, op=ALU.mult), **RW)
                    V(lambda e: e.tensor_tensor(out=tq, in0=abi, in1=ar, op=ALU.mult), **RW)
                    V(lambda e: e.tensor_tensor(out=tr, in0=nr, in1=ai, op=ALU.mult), **RW)
                    V(lambda e: e.tensor_tensor(out=tq, in0=tq, in1=tr, op=ALU.subtract), **RW)
                    V(lambda e: e.tensor_tensor(out=fi, in0=tq, in1=den, op=ALU.mult), **RW)
                    xlb = [Buf("xl0"), Buf("xl1")]
                    V(lambda e: e.memset(xl.t[:], 0.0), W=xlb)
                    with ExitStack() as ps1:
                        T1 = mk(ps1, "s")
                        t1 = T1("et1", [128, 64, 64]); t2 = T1("et2", [128, 64, 64])
                        A_ = T1("A_", [128, 64, 16]); Bm = T1("Bm", [128, 64, 16]); tb = T1("tb", [128, 64, 16])
                        V(lambda e: e.tensor_copy(out=EC.t[:, :, 0:1], in_=cs.unsqueeze(2)), R=[sp_], W=[EC])
                        V(lambda e: e.tensor_copy(out=ES.t[:, :, 0:1], in_=sn.unsqueeze(2)), R=[sp_], W=[ES])
                        for s_ in range(7):
                            n_ = 2 ** s_
                            cb_ = EC.t[:, :, n_ - 1:n_].to_broadcast([128, 64, n_]); sb_ = ES.t[:, :, n_ - 1:n_].to_broadcast([128, 64, n_])
                            V(lambda e: e.tensor_tensor(out=t1.t[:, :, :n_], in0=EC.t[:, :, 0:n_], in1=cb_, op=ALU.mult), R=[EC], W=[t1])
                            V(lambda e: e.tensor_tensor(out=t2.t[:, :, :n_], in0=ES.t[:, :, 0:n_], in1=sb_, op=ALU.mult), R=[ES], W=[t2])
                            V(lambda e: e.tensor_tensor(out=t1.t[:, :, :n_], in0=t1.t[:, :, :n_], in1=t2.t[:, :, :n_], op=ALU.subtract), R=[t1, t2], W=[t1])
                            V(lambda e: e.tensor_tensor(out=t2.t[:, :, :n_], in0=EC.t[:, :, 0:n_], in1=sb_, op=ALU.mult), R=[EC, ES], W=[t2])
                            V(lambda e: e.tensor_copy(out=EC.t[:, :, n_:2 * n_], in_=t1.t[:, :, :n_]), R=[t1], W=[EC])
                            V(lambda e: e.tensor_tensor(out=t1.t[:, :, :n_], in0=ES.t[:, :, 0:n_], in1=cb_, op=ALU.mult), R=[EC, ES], W=[t1])
                            V(lambda e: e.tensor_tensor(out=ES.t[:, :, n_:2 * n_], in0=t1.t[:, :, :n_], in1=t2.t[:, :, :n_], op=ALU.add), R=[t1, t2], W=[ES])
                        DMA("sp", A_.t[0:64], s5b_re[l].rearrange("g p n -> p g n"), "bb", W=[A_]); DMA("sp", A_.t[64:128], s5b_im[l].rearrange("g p n -> p g n"), "bb", W=[A_])
                        DMA("sp", Bm.t[0:64], s5b_im[l].rearrange("g p n -> p g n"), "bb", W=[Bm]); DMA("sp", Bm.t[64:128], s5b_re[l].rearrange("g p n -> p g n"), "bb", W=[Bm])
                        V(lambda e: e.tensor_scalar_mul(out=Bm.t[0:64], in0=Bm.t[0:64], scalar1=-1.0), R=[Bm], W=[Bm])
                        frb = fr.unsqueeze(2).to_broadcast([128, 64, 16]); fib = fi.unsqueeze(2).to_broadcast([128, 64, 16])
                        V(lambda e: e.tensor_tensor(out=X1_.t[:], in0=A_.t[:], in1=frb, op=ALU.mult), R=[A_, sp_], W=[X1_])
                        V(lambda e: e.tensor_tensor(out=tb.t[:], in0=Bm.t[:], in1=fib, op=ALU.mult), R=[Bm, sp_], W=[tb])
                        V(lambda e: e.tensor_tensor(out=X1_.t[:], in0=X1_.t[:], in1=tb.t[:], op=ALU.add), R=[X1_, tb], W=[X1_])
                        V(lambda e: e.tensor_tensor(out=X2_.t[:], in0=A_.t[:], in1=fib, op=ALU.mult), R=[A_, sp_], W=[X2_])
                        V(lambda e: e.tensor_tensor(out=tb.t[:], in0=Bm.t[:], in1=frb, op=ALU.mult), R=[Bm, sp_], W=[tb])
                        V(lambda e: e.tensor_tensor(out=X2_.t[:], in0=X2_.t[:], in1=tb.t[:], op=ALU.subtract), R=[X2_, tb], W=[X2_])
                        DMA("sp", CA.t[:, :, 0:64], s5c_re[l].rearrange("(t q) p -> q t p", q=128), "cc", W=[CA]); DMA("sp", CA.t[:, :, 64:128], s5c_im[l].rearrange("(t q) p -> q t p", q=128), "cc", W=[CA])
                        DMA("sp", CB.t[:, :, 0:64], s5c_im[l].rearrange("(t q) p -> q t p", q=128), "cc", W=[CB]); DMA("sp", CB.t[:, :, 64:128], s5c_re[l].rearrange("(t q) p -> q t p", q=128), "cc", W=[CB])
                        V(lambda e: e.tensor_scalar_mul(out=CA.t[:, :, 64:128], in0=CA.t[:, :, 64:128], scalar1=-1.0), R=[CA], W=[CA])
                        V(lambda e: e.tensor_scalar_mul(out=CB.t[:], in0=CB.t[:], scalar1=-1.0), R=[CB], W=[CB])
                        S.barrier()
                    cT = EC.t[:, :, 127]; sT = ES.t[:, :, 127]
                    with ExitStack() as ps1:
                        T1 = mk(ps1, "s")
                        LB = [T1(f"LB{v}", [128, 8, 128], BF16) for v in range(2)]; LC = [T1(f"LC{v}", [128, 8, 128], BF16) for v in range(2)]
                        for v in range(2):
                            G(lambda e: e.memset(LC[v].t[:], 0.0), W=[LC[v]])
                        uT = T1("uT", [128, NT], BF16); ypre = T1("ypre", [128, NT]); y5s = T1("y5s", [128, NT], BF16)
                        m1 = [T1(f"m1{i}", [128, 4, 128]) for i in range(2)]; m2 = [T1(f"m2{i}", [128, 4, 128]) for i in range(2)]
                        win = [T1(f"win{i}", [128, 4, 128]) for i in range(2)]; wv = [T1(f"wv{i}", [128, 4, 128]) for i in range(2)]
                        a1b = [T1(f"a1b{i}", [128, 4, 128], BF16) for i in range(2)]; a2b = [T1(f"a2b{i}", [128, 4, 128], BF16) for i in range(2)]
                        c1 = [T1(f"c1{i}", [128, 16]) for i in range(2)]; c2 = [T1(f"c2{i}", [128, 16]) for i in range(2)]
                        rp = T1("rp", [16, 3, 512]); rw = T1("rw", [16, 8, 512]); x0r = T1("x0r", [16, 512]); x0i = T1("x0i", [16, 512])
                        ax0 = T1("ax0", [16, 8, 128]); xnew = T1("xnew", [16, 8, 128]); xnb = T1("xnb", [16, 8, 128], BF16); xnT = T1("xnT", [128, 8, 16], BF16)
                        gz = T1("gz", [128, 512]); gs = T1("gs", [128, 512])
                        P1 = mk(ps1, "p")
                        PS1 = [P1(f"PS1{i}", [128, 4, 128]) for i in range(2)]; PS2 = [P1(f"PS2{i}", [128, 4, 128]) for i in range(2)]
                        PY = [P1(f"PY{i}", [128, 128]) for i in range(2)]; ptw = PY[0]
                        psw = P1("psw", [128, 16]); ptx = P1("ptx", [128, 8, 16], BF16)
                        for gt in range(8):
                            DMA("sp", uT.t[:], UT[gt * 128:(gt + 1) * 128, :], "uT", W=[uT])
                            for v, X_ in enumerate((X1_, X2_)):
                                M(lambda e: e.transpose(out=ptw.t[:], in_=X_.t[:, gt * 8:(gt + 1) * 8, :], identity=idf), R=[X_, cst], W=[ptw])
                                for gg in range(8):
                                    V(lambda e: e.tensor_scalar_mul(out=LB[v].t[:, gg, :], in0=ptw.t[:], scalar1=cst.t[:, C_GM + gg:C_GM + gg + 1]), R=[ptw, cst], W=[LB[v]])
                            for v, C_ in enumerate((CA, CB)):
                                M(lambda e: e.transpose(out=ptw.t[:], in_=C_.t[:, gt, :], identity=idf), R=[C_, cst], W=[ptw])
                                for gg in range(8):
                                    V(lambda e: e.tensor_copy(out=LC[v].t[:, gg, 16 * gg:16 * gg + 16], in_=ptw.t[:, 16 * gg:16 * gg + 16]), R=[ptw], W=[LC[v]])
                            dcol = vecT.t[:, V_SD + gt:V_SD + gt + 1]
                            units = [(bi, hh) for bi in range(NPT) for hh in range(2)]

                            def st1a(u):
                                bi, hh = u; t0 = bi * 128; g0 = gt * 8 + hh * 4; i = hh
                                for q in range(4):
                                    M(lambda e: e.matmul(PS1[i].t[:, q, :], lhsT=LB[0].t[:, hh * 4 + q, :], rhs=uT.t[:, t0:t0 + 128], start=True, stop=True), R=[LB[0], uT], W=[PS1[i]], inc=False)
                                    M(lambda e: e.matmul(PS2[i].t[:, q, :], lhsT=LB[1].t[:, hh * 4 + q, :], rhs=uT.t[:, t0:t0 + 128], start=True, stop=True), R=[LB[1], uT], W=[PS2[i]], inc=(q == 3))
                                V(lambda e: e.tensor_tensor(out=m1[i].t[:], in0=PS1[i].t[:], in1=EC.t[:, g0:g0 + 4, :], op=ALU.mult), R=[PS1[i], EC], W=[m1[i]])
                                V(lambda e: e.tensor_tensor(out=m2[i].t[:], in0=PS2[i].t[:], in1=ES.t[:, g0:g0 + 4, :], op=ALU.mult), R=[PS2[i], ES], W=[m2[i]])
                                G(lambda e: e.tensor_tensor(out=win[i].t[:], in0=m1[i].t[:], in1=m2[i].t[:], op=ALU.add), R=[m1[i], m2[i]], W=[win[i]])
                                for q in range(4):
                                    V(lambda e: e.tensor_tensor_scan(out=wv[i].t[:, q, :], data0=mag[:, g0 + q:g0 + q + 1].to_broadcast([128, 128]), data1=win[i].t[:, q, :],
                                                                     initial=xl.t[:, g0 + q:g0 + q + 1], op0=ALU.mult, op1=ALU.add), R=[win[i], xlb[hh], sp_], W=[wv[i]])
                                G(lambda e: e.tensor_tensor(out=a1b[i].t[:], in0=wv[i].t[:], in1=EC.t[:, g0:g0 + 4, :], op=ALU.mult), R=[wv[i], EC], W=[a1b[i]])
                                V(lambda e: e.tensor_tensor(out=a2b[i].t[:], in0=wv[i].t[:], in1=ES.t[:, g0:g0 + 4, :], op=ALU.mult), R=[wv[i], ES], W=[a2b[i]])

                            def st1b(u):
                                bi, hh = u; g0 = gt * 8 + hh * 4; i = hh
                                M(lambda e: e.matmul(psw.t[:, 0:4], lhsT=cst.t[:, C_J:C_J + 128], rhs=wv[i].t[:, :, 127], start=True, stop=True), R=[cst, wv[i]], W=[psw])
                                V(lambda e: e.tensor_tensor(out=c1[i].t[:, 0:4], in0=wv[i].t[:, :, 127], in1=cT[:, g0:g0 + 4], op=ALU.mult), R=[wv[i], EC], W=[c1[i]])
                                V(lambda e: e.tensor_tensor(out=c2[i].t[:, 0:4], in0=psw.t[:, 0:4], in1=sT[:, g0:g0 + 4], op=ALU.mult), R=[psw, ES], W=[c2[i]])
                                V(lambda e: e.tensor_tensor(out=xl.t[:, g0:g0 + 4], in0=c1[i].t[:, 0:4], in1=c2[i].t[:, 0:4], op=ALU.add), R=[c1[i], c2[i]], W=[xlb[hh]])

                            def st2(u):
                                bi, hh = u; t0 = bi * 128; i = hh; py = PY[bi % 2]
                                for q in range(4):
                                    gg = hh * 4 + q
                                    M(lambda e: e.matmul(py.t[:], lhsT=LC[0].t[:, gg, :], rhs=a1b[i].t[:, q, :], start=(gg == 0), stop=False), R=[LC[0], a1b[i]], W=[py], inc=False)
                                    M(lambda e: e.matmul(py.t[:], lhsT=LC[1].t[:, gg, :], rhs=a2b[i].t[:, q, :], start=False, stop=(gg == 7)), R=[LC[1], a2b[i]], W=[py], inc=(q == 3))
                                if hh == 1:
                                    V(lambda e: e.scalar_tensor_tensor(out=ypre.t[:, t0:t0 + 128], in0=uT.t[:, t0:t0 + 128], scalar=dcol, in1=py.t[:], op0=ALU.mult, op1=ALU.add), R=[uT, vecT, py], W=[ypre])

                            st1a(units[0])
                            for ui_ in range(1, len(units)):
                                st1a(units[ui_]); st1b(units[ui_ - 1]); st2(units[ui_ - 1])
                            st1b(units[-1]); st2(units[-1])
                            DMA("sp", rp.t[:], s5row[l][:, :, gt * 512:(gt + 1) * 512].rearrange("a b x -> b a x"), "rp", W=[rp])
                            DMA("sp", x0r.t[:], s5r0[l][:, gt * 512:(gt + 1) * 512], "rp", W=[x0r]); DMA("sp", x0i.t[:], s5i0[l][:, gt * 512:(gt + 1) * 512], "rp", W=[x0i])
                            (dtr, mgr, thr, a1r, snr, csr, q1, q2) = [rw.t[:, i_, :] for i_ in range(8)]; abrr = thr; abir = dtr
                            RWs = dict(R=[rp, rw], W=[rw])
                            A(lambda e: e.activation(out=dtr, in_=rp.t[:, 2, :], func=AF.Exp), **RWs)
                            V(lambda e: e.tensor_tensor(out=q1, in0=rp.t[:, 0, :], in1=dtr, op=ALU.mult), **RWs)
                            A(lambda e: e.activation(out=mgr, in_=q1, func=AF.Exp), **RWs)
                            V(lambda e: e.tensor_tensor(out=thr, in0=rp.t[:, 1, :], in1=dtr, op=ALU.mult), **RWs)
                            sincos(thr, snr, csr, a1r, q1, RWs)
                            V(lambda e: e.tensor_tensor(out=abrr, in0=mgr, in1=csr, op=ALU.mult), **RWs)
                            V(lambda e: e.tensor_tensor(out=abir, in0=mgr, in1=snr, op=ALU.mult), **RWs)
                            v3 = lambda ap: ap.rearrange("b (g p) -> b g p", p=64)
                            V(lambda e: e.tensor_tensor(out=q1, in0=abrr, in1=x0r.t[:], op=ALU.mult), R=[rw, x0r], W=[rw])
                            V(lambda e: e.tensor_tensor(out=q2, in0=abir, in1=x0i.t[:], op=ALU.mult), R=[rw, x0i], W=[rw])
                            V(lambda e: e.tensor_tensor(out=ax0.t[:, :, 0:64], in0=v3(q1), in1=v3(q2), op=ALU.subtract), R=[rw], W=[ax0])
                            V(lambda e: e.tensor_tensor(out=q1, in0=abrr, in1=x0i.t[:], op=ALU.mult), R=[rw, x0i], W=[rw])
                            V(lambda e: e.tensor_tensor(out=q2, in0=abir, in1=x0r.t[:], op=ALU.mult), R=[rw, x0r], W=[rw])
                            V(lambda e: e.tensor_tensor(out=ax0.t[:, :, 64:128], in0=v3(q1), in1=v3(q2), op=ALU.add), R=[rw], W=[ax0])
                            for hh in range(2):
                                for q in range(4):
                                    M(lambda e: e.matmul(PS1[hh].t[:16, q, :], lhsT=uT.t[:, NP:NT], rhs=LB[0].t[:, hh * 4 + q, :], start=True, stop=True), R=[uT, LB[0]], W=[PS1[hh]], inc=(q == 3))
                                V(lambda e: e.tensor_tensor(out=xnew.t[:, hh * 4:hh * 4 + 4, :], in0=PS1[hh].t[:16], in1=ax0.t[:, hh * 4:hh * 4 + 4, :], op=ALU.add), R=[PS1[hh], ax0], W=[xnew])
                            DMA("sp", ss5[l][:, gt * 8:(gt + 1) * 8, :], xnew.t[:], "xnw", R=[xnew])
                            V(lambda e: e.tensor_copy(out=xnb.t[:], in_=xnew.t[:]), R=[xnew], W=[xnb])
                            for gg in range(8):
                                M(lambda e: e.transpose(out=ptx.t[:, gg, :], in_=xnb.t[:, gg, :], identity=idb.t[:16, :16]), R=[xnb, idb], W=[ptx], inc=(gg == 7))
                            A(lambda e: e.activation(out=xnT.t[:], in_=ptx.t[:], func=AF.Copy), R=[ptx], W=[xnT])
                            for gg in range(8):
                                M(lambda e: e.matmul(PY[1].t[:, 0:16], lhsT=LC[0].t[:, gg, :], rhs=xnT.t[:, gg, :], start=(gg == 0), stop=(gg == 7)), R=[LC[0], xnT], W=[PY[1]], inc=(gg == 7))
                            V(lambda e: e.scalar_tensor_tensor(out=ypre.t[:, NP:NT], in0=uT.t[:, NP:NT], scalar=dcol, in1=PY[1].t[:, 0:16], op0=ALU.mult, op1=ALU.add), R=[uT, vecT, PY[1]], W=[ypre])
                            for (b0, bn) in blocks:
                                yv = ypre.t[:, b0:b0 + bn]
                                A(lambda e: e.activation(out=gz.t[:, :bn], in_=yv, func=AF.Square), R=[ypre], W=[gz])
                                V(lambda e: e.tensor_scalar(out=gz.t[:, :bn], in0=gz.t[:, :bn], scalar1=0.044715, scalar2=1.0, op0=ALU.mult, op1=ALU.add), R=[gz], W=[gz])
                                V(lambda e: e.tensor_tensor(out=gz.t[:, :bn], in0=gz.t[:, :bn], in1=yv, op=ALU.mult), R=[gz, ypre], W=[gz])
                                A(lambda e: e.activation(out=gs.t[:, :bn], in_=gz.t[:, :bn], func=AF.Sigmoid, scale=1.5957691216057308), R=[gz], W=[gs])
                                V(lambda e: e.tensor_tensor(out=y5s.t[:, b0:b0 + bn], in0=gs.t[:, :bn], in1=yv, op=ALU.mult), R=[gs, ypre], W=[y5s])
                            DMA("sp", Y5A[gt * 128:(gt + 1) * 128, :], y5s.t[:], "y5s", R=[y5s])
                        M(lambda e: e.transpose(out=ptw.t[:64, :], in_=xl.t[:], identity=idf), R=xlb + [cst], W=[ptw])
                        A(lambda e: e.activation(out=gz.t[:64, 0:128], in_=ptw.t[:64, :], func=AF.Copy), R=[ptw], W=[gz])
                        DMA("sp", ps5[l], gz.t[:64, 0:128], "xnw", R=[gz])
                        S.barrier()
                    with ExitStack() as ps1:
                        T1 = mk(ps1, "s")
                        y5a = T1("y5a", [128, 8, NT], BF16)
                        DMA("sp", y5a.t[:], Y5A.rearrange("(k p) t -> p k t", p=128), "y5s", W=[y5a])
                        wgl = T1("wgl", [128, 8, 1024], BF16); sgz = T1("sgz", [128, 512]); stb = [T1(f"gstb{i}", [128, 512], BF16) for i in range(2)]
                        pz = [mk(ps1, "p")(f"pz{i}", [128, 512]) for i in range(2)]
                        DMA("pool", wgl.t[:], w_glu[l].rearrange("(k p) e -> p k e", p=128), "w0", W=[wgl])
                        ui = 0
                        for et in range(8):
                            for (b0, bn) in blocks:
                                i = ui % 2; ui += 1
                                for kc in range(8):
                                    M(lambda e: e.matmul(pz[i].t[:, :bn], lhsT=wgl.t[:, kc, et * 128:(et + 1) * 128], rhs=y5a.t[:, kc, b0:b0 + bn], start=(kc == 0), stop=(kc == 7)),
                                      R=[wgl, y5a], W=[pz[i]], inc=(kc == 7))
                                A(lambda e: e.activation(out=sgz.t[:, :bn], in_=pz[i].t[:, :bn], func=AF.Sigmoid, bias=vecT.t[:, V_BG + et:V_BG + et + 1]), R=[pz[i], vecT], W=[sgz])
                                V(lambda e: e.tensor_tensor(out=stb[i].t[:, :bn], in0=sgz.t[:, :bn], in1=y5a.t[:, et, b0:b0 + bn], op=ALU.mult), R=[sgz, y5a], W=[stb[i]])
                                DMA("sp", A5T[et * 128:(et + 1) * 128, b0:b0 + bn], stb[i].t[:, :bn], f"sb{i}", R=[stb[i]])
                        S.barrier()

                SC = 128.0 ** -0.5
                with ExitStack() as ps:
                    TT = mk(ps, "s"); PP = mk(ps, "p")
                    wgt = TT("wgt", [16, 512]); gnr = TT("gnr", [128, 256]); nb = TT("nb", [128, 16]); glr = TT("glr", [16, NT])
                    DMA("sp", wgt.t[:], w_gate[l], "g0", W=[wgt]); DMA("sp", gnr.t[:], gnorm[l], "g0", W=[gnr]); DMA("sp", glr.t[:], GLRT, "g0", W=[glr])
                    V(lambda e: e.tensor_scalar_mul(out=nb.t[:, 0:4], in0=vecT.t[:, V_GB:V_GB + 4], scalar1=-1.0), R=[vecT], W=[nb])
                    qT = TT("gqT", [128, NT]); kT = TT("gkT", [128, NT]); l1 = TT("gl1", [128, NT]); bc = TT("gbc", [128, NT]); enb = TT("genb", [128, NT])
                    eb = [TT(f"geb{h}", [128, NT]) for h in range(4)]
                    qp = [TT(f"gqp{h}", [128, NT], BF16) for h in range(4)]; kp = [TT(f"gkp{h}", [128, NT], BF16) for h in range(4)]
                    ksT = [TT(f"gksT{h}", [16, 128]) for h in range(4)]; qs = [TT(f"gqs{h}", [128, 16]) for h in range(4)]
                    St = [TT(f"gS{h}", [128, 256]) for h in range(4)]; Sb = [TT(f"gSb{h}", [128, 256], BF16) for h in range(4)]
                    kpt = [TT(f"gkpt{h}", [128, 128], BF16) for h in range(4)]; scm = [TT(f"gscm{h}", [128, 128], BF16) for h in range(4)]
                    ss = [TT(f"gss{h}", [128, 16]) for h in range(4)]; rs = [TT(f"grs{h}", [128, 16]) for h in range(4)]
                    on = [TT(f"gon{h}", [128, 256]) for h in range(4)]; og = [TT(f"gog{h}", [128, 256], BF16) for h in range(4)]; ost = [TT(f"gost{h}", [128, 2, 128], BF16) for h in range(4)]
                    vt = [TT(f"gv{i}", [128, 1024], BF16) for i in range(2)]; grt = [TT(f"ggr{i}", [128, 1024]) for i in range(2)]
                    Vbd = TT("gVbd", [16, 16, 256]); Qbd = TT("gQbd", [128, 16, 16]); S0 = [TT(f"gS0{i}", [128, 256]) for i in range(2)]; Sn = [TT(f"gSn{i}", [128, 256]) for i in range(2)]
                    pgt = PP("pgt", [128, 512]); ptw = PP("gptw", [16, 128])
                    pab = [PP(f"gpab{i}", [128, 2, 256]) for i in range(2)]; psc = [PP(f"gpsc{i}", [128, 256]) for i in range(2)]; pbf = [PP(f"gpbf{i}", [128, 3, 128], BF16) for i in range(2)]
                    pob = [pab[i].b for i in range(2)]; pkb = pob; ptkb = [pbf[i].b for i in range(2)]; ptob = ptkb

                    def gpost(n, t0, h, po_ap, po_buf, grt_ap, grt_tl, si):
                        V(lambda e: e.memset(ss[h].t[:], 0.0), W=[ss[h]])
                        A(lambda e: e.activation(out=on[h].t[:n, :], in_=po_ap, func=AF.Square, accum_out=ss[h].t[:n, 0:1]), R=[po_buf], W=[on[h], ss[h]])
                        A(lambda e: e.activation(out=rs[h].t[:n, 0:1], in_=ss[h].t[:n, 0:1], func=AF.Sqrt, scale=1.0 / 256, bias=EPS), R=[ss[h]], W=[rs[h]])
                        V(lambda e: e.reciprocal(out=rs[h].t[:n, 0:1], in_=rs[h].t[:n, 0:1]), R=[rs[h]], W=[rs[h]])
                        V(lambda e: e.scalar_tensor_tensor(out=on[h].t[:n, :], in0=po_ap, scalar=rs[h].t[:n, 0:1], in1=gnr.t[:n, :], op0=ALU.mult, op1=ALU.mult), R=[po_buf, rs[h], gnr], W=[on[h]])
                        G(lambda e: e.tensor_tensor(out=og[h].t[:n, :], in0=on[h].t[:n, :], in1=grt_ap, op=ALU.mult), R=[on[h], grt_tl], W=[og[h]])
                        for j in range(2):
                            M(lambda e: e.transpose(out=pbf[si].t[:, 1 + j, :n], in_=og[h].t[:n, j * 128:(j + 1) * 128], identity=idb.t[:n, :n]), R=[og[h], idb], W=[ptob[si]], inc=(j == 1))
                        A(lambda e: e.activation(out=ost[h].t[:, :, :n], in_=pbf[si].t[:, 1:3, :n], func=AF.Copy), R=[ptob[si]], W=[ost[h]])
                        for j in range(2):
                            DMA("sp", AGT[h * 256 + j * 128:h * 256 + (j + 1) * 128, t0:t0 + n], ost[h].t[:, j, :n], f"g1{h}", R=[ost[h]])

                    for h in range(4):
                        DMA("sp", qT.t[:], GQT[h * 128:(h + 1) * 128, :], "g2", W=[qT]); DMA("sp", kT.t[:], GKT[h * 128:(h + 1) * 128, :], "g2", W=[kT])
                        for (b0, bn) in blocks:
                            M(lambda e: e.matmul(pgt.t[:, :bn], lhsT=wgt.t[0:16, h * 128:(h + 1) * 128], rhs=glr.t[0:16, b0:b0 + bn], start=True, stop=True), R=[wgt, glr], W=[pgt])
                            A(lambda e: e.activation(out=l1.t[:, b0:b0 + bn], in_=pgt.t[:, :bn], func=AF.Exp, scale=-1.0, bias=nb.t[:, h:h + 1]), R=[pgt, nb], W=[l1])
                            A(lambda e: e.activation(out=l1.t[:, b0:b0 + bn], in_=l1.t[:, b0:b0 + bn], func=AF.Ln, bias=1.0), R=[l1], W=[l1])
                            if b0 < NP:
                                V(lambda e: e.tensor_tensor_scan(out=bc.t[:, b0:b0 + bn], data0=cst.t[:, C_D0:C_D0 + bn], data1=l1.t[:, b0:b0 + bn], initial=0.0, op0=ALU.mult, op1=ALU.add), R=[cst, l1], W=[bc])
                            else:
                                V(lambda e: e.tensor_copy(out=bc.t[:, b0:b0 + bn], in_=l1.t[:, b0:b0 + bn]), R=[l1], W=[bc])
                        A(lambda e: e.activation(out=eb[h].t[:], in_=bc.t[:], func=AF.Exp, scale=-1.0 / 16), R=[bc], W=[eb[h]])
                        A(lambda e: e.activation(out=enb.t[:], in_=bc.t[:], func=AF.Exp, scale=1.0 / 16), R=[bc], W=[enb])
                        V(lambda e: e.scalar_tensor_tensor(out=qp[h].t[:], in0=qT.t[:], scalar=SC, in1=eb[h].t[:], op0=ALU.mult, op1=ALU.mult), R=[qT, eb[h]], W=[qp[h]])
                        G(lambda e: e.tensor_tensor(out=kp[h].t[:], in0=kT.t[:], in1=enb.t[:], op=ALU.mult), R=[kT, enb], W=[kp[h]])
                        V(lambda e: e.memset(St[h].t[:], 0.0), W=[St[h]]); V(lambda e: e.memset(Sb[h].t[:], 0.0), W=[Sb[h]])
                        M(lambda e: e.transpose(out=ptw.t[:], in_=kT.t[:, NP:NT], identity=idf), R=[kT, cst], W=[ptw])
                        A(lambda e: e.activation(out=ksT[h].t[:], in_=ptw.t[:], func=AF.Copy), R=[ptw], W=[ksT[h]])
                        V(lambda e: e.tensor_scalar_mul(out=qs[h].t[:], in0=qT.t[:, NP:NT], scalar1=SC), R=[qT], W=[qs[h]])
                    for c in range(NPT):
                        t0 = c * 128; vi = c % 2
                        DMA("sp", vt[vi].t[:], GV[t0:t0 + 128, :], f"g3{vi}", W=[vt[vi]]); DMA("sp", grt[vi].t[:], GR[t0:t0 + 128, :], f"g3{vi}", W=[grt[vi]])
                        for pr in range(2):
                            hs_ = (2 * pr, 2 * pr + 1)
                            for h in hs_:
                                si = h % 2
                                M(lambda e: e.transpose(out=pbf[si].t[:, 0, :], in_=kp[h].t[:, t0:t0 + 128], identity=idb.t[:]), R=[kp[h], idb], W=[ptkb[si]])
                                M(lambda e: e.matmul(psc[si].t[:, 0:128], lhsT=kp[h].t[:, t0:t0 + 128], rhs=qp[h].t[:, t0:t0 + 128], start=True, stop=True), R=[kp[h], qp[h]], W=[psc[si]])
                            for h in hs_:
                                si = h % 2
                                A(lambda e: e.activation(out=kpt[h].t[:], in_=pbf[si].t[:, 0, :], func=AF.Copy), R=[ptkb[si]], W=[kpt[h]])
                                V(lambda e: e.tensor_tensor(out=scm[h].t[:], in0=psc[si].t[:, 0:128], in1=cst.t[:, C_TRI:C_TRI + 128], op=ALU.mult), R=[psc[si], cst], W=[scm[h]])
                            for h in hs_:
                                si = h % 2; vh = vt[vi].t[:, h * 256:(h + 1) * 256]
                                M(lambda e: e.matmul(pab[si].t[:, 0, :], lhsT=scm[h].t[:], rhs=vh, start=True, stop=False), R=[scm[h], vt[vi]], W=[pob[si]], inc=False)
                                M(lambda e: e.matmul(pab[si].t[:, 0, :], lhsT=qp[h].t[:, t0:t0 + 128], rhs=Sb[h].t[:], start=False, stop=True), R=[qp[h], Sb[h]], W=[pob[si]])
                                M(lambda e: e.matmul(pab[si].t[:, 1, :], lhsT=kpt[h].t[:], rhs=vh, start=True, stop=True), R=[kpt[h], vt[vi]], W=[pkb[si]])
                            for h in hs_:
                                si = h % 2
                                ecol = eb[h].t[:, t0 + 127:t0 + 128]
                                V(lambda e: e.tensor_scalar_mul(out=St[h].t[:], in0=St[h].t[:], scalar1=ecol), R=[St[h], eb[h]], W=[St[h]])
                                V(lambda e: e.scalar_tensor_tensor(out=St[h].t[:], in0=pab[si].t[:, 1, :], scalar=ecol, in1=St[h].t[:], op0=ALU.mult, op1=ALU.add), R=[pkb[si], eb[h], St[h]], W=[St[h]])
                                A(lambda e: e.activation(out=Sb[h].t[:], in_=St[h].t[:], func=AF.Copy), R=[St[h]], W=[Sb[h]])
                            for h in hs_:
                                si = h % 2
                                gpost(128, t0, h, pab[si].t[:, 0, :], pob[si], grt[vi].t[:, h * 256:(h + 1) * 256], grt[vi], si)
                    for h in range(4):
                        DMA("sp", pgla[l, h], St[h].t[:], "g4", R=[St[h]])
                    DMA("sp", vt[0].t[:NS, :], GV[NP:NT, :], "g30", W=[vt[0]]); DMA("sp", grt[0].t[:NS, :], GR[NP:NT, :], "g30", W=[grt[0]])
                    for h in range(4):
                        si = h % 2
                        V(lambda e: e.tensor_tensor(out=Vbd.t[:], in0=vt[0].t[:NS, h * 256:(h + 1) * 256].unsqueeze(1).to_broadcast([NS, NS, 256]),
                                                    in1=cst.t[:NS, C_ID:C_ID + NS].unsqueeze(2).to_broadcast([NS, NS, 256]), op=ALU.mult), R=[vt[0], cst], W=[Vbd])
                        V(lambda e: e.tensor_tensor(out=Qbd.t[:], in0=qs[h].t[:].unsqueeze(1).to_broadcast([128, NS, NS]),
                                                    in1=cst.t[:, C_E16:C_E16 + 256].rearrange("p (a b) -> p a b", a=NS), op=ALU.mult), R=[qs[h], cst], W=[Qbd])
                        for b_ in range(NS):
                            bi_ = b_ % 2
                            DMA("sp", S0[bi_].t[:], gla0[l, b_, h], f"g5{bi_}", W=[S0[bi_]])
                            M(lambda e: e.matmul(psc[bi_].t[:], lhsT=ksT[h].t[:], rhs=Vbd.t[:, b_, :], start=True, stop=True), R=[ksT[h], Vbd], W=[psc[bi_]])
                            V(lambda e: e.scalar_tensor_tensor(out=Sn[bi_].t[:], in0=S0[bi_].t[:], scalar=eb[h].t[:, NP + b_:NP + b_ + 1], in1=psc[bi_].t[:], op0=ALU.mult, op1=ALU.add),
                              R=[S0[bi_], eb[h], psc[bi_]], W=[Sn[bi_]])
                            DMA("sp", sgla[l, b_, h], Sn[bi_].t[:], f"g6{bi_}", R=[Sn[bi_]])
                            M(lambda e: e.matmul(pgt.t[:NS, 0:256], lhsT=Qbd.t[:, b_, :], rhs=Sn[bi_].t[:], start=(b_ == 0), stop=(b_ == NS - 1)), R=[Qbd, Sn[bi_]], W=[pgt])
                        gpost(NS, NP, h, pgt.t[:NS, 0:256], pgt, grt[0].t[:NS, h * 256:(h + 1) * 256], grt[0], si)
                    S.barrier()

                with ExitStack() as ps:
                    TT = mk(ps, "s"); PP = mk(ps, "p")
                    rq = TT("rq", [128, 1024]); rk = TT("rk", [128, 1024]); rv = TT("rv", [128, 1024], BF16); rg = TT("rg", [128, 1024]); rt = TT("rt", [128, 128])
                    qr = TT("qr", [128, 1024]); kr = TT("kr", [128, 1024]); ta = TT("rta", [128, 8, 64]); tb = TT("rtb", [128, 8, 64])
                    qb = TT("rqb", [128, 1024], BF16); kb = TT("rkb", [128, 1024], BF16); k2 = TT("rk2", [128, 1024], BF16)
                    qTt = TT("rqT", [128, 8, 128], BF16); kTt = TT("rkT", [128, 8, 128], BF16)
                    S4 = [TT(f"rS{g_}", [128, 4, 128]) for g_ in range(2)]; Sb4 = [TT(f"rSb{g_}", [128, 4, 128], BF16) for g_ in range(2)]
                    scm4 = TT("rscm", [128, 4, 128], BF16); pos4 = TT("rpos", [128, 4, 128]); o_ = TT("ro", [128, 8, 128]); sq = TT("rsq", [128, 8, 128])
                    st1 = TT("rst1", [128, 16]); st2 = TT("rst2", [128, 16]); og = TT("rog", [128, 1024], BF16); ost = TT("rost", [128, 8, 128], BF16)
                    Vbd = TT("rVbd", [16, 16, 128]); qs = TT("rqs", [128, 16]); Qbd = TT("rQbd", [128, 16, 16]); S0 = TT("rS0", [128, 128]); Sn = TT("rSn", [128, 128])
                    ptq = [PP(f"rptq{i}", [128, 4, 128], BF16) for i in range(2)]; psc = PP("rpsc", [128, 4, 128]); po = PP("rpo", [128, 4, 128]); pi_ = PP("rpi", [128, 4, 128]); pkv = PP("rpkv", [128, 4, 128])
                    ptw = PP("rptw", [128, 16])
                    for g_ in range(2):
                        V(lambda e: e.memset(S4[g_].t[:], 0.0), W=[S4[g_]]); V(lambda e: e.memset(Sb4[g_].t[:], 0.0), W=[Sb4[g_]])

                    def rotary(src, dst, n, scale):
                        s4 = src.t[:n, :].rearrange("p (h i two) -> p h i two", h=8, two=2); d4 = dst.t[:n, :].rearrange("p (h i two) -> p h i two", h=8, two=2)
                        cb_ = rt.t[:n, 0:64].unsqueeze(1).to_broadcast([n, 8, 64]); sb_ = rt.t[:n, 64:128].unsqueeze(1).to_broadcast([n, 8, 64])
                        V(lambda e: e.tensor_tensor(out=ta.t[:n], in0=s4[:, :, :, 0], in1=cb_, op=ALU.mult), R=[src, rt], W=[ta])
                        V(lambda e: e.tensor_tensor(out=tb.t[:n], in0=s4[:, :, :, 1], in1=sb_, op=ALU.mult), R=[src, rt], W=[tb])
                        V(lambda e: e.tensor_tensor(out=d4[:, :, :, 0], in0=ta.t[:n], in1=tb.t[:n], op=ALU.subtract), R=[ta, tb], W=[dst])
                        V(lambda e: e.tensor_tensor(out=ta.t[:n], in0=s4[:, :, :, 1], in1=cb_, op=ALU.mult), R=[src, rt], W=[ta])
                        V(lambda e: e.tensor_tensor(out=tb.t[:n], in0=s4[:, :, :, 0], in1=sb_, op=ALU.mult), R=[src, rt], W=[tb])
                        V(lambda e: e.tensor_tensor(out=d4[:, :, :, 1], in0=ta.t[:n], in1=tb.t[:n], op=ALU.add), R=[ta, tb], W=[dst])
                        if scale != 1.0:
                            V(lambda e: e.tensor_scalar_mul(out=dst.t[:n, :], in0=dst.t[:n, :], scalar1=scale), R=[dst], W=[dst])

                    for ti, (r0, n) in enumerate(tiles):
                        DMA("sp", rq.t[:n, :], RQ[r0:r0 + n, :], "r0", W=[rq]); DMA("sp", rk.t[:n, :], RK[r0:r0 + n, :], "r0", W=[rk])
                        DMA("sp", rv.t[:n, :], RV[r0:r0 + n, :], "r0", W=[rv]); DMA("sp", rg.t[:n, :], RG[r0:r0 + n, :], "r0", W=[rg]); DMA("sp", rt.t[:n, :], rot[r0:r0 + n, :], "r0", W=[rt])
                        rotary(rq, qr, n, 1.0); rotary(rk, kr, n, SC)
                        if ti < NPT:
                            G(lambda e: e.tensor_copy(out=qb.t[:], in_=qr.t[:]), R=[qr], W=[qb]); G(lambda e: e.tensor_copy(out=kb.t[:], in_=kr.t[:]), R=[kr], W=[kb])
                            V(lambda e: e.tensor_tensor(out=k2.t[:].rearrange("p (h k) -> p h k", h=8), in0=kr.t[:].rearrange("p (h k) -> p h k", h=8),
                                                        in1=cst.t[:, C_KDEC:C_KDEC + 8].unsqueeze(2).to_broadcast([128, 8, 128]), op=ALU.mult), R=[kr, cst], W=[k2])
                            for (srcb, dstT) in ((qb, qTt), (kb, kTt)):
                                for hq in range(2):
                                    for j in range(4):
                                        hh_ = hq * 4 + j
                                        M(lambda e: e.transpose(out=ptq[hq].t[:, j, :], in_=srcb.t[:, hh_ * 128:(hh_ + 1) * 128], identity=idb.t[:]), R=[srcb, idb], W=[ptq[hq]], inc=(j == 3))
                                    A(lambda e: e.activation(out=dstT.t[:, hq * 4:hq * 4 + 4, :], in_=ptq[hq].t[:], func=AF.Copy), R=[ptq[hq]], W=[dstT])
                            for g_ in range(2):
                                hsl = [(j, g_ * 4 + j, slice((g_ * 4 + j) * 128, (g_ * 4 + j + 1) * 128)) for j in range(4)]
                                for (j, h, hs) in hsl:
                                    M(lambda e: e.matmul(psc.t[:, j, :], lhsT=kTt.t[:, h, :], rhs=qTt.t[:, h, :], start=True, stop=True), R=[kTt, qTt], W=[psc], inc=(j == 3))
                                V(lambda e: e.tensor_tensor(out=scm4.t[:], in0=psc.t[:], in1=cst.t[:, C_DH + g_ * 512:C_DH + (g_ + 1) * 512].rearrange("p (a b) -> p a b", a=4), op=ALU.mult), R=[psc, cst], W=[scm4])
                                for (j, h, hs) in hsl:
                                    M(lambda e: e.matmul(po.t[:, j, :], lhsT=scm4.t[:, j, :], rhs=rv.t[:, hs], start=True, stop=True), R=[scm4, rv], W=[po], inc=(j == 3))
                                for (j, h, hs) in hsl:
                                    M(lambda e: e.matmul(pi_.t[:, j, :], lhsT=qTt.t[:, h, :], rhs=Sb4[g_].t[:, j, :], start=True, stop=True), R=[qTt, Sb4[g_]], W=[pi_], inc=(j == 3))
                                for (j, h, hs) in hsl:
                                    M(lambda e: e.matmul(pkv.t[:, j, :], lhsT=k2.t[:, hs], rhs=rv.t[:, hs], start=True, stop=True), R=[k2, rv], W=[pkv], inc=(j == 3))
                                V(lambda e: e.tensor_tensor(out=S4[g_].t[:], in0=S4[g_].t[:], in1=cst.t[:, C_G128 + g_ * 4:C_G128 + g_ * 4 + 4].unsqueeze(2).to_broadcast([128, 4, 128]), op=ALU.mult), R=[S4[g_], cst], W=[S4[g_]])
                                V(lambda e: e.tensor_tensor(out=S4[g_].t[:], in0=S4[g_].t[:], in1=pkv.t[:], op=ALU.add), R=[S4[g_], pkv], W=[S4[g_]])
                                A(lambda e: e.activation(out=Sb4[g_].t[:], in_=S4[g_].t[:], func=AF.Copy), R=[S4[g_]], W=[Sb4[g_]])
                                A(lambda e: e.activation(out=pos4.t[:], in_=po.t[:], func=AF.Copy), R=[po], W=[pos4])
                                V(lambda e: e.tensor_tensor(out=o_.t[:, g_ * 4:g_ * 4 + 4, :], in0=pi_.t[:], in1=cst.t[:, C_GPOW + g_ * 4:C_GPOW + g_ * 4 + 4].unsqueeze(2).to_broadcast([128, 4, 128]), op=ALU.mult), R=[pi_, cst], W=[o_])
                                V(lambda e: e.tensor_tensor(out=o_.t[:, g_ * 4:g_ * 4 + 4, :], in0=o_.t[:, g_ * 4:g_ * 4 + 4, :], in1=pos4.t[:], op=ALU.add), R=[o_, pos4], W=[o_])
                        else:
                            for h in range(8):
                                hs = slice(h * 128, (h + 1) * 128)
                                M(lambda e: e.transpose(out=ptw.t[:], in_=qr.t[:NS, hs], identity=idf[:NS, :NS]), R=[qr, cst], W=[ptw])
                                V(lambda e: e.tensor_copy(out=qs.t[:], in_=ptw.t[:]), R=[ptw], W=[qs])
                                V(lambda e: e.tensor_tensor(out=Qbd.t[:], in0=qs.t[:].unsqueeze(1).to_broadcast([128, NS, NS]),
                                                            in1=cst.t[:, C_E16:C_E16 + 256].rearrange("p (a b) -> p a b", a=NS), op=ALU.mult), R=[qs, cst], W=[Qbd])
                                V(lambda e: e.tensor_tensor(out=Vbd.t[:], in0=rv.t[:NS, hs].unsqueeze(1).to_broadcast([NS, NS, 128]),
                                                            in1=cst.t[:NS, C_ID:C_ID + NS].unsqueeze(2).to_broadcast([NS, NS, 128]), op=ALU.mult), R=[rv, cst], W=[Vbd])
                                for b in range(NS):
                                    DMA("sp", S0.t[:], ret0[l, b, h], "r1", W=[S0])
                                    M(lambda e: e.matmul(pkv.t[:, 0, :], lhsT=kr.t[:NS, hs], rhs=Vbd.t[:, b, :], start=True, stop=True), R=[kr, Vbd], W=[pkv])
                                    V(lambda e: e.scalar_tensor_tensor(out=Sn.t[:], in0=S0.t[:], scalar=cst.t[:, C_G1 + h:C_G1 + h + 1], in1=pkv.t[:, 0, :], op0=ALU.mult, op1=ALU.add), R=[S0, cst, pkv], W=[Sn])
                                    DMA("sp", sret[l, b, h], Sn.t[:], "r2", R=[Sn])
                                    M(lambda e: e.matmul(po.t[:NS, 0, :], lhsT=Qbd.t[:, b, :], rhs=Sn.t[:], start=(b == 0), stop=(b == NS - 1)), R=[Qbd, Sn], W=[po])
                                V(lambda e: e.tensor_copy(out=o_.t[:NS, h, :], in_=po.t[:NS, 0, :]), R=[po], W=[o_])
                        V(lambda e: e.reduce_sum(out=st1.t[:n, 0:8], in_=o_.t[:n], axis=AX.X), R=[o_], W=[st1])
                        G(lambda e: e.tensor_tensor(out=sq.t[:n], in0=o_.t[:n], in1=o_.t[:n], op=ALU.mult), R=[o_], W=[sq])
                        V(lambda e: e.reduce_sum(out=st2.t[:n, 0:8], in_=sq.t[:n], axis=AX.X), R=[sq], W=[st2])
                        V(lambda e: e.tensor_scalar_mul(out=st1.t[:n, 0:8], in0=st1.t[:n, 0:8], scalar1=1.0 / 128), R=[st1], W=[st1])
                        V(lambda e: e.tensor_tensor(out=st1.t[:n, 8:16], in0=st1.t[:n, 0:8], in1=st1.t[:n, 0:8], op=ALU.mult), R=[st1], W=[st1])
                        V(lambda e: e.scalar_tensor_tensor(out=st2.t[:n, 0:8], in0=st2.t[:n, 0:8], scalar=1.0 / 128, in1=st1.t[:n, 8:16], op0=ALU.mult, op1=ALU.subtract), R=[st2, st1], W=[st2])
                        A(lambda e: e.activation(out=st2.t[:n, 0:8], in_=st2.t[:n, 0:8], func=AF.Sqrt, bias=EPS), R=[st2], W=[st2])
                        V(lambda e: e.reciprocal(out=st2.t[:n, 0:8], in_=st2.t[:n, 0:8]), R=[st2], W=[st2])
                        V(lambda e: e.tensor_tensor(out=o_.t[:n], in0=o_.t[:n], in1=st1.t[:n, 0:8].unsqueeze(2).to_broadcast([n, 8, 128]), op=ALU.subtract), R=[o_, st1], W=[o_])
                        V(lambda e: e.tensor_tensor(out=o_.t[:n], in0=o_.t[:n], in1=st2.t[:n, 0:8].unsqueeze(2).to_broadcast([n, 8, 128]), op=ALU.mult), R=[o_, st2], W=[o_])
                        G(lambda e: e.tensor_tensor(out=og.t[:n, :], in0=o_.t[:n].rearrange("p h k -> p (h k)"), in1=rg.t[:n, :], op=ALU.mult), R=[o_, rg], W=[og])
                        for hq in range(2):
                            for j in range(4):
                                hh_ = hq * 4 + j
                                M(lambda e: e.transpose(out=ptq[hq].t[:, j, :n], in_=og.t[:n, hh_ * 128:(hh_ + 1) * 128], identity=idb.t[:n, :n]), R=[og, idb], W=[ptq[hq]], inc=(j == 3))
                            A(lambda e: e.activation(out=ost.t[:, hq * 4:hq * 4 + 4, :n], in_=ptq[hq].t[:, :, :n], func=AF.Copy), R=[ptq[hq]], W=[ost])
                        DMA("sp", ART[:, r0:r0 + n].rearrange("(k p) t -> p k t", p=128), ost.t[:, :, :n], "r3", R=[ost])
                    for h in range(8):
                        DMA("sp", pret[l, h], S4[h // 4].t[:, h % 4, :], "r4", R=[S4[h // 4]])
                    S.barrier()

                with ExitStack() as ps:
                    TT = mk(ps, "s"); PP = mk(ps, "p")
                    mT = TT("mT", [128, 16, NT], BF16)
                    mTb = [Buf(f"mT{d_}") for d_ in range(16)]
                    with ExitStack() as ps1:
                        T1 = mk(ps1, "s"); P1 = mk(ps1, "p")
                        at = [T1(f"at{b}", [128, 8, NT], BF16) for b in range(3)]
                        for b, src in enumerate((A5T, AGT, ART)):
                            DMA("sp", at[b].t[:], src.rearrange("(k p) t -> p k t", p=128), f"at{b}", W=[at[b]])
                        wb = [[T1(f"wb{i}_{b}", [128, 8, 128], BF16) for b in range(3)] for i in range(2)]
                        gt = [T1(f"gt{i}", [128, 3, 512], BF16) for i in range(2)]
                        pmb = [[P1(f"pb{i}_{b}", [128, 512]) for b in range(3)] for i in range(2)]
                        t1 = [T1(f"t1_{i}", [128, 512]) for i in range(2)]; t2 = [T1(f"t2_{i}", [128, 512]) for i in range(2)]
                        ui = 0
                        for dt_ in range(16):
                            w = wb[dt_ % 2]
                            for b, wsrc in enumerate((w_s5o, w_glao, w_reto)):
                                DMA("pool", w[b].t[:], wsrc[l][:, dt_ * 128:(dt_ + 1) * 128].rearrange("(k p) e -> p k e", p=128), f"wb{dt_ % 2}", W=[w[b]])
                            for (b0, bn) in blocks:
                                i = ui % 2; ui += 1
                                for b in range(3):
                                    DMA("sp", gt[i].t[:, b, :bn], MGT[b * 2048 + dt_ * 128:b * 2048 + (dt_ + 1) * 128, b0:b0 + bn], f"gt{i}", W=[gt[i]])
                                    for kc in range(8):
                                        M(lambda e: e.matmul(pmb[i][b].t[:, :bn], lhsT=w[b].t[:, kc, :], rhs=at[b].t[:, kc, b0:b0 + bn], start=(kc == 0), stop=(kc == 7)),
                                          R=[w[b], at[b]], W=[pmb[i][b]], inc=(kc == 7))
                                V(lambda e: e.tensor_tensor(out=t1[i].t[:, :bn], in0=pmb[i][0].t[:, :bn], in1=gt[i].t[:, 0, :bn], op=ALU.mult), R=[pmb[i][0], gt[i]], W=[t1[i]])
                                V(lambda e: e.tensor_tensor(out=t2[i].t[:, :bn], in0=pmb[i][1].t[:, :bn], in1=gt[i].t[:, 1, :bn], op=ALU.mult), R=[pmb[i][1], gt[i]], W=[t2[i]])
                                G(lambda e: e.tensor_tensor(out=t1[i].t[:, :bn], in0=t1[i].t[:, :bn], in1=t2[i].t[:, :bn], op=ALU.add), R=[t1[i], t2[i]], W=[t1[i]])
                                V(lambda e: e.tensor_tensor(out=t2[i].t[:, :bn], in0=pmb[i][2].t[:, :bn], in1=gt[i].t[:, 2, :bn], op=ALU.mult), R=[pmb[i][2], gt[i]], W=[t2[i]])
                                V(lambda e: e.tensor_tensor(out=mT.t[:, dt_, b0:b0 + bn], in0=t1[i].t[:, :bn], in1=t2[i].t[:, :bn], op=ALU.add), R=[t1[i], t2[i]], W=[mTb[dt_]])
                        S.barrier()
                    wsl = [TT(f"w{i}", [128, 16, 512], BF16) for i in range(2)]
                    pm = [PP(f"pm{i}", [128, 512]) for i in range(4)]
                    xr = [TT(f"xr{i}", [128, 512]) for i in range(3)]
                    stf = [TT(f"stf{i}", [128, 512]) for i in range(3)]
                    ui = 0
                    for eb in range(4):
                        w = wsl[eb % 2]
                        DMA("pool", w.t[:], w_mix[l][:, eb * 512:(eb + 1) * 512].rearrange("(k p) e -> p k e", p=128), f"w{eb % 2}", W=[w])
                        for ti, (r0, n) in enumerate(tiles):
                            p = pm[ui % 4]; x_ = xr[ui % 3]; st = stf[ui % 3]; ci = ui % 3; ui += 1
                            DMA("sp", x_.t[:n, :], xrows(xsrc, ti)[:, eb * 512:(eb + 1) * 512], f"xr{ci}", W=[x_])
                            for kc in range(16):
                                M(lambda e: e.matmul(p.t[:n, :], lhsT=mT.t[:, kc, r0:r0 + n], rhs=w.t[:, kc, :], start=(kc == 0), stop=(kc == 15)),
                                  R=mTb + [w], W=[p], inc=(kc == 15))
                            V(lambda e: e.tensor_tensor(out=st.t[:n, :], in0=p.t[:n, :], in1=x_.t[:n, :], op=ALU.add), R=[p, x_], W=[st])
                            DMA("sp", XMID[r0:r0 + n, eb * 512:(eb + 1) * 512], st.t[:n, :], f"sf{ci}", R=[st])
                    S.barrier()

                with ExitStack() as ps:
                    TT = mk(ps, "s"); PP = mk(ps, "p")
                    hT = TT("hT2", [128, 16, NT], BF16)
                    hTb = [Buf(f"hT2{ti}") for ti in range(len(tiles))]
                    with ExitStack() as ps1:
                        norm_T(mk(ps1, "s"), mk(ps1, "p"), XMID, V_NF, hT, hTb, vecT, "n2")
                        S.barrier()
                    wg = [TT(f"wg{i}", [128, 16, 512], BF16) for i in range(2)]; wu = [TT(f"wu{i}", [128, 16, 512], BF16) for i in range(2)]
                    pg = [PP(f"pg{i}", [128, 512]) for i in range(4)]; pu = [PP(f"pu{i}", [128, 512]) for i in range(4)]
                    sg = [TT(f"sg{i}", [128, 512]) for i in range(4)]; stb = [TT(f"stb{i}", [128, 512], BF16) for i in range(6)]
                    ui = 0
                    for fb in range(DFF // 512):
                        i2 = fb % 2
                        DMA("pool", wg[i2].t[:], w_fg[l][:, fb * 512:(fb + 1) * 512].rearrange("(k p) e -> p k e", p=128), f"wg{i2}", W=[wg[i2]])
                        DMA("pool", wu[i2].t[:], w_fu[l][:, fb * 512:(fb + 1) * 512].rearrange("(k p) e -> p k e", p=128), f"wu{i2}", W=[wu[i2]])
                        for j in range(4):
                            ft = fb * 4 + j
                            for (b0, bn) in blocks:
                                i = ui % 4; si = ui % 6; ui += 1
                                bt = [hTb[t_] for t_ in blk_tiles(b0, bn)]
                                for kc in range(16):
                                    M(lambda e: e.matmul(pg[i].t[:, :bn], lhsT=wg[i2].t[:, kc, j * 128:(j + 1) * 128], rhs=hT.t[:, kc, b0:b0 + bn], start=(kc == 0), stop=(kc == 15)),
                                      R=bt + [wg[i2]], W=[pg[i]], inc=(kc == 15))
                                for kc in range(16):
                                    M(lambda e: e.matmul(pu[i].t[:, :bn], lhsT=wu[i2].t[:, kc, j * 128:(j + 1) * 128], rhs=hT.t[:, kc, b0:b0 + bn], start=(kc == 0), stop=(kc == 15)),
                                      R=bt + [wu[i2]], W=[pu[i]], inc=(kc == 15))
                                A(lambda e: e.activation(out=sg[i].t[:, :bn], in_=pg[i].t[:, :bn], func=AF.Silu), R=[pg[i]], W=[sg[i]])
                                V(lambda e: e.tensor_tensor(out=stb[si].t[:, :bn], in0=pu[i].t[:, :bn], in1=sg[i].t[:, :bn], op=ALU.mult), R=[pu[i], sg[i]], W=[stb[si]])
                                bts = blk_tiles(b0, bn)
                                if bn % 128 == 0:
                                    DMA("sp" if ui % 2 else "act", FFTL.rearrange("n p k t -> p n k t")[:, bts[0]:bts[0] + len(bts), ft, :],
                                        stb[si].t[:, :bn].rearrange("p (n t) -> p n t", t=128), f"sb{si}", R=[stb[si]])
                                else:
                                    DMA("sp", FFTL[bts[0], :, ft, :bn], stb[si].t[:, :bn], f"sb{si}", R=[stb[si]])
                    S.barrier()
                with ExitStack() as ps:
                    TT = mk(ps, "s"); PP = mk(ps, "p")
                    NF = DFF // 128
                    wd = TT("wd", [128, NF, 1024], BF16)
                    ff = [TT(f"ff{i}", [128, NF, 128], BF16) for i in range(3)]
                    pm = [PP(f"pm{i}", [128, 1024]) for i in range(3)]
                    xr = [TT(f"xr{i}", [128, 1024]) for i in range(2)]; stf = [TT(f"stf{i}", [128, 1024]) for i in range(2)]
                    ui = 0
                    for db in range(2):
                        for hf in range(2):
                            DMA("pool", wd.t[:, :, hf * 512:(hf + 1) * 512], w_fd[l][:, db * 1024 + hf * 512:db * 1024 + (hf + 1) * 512].rearrange("(k p) e -> p k e", p=128), "w0", W=[wd])
                        for ti, (r0, n) in enumerate(tiles):
                            p = pm[ui % 3]; x_ = xr[ui % 2]; st = stf[ui % 2]; ci = ui % 2; f_ = ff[ui % 3]; fi_ = ui % 3; ui += 1
                            DMA("sp", f_.t[:, :, :n], FFTL[ti][:, :, :n], f"ff{fi_}", W=[f_])
                            DMA("act", x_.t[:n, :], XMID[r0:r0 + n, db * 1024:(db + 1) * 1024], f"xr{ci}", W=[x_])
                            for hf in range(2):
                                for kc in range(NF):
                                    M(lambda e: e.matmul(p.t[:n, hf * 512:(hf + 1) * 512], lhsT=f_.t[:, kc, :n], rhs=wd.t[:, kc, hf * 512:(hf + 1) * 512], start=(kc == 0), stop=(kc == NF - 1)),
                                      R=[f_, wd], W=[p], inc=(kc == NF - 1 and hf == 1))
                            V(lambda e: e.tensor_tensor(out=st.t[:n, :], in0=p.t[:n, :], in1=x_.t[:n, :], op=ALU.add), R=[p, x_], W=[st])
                            DMA("sp", X1[r0:r0 + n, db * 1024:(db + 1) * 1024], st.t[:n, :], f"sf{ci}", R=[st])
                    S.barrier()
        with ExitStack() as ps:
            TT = mk(ps, "s")
            nf = TT("nf", [128, D]); junk = TT("fjunk", [128, D], BF16)
            DMA("sp", nf.t[:], nfin, "nf", W=[nf])
            xt = [TT(f"fxt{i}", [128, D]) for i in range(2)]; yo = [TT(f"fyo{i}", [128, D]) for i in range(2)]
            ss = [TT(f"fss{i}", [128, 16]) for i in range(2)]; rs = [TT(f"frs{i}", [128, 16]) for i in range(2)]
            for ti, (r0, n) in enumerate(tiles):
                s_ = ti % 2
                DMA("sp", xt[s_].t[:n, :], X1[r0:r0 + n, :], f"xr{s_}", W=[xt[s_]])
                V(lambda e: e.memset(ss[s_].t[:], 0.0), W=[ss[s_]])
                A(lambda e: e.activation(out=junk.t[:n, :], in_=xt[s_].t[:n, :], func=AF.Square, accum_out=ss[s_].t[:n, 0:1]), R=[xt[s_]], W=[junk, ss[s_]])
                A(lambda e: e.activation(out=rs[s_].t[:n, 0:1], in_=ss[s_].t[:n, 0:1], func=AF.Sqrt, scale=1.0 / D, bias=EPS), R=[ss[s_]], W=[rs[s_]])
                V(lambda e: e.reciprocal(out=rs[s_].t[:n, 0:1], in_=rs[s_].t[:n, 0:1]), R=[rs[s_]], W=[rs[s_]])
                V(lambda e: e.scalar_tensor_tensor(out=yo[s_].t[:n, :], in0=xt[s_].t[:n, :], scalar=rs[s_].t[:n, 0:1], in1=nf.t[:n, :], op0=ALU.mult, op1=ALU.mult),
                  R=[xt[s_], rs[s_], nf], W=[yo[s_]])
                DMA("sp", (yp[r0:r0 + n, :] if ti < NPT else ys[:, :]), yo[s_].t[:n, :], f"sf{s_}", R=[yo[s_]])
            S.barrier()
        S.barrier()
    return nc


def make_consts():
    c = np.zeros((128, C_N), np.float32)
    c[:, C_ID:C_ID + 128] = np.eye(128)
    for p in range(64):
        c[64 + p, C_J + p] = -1.0
        c[p, C_J + 64 + p] = 1.0
    s = np.arange(128)[:, None]; t = np.arange(128)[None, :]
    c[:, C_TRI:C_TRI + 128] = (t >= s)
    for gg in range(8):
        c[gg * 16:(gg + 1) * 16, C_GM + gg] = 1.0
    c[:64, C_SGN] = -1.0; c[64:, C_SGN] = 1.0
    for b in range(16):
        c[:, C_E16 + b * 16 + b] = 1.0
    c[:, C_D0:C_D0 + 512] = 1.0
    c[:, C_D0:C_D0 + 512:128] = 0.0
    lg = np.log1p(-np.power(2.0, -5.0 - np.arange(8, dtype=np.float64)))
    for h in range(8):
        c[:, C_DH + h * 128:C_DH + (h + 1) * 128] = np.where(t >= s, np.exp(lg[h] * (t - s)), 0.0)
        c[:, C_GPOW + h] = np.exp(lg[h] * (np.arange(128) + 1))
        c[:, C_KDEC + h] = np.exp(lg[h] * (127 - np.arange(128)))
        c[:, C_G128 + h] = np.exp(lg[h] * 128)
        c[:, C_G1 + h] = np.exp(lg[h])
    return c


def make_rot(NP):
    inv = (np.float32(1.0) / (np.float32(10000.0) ** np.linspace(0.0, 1.0, 64, dtype=np.float32))).astype(np.float32)
    pos = np.concatenate([np.arange(NP, dtype=np.float32), np.full(NS, 16384.0, np.float32)])
    ang = (pos[:, None] * inv[None, :]).astype(np.float32)
    return np.concatenate([np.cos(ang), np.sin(ang)], axis=1).astype(np.float32)


def prep_shared(inp, DEPTH):
    f = lambda a: np.ascontiguousarray(np.asarray(a, dtype=np.float32))
    a_re = f(inp["s5_a_re"]); a_im = f(inp["s5_a_im"]); ldt = f(inp["s5_log_dt"])
    s5dup = np.zeros((DEPTH, 3, 128, 64), np.float32); s5row = np.zeros((DEPTH, 3, NS, 4096), np.float32)
    vecs = np.zeros((DEPTH, V_N, 128), np.float32)
    for l in range(DEPTH):
        s5dup[l, 0] = np.concatenate([a_re[l].T, a_re[l].T], 0)
        s5dup[l, 1] = np.concatenate([a_im[l].T, a_im[l].T], 0)
        s5dup[l, 2] = np.broadcast_to(ldt[l][None, :], (128, 64))
        s5row[l, 0] = np.broadcast_to(a_re[l].reshape(1, 4096), (NS, 4096))
        s5row[l, 1] = np.broadcast_to(a_im[l].reshape(1, 4096), (NS, 4096))
        s5row[l, 2] = np.broadcast_to(np.repeat(ldt[l], 64)[None, :], (NS, 4096))
        vecs[l, V_NM:V_NM + 16] = f(inp["norm_mix"])[l].reshape(16, 128)
        vecs[l, V_NF:V_NF + 16] = f(inp["norm_ffn"])[l].reshape(16, 128)
        vecs[l, V_SD:V_SD + 8] = f(inp["s5_d"])[l].reshape(8, 128)
        vecs[l, V_BG:V_BG + 8] = f(inp["s5_b_glu"])[l].reshape(8, 128)
        vecs[l, V_GB:V_GB + 4] = f(inp["gla_b_gate"])[l].reshape(4, 128)
    sh = {
        "w_in": f(inp["w_in"])[:DEPTH], "s5dup": s5dup, "s5row": s5row,
        "s5b_re": f(inp["s5_b_re"])[:DEPTH], "s5b_im": f(inp["s5_b_im"])[:DEPTH],
        "s5c_re": f(inp["s5_c_re"])[:DEPTH].reshape(DEPTH, 1024, 64), "s5c_im": f(inp["s5_c_im"])[:DEPTH].reshape(DEPTH, 1024, 64),
        "w_glu": f(inp["s5_w_glu"])[:DEPTH], "w_s5o": f(inp["s5_w_out"])[:DEPTH], "w_gate": f(inp["gla_w_gate"])[:DEPTH],
        "gnorm": np.ascontiguousarray(np.broadcast_to(f(inp["gla_norm"])[:DEPTH, None, :], (DEPTH, 128, 256))),
        "w_glao": f(inp["gla_w_out"])[:DEPTH], "w_reto": f(inp["ret_w_out"])[:DEPTH], "w_mix": f(inp["w_mix_out"])[:DEPTH],
        "w_fg": f(inp["w_ffn_gate"])[:DEPTH], "w_fu": f(inp["w_ffn_up"])[:DEPTH], "w_fd": f(inp["w_ffn_down"])[:DEPTH],
        "vecs": vecs, "nfin": np.ascontiguousarray(np.broadcast_to(f(inp["norm_final"])[None, :], (128, D))),
        "consts": make_consts(),
    }
    return sh


def prep_core(inp, sh, seq, srow0, NPT, DEPTH):
    f = lambda a: np.ascontiguousarray(np.asarray(a, dtype=np.float32))
    NP = NPT * 128
    m = dict(sh)
    m["xp"] = f(inp["x_prompt"][seq, :NP])
    m["xs"] = f(inp["x_sample"][srow0:srow0 + NS, 0])
    m["s5r0"] = f(inp["state_s5_re"][:DEPTH, srow0:srow0 + NS]).reshape(DEPTH, NS, 4096)
    m["s5i0"] = f(inp["state_s5_im"][:DEPTH, srow0:srow0 + NS]).reshape(DEPTH, NS, 4096)
    m["gla0"] = f(inp["state_gla"][:DEPTH, srow0:srow0 + NS])
    m["ret0"] = f(inp["state_ret"][:DEPTH, srow0:srow0 + NS])
    m["rot"] = make_rot(NP)
    return m


_NC_CACHE = {}


def kernel(**inputs):
    NPT, DEPTH, NCORES = 16, 2, 8
    if "nc" not in _NC_CACHE:
        _NC_CACHE["nc"] = build(NPT, DEPTH)
    nc = _NC_CACHE["nc"]
    sh = prep_shared(inputs, DEPTH)
    in_maps = [prep_core(inputs, sh, c % 4, c * NS, NPT, DEPTH) for c in range(NCORES)]
    res = run_bass_kernel_spmd(nc, in_maps, core_ids=list(range(NCORES)))
    r = res.results
    g = lambda c, k: np.asarray(r[c][k], dtype=np.float32)
    y_prompt = np.stack([g(c, "yp") for c in range(4)], 0)
    y_sample = np.concatenate([g(c, "ys") for c in range(NCORES)], 0)[:, None, :]
    ps5 = np.stack([g(c, "ps5") for c in range(4)], 1)
    pgla = np.stack([g(c, "pgla") for c in range(4)], 1)
    pret = np.stack([g(c, "pret") for c in range(4)], 1)
    ss5 = np.concatenate([g(c, "ss5") for c in range(NCORES)], 1)
    sgla = np.concatenate([g(c, "sgla") for c in range(NCORES)], 1)
    sret = np.concatenate([g(c, "sret") for c in range(NCORES)], 1)
    return (y_prompt, y_sample,
            np.ascontiguousarray(ps5[..., :64]), np.ascontiguousarray(ps5[..., 64:]), pgla, pret,
            np.ascontiguousarray(ss5[..., :64]), np.ascontiguousarray(ss5[..., 64:]), sgla, sret)
```

```python
from contextlib import ExitStack
import numpy as np
import concourse.bass as bass
import concourse.mybir as mybir
from concourse.bass_utils import run_bass_kernel_spmd

F32 = mybir.dt.float32
BF16 = mybir.dt.bfloat16
AF = mybir.ActivationFunctionType
ALU = mybir.AluOpType
AX = mybir.AxisListType


class Buf:
    __slots__ = ("name", "w", "r")

    def __init__(self, name=""):
        self.name = name
        self.w = None
        self.r = {}


class Sched:
    ENG = ("pe", "act", "dve", "pool", "sp")
    NCHAN = 90

    def __init__(self, nc, es):
        self.nc = nc
        self.es = es
        self.eng = {"pe": nc.tensor, "act": nc.scalar, "dve": nc.vector, "pool": nc.gpsimd, "sp": nc.sync}
        self.sems = {}
        self.cnt = {}
        self.seen = {e: {} for e in self.ENG}
        self.pend = {e: ([], []) for e in self.ENG}
        self.pool = []
        for e in self.ENG:
            h = self.es.enter_context(self.nc.semaphore("E_" + e))
            self.sems["E_" + e] = h
            self.cnt["E_" + e] = 0
            nc.sync.sem_clear(h)
        for i in range(self.NCHAN):
            h = self.es.enter_context(self.nc.semaphore(f"DS{i}"))
            nc.sync.sem_clear(h)
            self.pool.append(h)
        nc.all_engine_barrier()

    def chan(self, name):
        key = "D_" + name
        self.sems[key] = self.pool.pop()
        self.cnt[key] = 0
        return key

    def _wait(self, e, tok):
        key, val = tok
        if key == "E_pe" and e == "pe":
            return
        if key.startswith("D_"):
            val = max(val, self.cnt[key])
        if self.seen[e].get(key, 0) >= val:
            return
        self.eng[e].wait_ge(self.sems[key], val)
        self.seen[e][key] = val

    def _deps(self, e, R, W):
        for b in R:
            if b.w is not None:
                self._wait(e, b.w)
        for b in W:
            if b.w is not None:
                self._wait(e, b.w)
            for k, v in b.r.items():
                self._wait(e, (k, v))

    def _commit(self, tok, R, W):
        k, v = tok
        for b in R:
            if b.r.get(k, 0) < v:
                b.r[k] = v
        for b in W:
            b.w = tok
            b.r = {}

    def op(self, e, fn, R=(), W=(), inc=True):
        self._deps(e, R, W)
        inst = fn(self.eng[e])
        pr, pw = self.pend[e]
        pr.extend(R)
        pw.extend(W)
        if not inc:
            return None
        own = "E_" + e
        self.cnt[own] += 1
        inst.then_inc(self.sems[own], 1)
        tok = (own, self.cnt[own])
        self._commit(tok, pr, pw)
        self.pend[e] = ([], [])
        return tok

    def dma(self, q, out, in_, ch, R=(), W=()):
        self._deps(q, R, W)
        inst = self.eng[q].dma_start(out=out, in_=in_)
        self.cnt[ch] += 16
        inst.then_inc(self.sems[ch], 16)
        tok = (ch, self.cnt[ch])
        self._commit(tok, R, W)
        return tok

    def barrier(self):
        toks = [(k, v) for k, v in self.cnt.items() if v > 0]
        for e in self.ENG:
            for t in toks:
                self._wait(e, t)


D = 2048
WIN = 14352
DFF = 5632
NS = 16
EPS = 1e-6
DBG = False
C_ID, C_J, C_TRI, C_GM, C_SGN, C_E16, C_D0, C_DH, C_GPOW, C_KDEC, C_G128, C_G1, C_N = (
    0, 128, 256, 384, 392, 400, 656, 1168, 2192, 2200, 2208, 2216, 2224)
V_NM, V_NF, V_SD, V_BG, V_GB, V_N = 0, 16, 32, 40, 48, 52


class Tl:
    __slots__ = ("t", "b")

    def __init__(self, t, name):
        self.t = t
        self.b = Buf(name)


def build(NPT, DEPTH):
    NP = NPT * 128
    NT = NP + NS
    tiles = [(i * 128, 128) for i in range(NPT)] + [(NP, NS)]
    blocks = [(b0, min(512, NP - b0)) for b0 in range(0, NP, 512)] + [(NP, NS)]
    nc = bass.Bass("TRN2", target_bir_lowering=False)

    def din(name, shape, dt=F32):
        return nc.dram_tensor(name, list(shape), dt, kind="ExternalInput").ap()

    def dout(name, shape, dt=F32):
        return nc.dram_tensor(name, list(shape), dt, kind="ExternalOutput").ap()

    def dscr(name, shape, dt=F32):
        if DBG:
            return nc.dram_tensor(name, list(shape), dt, kind="ExternalOutput").ap()
        return nc.dram_tensor(name, list(shape), dt).ap()

    xp = din("xp", [NP, D]); xs = din("xs", [NS, D])
    s5r0 = din("s5r0", [DEPTH, NS, 4096]); s5i0 = din("s5i0", [DEPTH, NS, 4096])
    gla0 = din("gla0", [DEPTH, NS, 4, 128, 256]); ret0 = din("ret0", [DEPTH, NS, 8, 128, 128])
    w_in = din("w_in", [DEPTH, D, WIN])
    s5dup = din("s5dup", [DEPTH, 3, 128, 64]); s5row = din("s5row", [DEPTH, 3, NS, 4096])
    s5b_re = din("s5b_re", [DEPTH, 64, 64, 16]); s5b_im = din("s5b_im", [DEPTH, 64, 64, 16])
    s5c_re = din("s5c_re", [DEPTH, 1024, 64]); s5c_im = din("s5c_im", [DEPTH, 1024, 64])
    w_glu = din("w_glu", [DEPTH, 1024, 1024]); w_s5o = din("w_s5o", [DEPTH, 1024, D])
    w_gate = din("w_gate", [DEPTH, 16, 512]); gnorm = din("gnorm", [DEPTH, 128, 256])
    w_glao = din("w_glao", [DEPTH, 1024, D]); w_reto = din("w_reto", [DEPTH, 1024, D])
    w_mix = din("w_mix", [DEPTH, D, D])
    w_fg = din("w_fg", [DEPTH, D, DFF]); w_fu = din("w_fu", [DEPTH, D, DFF]); w_fd = din("w_fd", [DEPTH, DFF, D])
    vecs = din("vecs", [DEPTH, V_N, 128]); nfin = din("nfin", [128, D])
    consts = din("consts", [128, C_N]); rot = din("rot", [NT, 128])

    yp = dout("yp", [NP, D]); ys = dout("ys", [NS, D])
    ps5 = dout("ps5", [DEPTH, 64, 128]); pgla = dout("pgla", [DEPTH, 4, 128, 256]); pret = dout("pret", [DEPTH, 8, 128, 128])
    ss5 = dout("ss5", [DEPTH, NS, 64, 128]); sgla = dout("sgla", [DEPTH, NS, 4, 128, 256]); sret = dout("sret", [DEPTH, NS, 8, 128, 128])

    UT = dscr("UT", [1024, NT], BF16); GQT = dscr("GQT", [512, NT]); GKT = dscr("GKT", [512, NT]); GLRT = dscr("GLRT", [16, NT])
    GV = dscr("GV", [NT, 1024], BF16); GR = dscr("GR", [NT, 1024]); RQ = dscr("RQ", [NT, 1024]); RK = dscr("RK", [NT, 1024])
    RV = dscr("RV", [NT, 1024], BF16); RG = dscr("RG", [NT, 1024]); MGT = dscr("MGT", [6144, NT], BF16)
    Y5A = dscr("Y5A", [1024, NT], BF16); A5T = dscr("A5T", [1024, NT], BF16); AGT = dscr("AGT", [1024, NT], BF16); ART = dscr("ART", [1024, NT], BF16)
    XMID = dscr("XMID", [NT, D]); X1 = dscr("X1", [NT, D]); FFTL = dscr("FFTL", [NPT + 1, 128, DFF // 128, 128], BF16)

    def xrows(src, ti):
        r0, n = tiles[ti]
        if src is None:
            return xp[r0:r0 + n, :] if ti < NPT else xs[:, :]
        return src[r0:r0 + n, :]

    with ExitStack() as es:
        S = Sched(nc, es)
        chans = {}

        def ch(name):
            if name not in chans:
                chans[name] = S.chan(name)
            return chans[name]

        def V(fn, R=(), W=(), inc=True): return S.op("dve", fn, [x.b if isinstance(x, Tl) else x for x in R], [x.b if isinstance(x, Tl) else x for x in W], inc)
        def A(fn, R=(), W=(), inc=True): return S.op("act", fn, [x.b if isinstance(x, Tl) else x for x in R], [x.b if isinstance(x, Tl) else x for x in W], inc)
        def G(fn, R=(), W=(), inc=True): return S.op("pool", fn, [x.b if isinstance(x, Tl) else x for x in R], [x.b if isinstance(x, Tl) else x for x in W], inc)
        def M(fn, R=(), W=(), inc=True): return S.op("pe", fn, [x.b if isinstance(x, Tl) else x for x in R], [x.b if isinstance(x, Tl) else x for x in W], inc)

        def DMA(q, out, in_, chn, R=(), W=()):
            if len(W) == 0 and q == "sp":
                q = "act"
            elif len(W) > 0 and q == "act":
                q = "sp"
            return S.dma(q, out, in_, ch(chn), [x.b if isinstance(x, Tl) else x for x in R], [x.b if isinstance(x, Tl) else x for x in W])

        uid = [0]

        def mk(stack, kind):
            def f(name, shape, dt=F32):
                alloc = nc.sbuf_tensor if kind == "s" else nc.psum_tensor
                uid[0] += 1
                name = f"{name}_{uid[0]}"
                return Tl(stack.enter_context(alloc(name, list(shape), dt)), name)
            return f

        T0 = mk(es, "s")
        cst = T0("cst", [128, C_N]); idb = T0("idb", [128, 128], BF16)
        DMA("sp", cst.t[:], consts, "cst", W=[cst])
        V(lambda e: e.tensor_copy(out=idb.t[:], in_=cst.t[:, C_ID:C_ID + 128]), R=[cst], W=[idb])
        idf = cst.t[:, C_ID:C_ID + 128]

        def rms_rstd(TT, src_ap, src_tl, n, width, name):
            junk = TT(name + "_j", [128, width]); ss = TT(name + "_s", [128, 1]); rs = TT(name + "_r", [128, 1])
            return junk, ss, rs

        def norm_T(TT, PP, src, gcol, hT, hTb, vecT, pfx):
            xt = [TT(pfx + f"xt{i}", [128, D]) for i in range(2)]
            junk = TT(pfx + "junk", [128, D], BF16)
            ss = [TT(pfx + f"ss{i}", [128, 16]) for i in range(2)]
            rs = [TT(pfx + f"rs{i}", [128, 16]) for i in range(2)]
            xn = [TT(pfx + f"xn{i}", [128, D], BF16) for i in range(2)]
            ptr = [PP(pfx + f"ptr{i}", [128, 4, 128], BF16) for i in range(2)]
            for ti, (r0, n) in enumerate(tiles):
                s = ti % 2
                DMA("sp", xt[s].t[:n, :], xrows(src, ti), pfx + f"xt{s}", W=[xt[s]])
                V(lambda e: e.memset(ss[s].t[:], 0.0), W=[ss[s]])
                A(lambda e: e.activation(out=junk.t[:n, :], in_=xt[s].t[:n, :], func=AF.Square, accum_out=ss[s].t[:n, 0:1]), R=[xt[s]], W=[junk, ss[s]])
                A(lambda e: e.activation(out=rs[s].t[:n, 0:1], in_=ss[s].t[:n, 0:1], func=AF.Sqrt, scale=1.0 / D, bias=EPS), R=[ss[s]], W=[rs[s]])
                V(lambda e: e.reciprocal(out=rs[s].t[:n, 0:1], in_=rs[s].t[:n, 0:1]), R=[rs[s]], W=[rs[s]])
                A(lambda e: e.activation(out=xn[s].t[:n, :], in_=xt[s].t[:n, :], func=AF.Copy, scale=rs[s].t[:n, 0:1]), R=[xt[s], rs[s]], W=[xn[s]])
                for q in range(4):
                    p = ptr[q % 2]
                    for j in range(4):
                        kc = q * 4 + j
                        M(lambda e: e.transpose(out=p.t[:, j, :n], in_=xn[s].t[:n, kc * 128:(kc + 1) * 128], identity=idb.t[:n, :n]),
                          R=[xn[s], idb], W=[p], inc=(j == 3))
                    V(lambda e: e.tensor_tensor(out=hT.t[:, q * 4:q * 4 + 4, r0:r0 + n], in0=p.t[:, :, :n],
                                                in1=vecT.t[:, gcol + q * 4:gcol + q * 4 + 4].unsqueeze(2).to_broadcast([128, 4, n]), op=ALU.mult),
                      R=[p, vecT], W=[hTb[ti]])
            if DBG and pfx == "n1" and not hasattr(nc, "_dbg1"):
                nc._dbg1 = 1
                d1 = dscr("DBG_ss1", [128, 16]); d2 = dscr("DBG_rs1", [128, 16]); d3 = dscr("DBG_xt1", [128, D]); d4 = dscr("DBG_xn1", [128, D], BF16)
                DMA("sp", d1, ss[1].t[:], "dbg", R=[ss[1]]); DMA("sp", d2, rs[1].t[:], "dbg", R=[rs[1]])
                DMA("sp", d3, xt[1].t[:], "dbg", R=[xt[1]]); DMA("sp", d4, xn[1].t[:], "dbg", R=[xn[1]])

        def sincos(th, sn, cs, t1, t2, RW):
            A(lambda e: e.activation(out=t1, in_=th, func=AF.Sin, scale=1.0 / 16), **RW)
            V(lambda e: e.tensor_tensor(out=t1, in0=t1, in1=t1, op=ALU.mult), **RW)
            V(lambda e: e.tensor_scalar(out=cs, in0=t1, scalar1=-2.0, scalar2=1.0, op0=ALU.mult, op1=ALU.add), **RW)
            A(lambda e: e.activation(out=sn, in_=th, func=AF.Sin, scale=1.0 / 8), **RW)
            for _ in range(3):
                V(lambda e: e.tensor_tensor(out=t1, in0=sn, in1=cs, op=ALU.mult), **RW)
                V(lambda e: e.tensor_tensor(out=t2, in0=sn, in1=sn, op=ALU.mult), **RW)
                V(lambda e: e.tensor_tensor(out=cs, in0=cs, in1=cs, op=ALU.mult), **RW)
                V(lambda e: e.tensor_tensor(out=cs, in0=cs, in1=t2, op=ALU.subtract), **RW)
                V(lambda e: e.tensor_scalar_mul(out=sn, in0=t1, scalar1=2.0), **RW)

        def blk_tiles(b0, bn):
            return [ti for ti, (r0, n) in enumerate(tiles) if r0 < b0 + bn and r0 + n > b0]

        for l in range(DEPTH):
            xsrc = None if l == 0 else X1
            with ExitStack() as ls:
                TL = mk(ls, "s")
                vecT = TL("vecT", [128, V_N])
                with ExitStack() as ps:
                    TT = mk(ps, "s"); PP = mk(ps, "p")
                    vr = TT("vr", [V_N, 128]); pv = PP("pv", [128, V_N])
                    DMA("sp", vr.t[:], vecs[l], "vr", W=[vr])
                    M(lambda e: e.transpose(out=pv.t[:], in_=vr.t[:], identity=idf[:V_N, :V_N]), R=[vr, cst], W=[pv])
                    V(lambda e: e.tensor_copy(out=vecT.t[:], in_=pv.t[:]), R=[pv], W=[vecT])
                    S.barrier()

                with ExitStack() as ps:
                    TT = mk(ps, "s"); PP = mk(ps, "p")
                    hT = TT("hT", [128, 16, NT], BF16)
                    hTb = [Buf(f"hT{ti}") for ti in range(len(tiles))]
                    with ExitStack() as ps1:
                        norm_T(mk(ps1, "s"), mk(ps1, "p"), xsrc, V_NM, hT, hTb, vecT, "n1")
                        S.barrier()
                    if DBG and l == 0:
                        dbg_hT = dscr("DBG_hT", [128, 16, NT], BF16)
                        DMA("sp", dbg_hT, hT.t[:], "dbg", R=hTb)
                    wsl = [TT(f"w{i}", [128, 16, 512], BF16) for i in range(3)]
                    pm = [PP(f"pm{i}", [128, 512]) for i in range(4)]
                    stf = [TT(f"stf{i}", [128, 512]) for i in range(4)]
                    stb = [TT(f"stb{i}", [128, 512], BF16) for i in range(4)]
                    segs = [(0, 1024, "F", AF.Copy, UT, True), (1024, 512, "F", AF.Copy, GQT, False), (1536, 512, "F", AF.Copy, GKT, False),
                            (2048, 1024, "T", AF.Copy, GV, True), (3072, 1024, "T", AF.Silu, GR, False), (4096, 16, "F", AF.Copy, GLRT, False),
                            (4112, 1024, "T", AF.Copy, RQ, False), (5136, 1024, "T", AF.Copy, RK, False), (6160, 1024, "T", AF.Copy, RV, True),
                            (7184, 1024, "T", AF.Silu, RG, False), (8208, 6144, "F", AF.Sigmoid, MGT, True)]
                    wi = 0; ui = 0
                    for (c0, ncol, form, func, dst, isb) in segs:
                        for cb in range(0, ncol, 512):
                            ncb = min(512, ncol - cb)
                            w = wsl[wi % 3]
                            DMA("pool", w.t[:, :, :ncb], w_in[l][:, c0 + cb:c0 + cb + ncb].rearrange("(k p) e -> p k e", p=128), f"w{wi % 3}", W=[w])
                            wi += 1
                            if form == "T":
                                for ti, (r0, n) in enumerate(tiles):
                                    p = pm[ui % 4]; st = (stb if isb else stf)[ui % 4]; ui += 1
                                    for kc in range(16):
                                        M(lambda e: e.matmul(p.t[:n, :ncb], lhsT=hT.t[:, kc, r0:r0 + n], rhs=w.t[:, kc, :ncb], start=(kc == 0), stop=(kc == 15)),
                                          R=[hTb[ti], w], W=[p], inc=(kc == 15))
                                    A(lambda e: e.activation(out=st.t[:n, :ncb], in_=p.t[:n, :ncb], func=func), R=[p], W=[st])
                                    DMA("sp", dst[r0:r0 + n, cb:cb + ncb], st.t[:n, :ncb], ("sb" if isb else "sf") + str((ui - 1) % 4), R=[st])
                            else:
                                for j0 in range(0, ncb, 128):
                                    m = min(128, ncb - j0)
                                    for (b0, bn) in blocks:
                                        p = pm[ui % 4]; st = (stb if isb else stf)[ui % 4]; ui += 1
                                        for kc in range(16):
                                            M(lambda e: e.matmul(p.t[:m, :bn], lhsT=w.t[:, kc, j0:j0 + m], rhs=hT.t[:, kc, b0:b0 + bn], start=(kc == 0), stop=(kc == 15)),
                                              R=[hTb[t_] for t_ in blk_tiles(b0, bn)] + [w], W=[p], inc=(kc == 15))
                                        A(lambda e: e.activation(out=st.t[:m, :bn], in_=p.t[:m, :bn], func=func), R=[p], W=[st])
                                        DMA("sp", dst[cb + j0:cb + j0 + m, b0:b0 + bn], st.t[:m, :bn], ("sb" if isb else "sf") + str((ui - 1) % 4), R=[st])
                    S.barrier()

                PI = float(np.pi)
                with ExitStack() as ps:
                    TT = mk(ps, "s"); PP = mk(ps, "p")
                    EC = TT("EC", [128, 64, 128]); ES = TT("ES", [128, 64, 128])
                    sp_ = TT("s5p", [128, 16, 64]); prm = TT("prm", [128, 3, 64])
                    X1_ = TT("X1_", [128, 64, 16]); X2_ = TT("X2_", [128, 64, 16])
                    CA = TT("CA", [128, 8, 128]); CB = TT("CB", [128, 8, 128])
                    xl = TT("xl", [128, 64])
                    DMA("sp", prm.t[:], s5dup[l].rearrange("a p g -> p a g"), "prm", W=[prm])
                    ar = prm.t[:, 0, :]; ai = prm.t[:, 1, :]; ldt = prm.t[:, 2, :]
                    (dtb, mag, th, a1, sn, cs, abr, abi, den, nr, fr, fi, tq, tr) = [sp_.t[:, i_, :] for i_ in range(14)]
                    RW = dict(R=[sp_, prm], W=[sp_])
                    A(lambda e: e.activation(out=dtb, in_=ldt, func=AF.Exp), **RW)
                    V(lambda e: e.tensor_tensor(out=tq, in0=ar, in1=dtb, op=ALU.mult), **RW)
                    A(lambda e: e.activation(out=mag, in_=tq, func=AF.Exp), **RW)
                    V(lambda e: e.tensor_tensor(out=th, in0=ai, in1=dtb, op=ALU.mult), **RW)
                    sincos(th, sn, cs, a1, tq, RW)
                    V(lambda e: e.tensor_tensor(out=abr, in0=mag, in1=cs, op=ALU.mult), **RW)
                    V(lambda e: e.tensor_tensor(out=abi, in0=mag, in1=sn, op=ALU.mult), **RW)
                    V(lambda e: e.tensor_tensor(out=den, in0=ar, in1=ar, op=ALU.mult), **RW)
                    V(lambda e: e.tensor_tensor(out=tq, in0=ai, in1=ai, op=ALU.mult), **RW)
                    V(lambda e: e.tensor_tensor(out=den, in0=den, in1=tq, op=ALU.add), **RW)
                    V(lambda e: e.reciprocal(out=den, in_=den), **RW)
                    V(lambda e: e.tensor_scalar_add(out=nr, in0=abr, scalar1=-1.0), **RW)
                    V(lambda e: e.tensor_tensor(out=tq, in0=nr, in1=ar, op=ALU.mult), **RW)
                    V(lambda e: e.tensor_tensor(out=tr, in0=abi, in1=ai, op=ALU.mult), **RW)
                    V(lambda e: e.tensor_tensor(out=tq, in0=tq, in1=tr, op=ALU.add), **RW)
                    V(lambda e: e.tensor_tensor(out=fr, in0=tq, in1=den, op=ALU.mult), **RW)
                    V(lambda e: e.tensor_tensor(out=tq, in0=abi, in1=ar, op=ALU.mult), **RW)
                    V(lambda e: e.tensor_tensor(out=tr, in0=nr, in1=ai, op=ALU.mult), **RW)
                    V(lambda e: e.tensor_tensor(out=tq, in0=tq, in1=tr, op=ALU.subtract), **RW)
                    V(lambda e: e.tensor_tensor(out=fi, in0=tq, in1=den, op=ALU.mult), **RW)
                    xlb = [Buf("xl0"), Buf("xl1")]
                    V(lambda e: e.memset(xl.t[:], 0.0), W=xlb)
                    with ExitStack() as ps1:
                        T1 = mk(ps1, "s")
                        t1 = T1("et1", [128, 64, 64]); t2 = T1("et2", [128, 64, 64])
                        A_ = T1("A_", [128, 64, 16]); Bm = T1("Bm", [128, 64, 16]); tb = T1("tb", [128, 64, 16])
                        V(lambda e: e.tensor_copy(out=EC.t[:, :, 0:1], in_=cs.unsqueeze(2)), R=[sp_], W=[EC])
                        V(lambda e: e.tensor_copy(out=ES.t[:, :, 0:1], in_=sn.unsqueeze(2)), R=[sp_], W=[ES])
                        for s_ in range(7):
                            n_ = 2 ** s_
                            cb_ = EC.t[:, :, n_ - 1:n_].to_broadcast([128, 64, n_]); sb_ = ES.t[:, :, n_ - 1:n_].to_broadcast([128, 64, n_])
                            V(lambda e: e.tensor_tensor(out=t1.t[:, :, :n_], in0=EC.t[:, :, 0:n_], in1=cb_, op=ALU.mult), R=[EC], W=[t1])
                            V(lambda e: e.tensor_tensor(out=t2.t[:, :, :n_], in0=ES.t[:, :, 0:n_], in1=sb_, op=ALU.mult), R=[ES], W=[t2])
                            V(lambda e: e.tensor_tensor(out=t1.t[:, :, :n_], in0=t1.t[:, :, :n_], in1=t2.t[:, :, :n_], op=ALU.subtract), R=[t1, t2], W=[t1])
                            V(lambda e: e.tensor_tensor(out=t2.t[:, :, :n_], in0=EC.t[:, :, 0:n_], in1=sb_, op=ALU.mult), R=[EC, ES], W=[t2])
                            V(lambda e: e.tensor_copy(out=EC.t[:, :, n_:2 * n_], in_=t1.t[:, :, :n_]), R=[t1], W=[EC])
                            V(lambda e: e.tensor_tensor(out=t1.t[:, :, :n_], in0=ES.t[:, :, 0:n_], in1=cb_, op=ALU.mult), R=[EC, ES], W=[t1])
                            V(lambda e: e.tensor_tensor(out=ES.t[:, :, n_:2 * n_], in0=t1.t[:, :, :n_], in1=t2.t[:, :, :n_], op=ALU.add), R=[t1, t2], W=[ES])
                        DMA("sp", A_.t[0:64], s5b_re[l].rearrange("g p n -> p g n"), "bb", W=[A_]); DMA("sp", A_.t[64:128], s5b_im[l].rearrange("g p n -> p g n"), "bb", W=[A_])
                        DMA("sp", Bm.t[0:64], s5b_im[l].rearrange("g p n -> p g n"), "bb", W=[Bm]); DMA("sp", Bm.t[64:128], s5b_re[l].rearrange("g p n -> p g n"), "bb", W=[Bm])
                        V(lambda e: e.tensor_scalar_mul(out=Bm.t[0:64], in0=Bm.t[0:64], scalar1=-1.0), R=[Bm], W=[Bm])
                        frb = fr.unsqueeze(2).to_broadcast([128, 64, 16]); fib = fi.unsqueeze(2).to_broadcast([128, 64, 16])
                        V(lambda e: e.tensor_tensor(out=X1_.t[:], in0=A_.t[:], in1=frb, op=ALU.mult), R=[A_, sp_], W=[X1_])
                        V(lambda e: e.tensor_tensor(out=tb.t[:], in0=Bm.t[:], in1=fib, op=ALU.mult), R=[Bm, sp_], W=[tb])
                        V(lambda e: e.tensor_tensor(out=X1_.t[:], in0=X1_.t[:], in1=tb.t[:], op=ALU.add), R=[X1_, tb], W=[X1_])
                        V(lambda e: e.tensor_tensor(out=X2_.t[:], in0=A_.t[:], in1=fib, op=ALU.mult), R=[A_, sp_], W=[X2_])
                        V(lambda e: e.tensor_tensor(out=tb.t[:], in0=Bm.t[:], in1=frb, op=ALU.mult), R=[Bm, sp_], W=[tb])
                        V(lambda e: e.tensor_tensor(out=X2_.t[:], in0=X2_.t[:], in1=tb.t[:], op=ALU.subtract), R=[X2_, tb], W=[X2_])
                        DMA("sp", CA.t[:, :, 0:64], s5c_re[l].rearrange("(t q) p -> q t p", q=128), "cc", W=[CA]); DMA("sp", CA.t[:, :, 64:128], s5c_im[l].rearrange("(t q) p -> q t p", q=128), "cc", W=[CA])
                        DMA("sp", CB.t[:, :, 0:64], s5c_im[l].rearrange("(t q) p -> q t p", q=128), "cc", W=[CB]); DMA("sp", CB.t[:, :, 64:128], s5c_re[l].rearrange("(t q) p -> q t p", q=128), "cc", W=[CB])
                        V(lambda e: e.tensor_scalar_mul(out=CA.t[:, :, 64:128], in0=CA.t[:, :, 64:128], scalar1=-1.0), R=[CA], W=[CA])
                        V(lambda e: e.tensor_scalar_mul(out=CB.t[:], in0=CB.t[:], scalar1=-1.0), R=[CB], W=[CB])
                        S.barrier()
                    cT = EC.t[:, :, 127]; sT = ES.t[:, :, 127]
                    with ExitStack() as ps1:
                        T1 = mk(ps1, "s")
                        LB = [T1(f"LB{v}", [128, 8, 128], BF16) for v in range(2)]; LC = [T1(f"LC{v}", [128, 8, 128], BF16) for v in range(2)]
                        for v in range(2):
                            G(lambda e: e.memset(LC[v].t[:], 0.0), W=[LC[v]])
                        uT = T1("uT", [128, NT], BF16); ypre = T1("ypre", [128, NT]); y5s = T1("y5s", [128, NT], BF16)
                        m1 = [T1(f"m1{i}", [128, 4, 128]) for i in range(2)]; m2 = [T1(f"m2{i}", [128, 4, 128]) for i in range(2)]
                        win = [T1(f"win{i}", [128, 4, 128]) for i in range(2)]; wv = [T1(f"wv{i}", [128, 4, 128]) for i in range(2)]
                        a1b = [T1(f"a1b{i}", [128, 4, 128], BF16) for i in range(2)]; a2b = [T1(f"a2b{i}", [128, 4, 128], BF16) for i in range(2)]
                        c1 = [T1(f"c1{i}", [128, 16]) for i in range(2)]; c2 = [T1(f"c2{i}", [128, 16]) for i in range(2)]
                        rp = T1("rp", [16, 3, 512]); rw = T1("rw", [16, 8, 512]); x0r = T1("x0r", [16, 512]); x0i = T1("x0i", [16, 512])
                        ax0 = T1("ax0", [16, 8, 128]); xnew = T1("xnew", [16, 8, 128]); xnb = T1("xnb", [16, 8, 128], BF16); xnT = T1("xnT", [128, 8, 16], BF16)
                        gz = T1("gz", [128, 512]); gs = T1("gs", [128, 512])
                        P1 = mk(ps1, "p")
                        PS1 = [P1(f"PS1{i}", [128, 4, 128]) for i in range(2)]; PS2 = [P1(f"PS2{i}", [128, 4, 128]) for i in range(2)]
                        PY = [P1(f"PY{i}", [128, 128]) for i in range(2)]; ptw = PY[0]
                        psw = P1("psw", [128, 16]); ptx = P1("ptx", [128, 8, 16], BF16)
                        for gt in range(8):
                            DMA("sp", uT.t[:], UT[gt * 128:(gt + 1) * 128, :], "uT", W=[uT])
                            for v, X_ in enumerate((X1_, X2_)):
                                M(lambda e: e.transpose(out=ptw.t[:], in_=X_.t[:, gt * 8:(gt + 1) * 8, :], identity=idf), R=[X_, cst], W=[ptw])
                                for gg in range(8):
                                    V(lambda e: e.tensor_scalar_mul(out=LB[v].t[:, gg, :], in0=ptw.t[:], scalar1=cst.t[:, C_GM + gg:C_GM + gg + 1]), R=[ptw, cst], W=[LB[v]])
                            for v, C_ in enumerate((CA, CB)):
                                M(lambda e: e.transpose(out=ptw.t[:], in_=C_.t[:, gt, :], identity=idf), R=[C_, cst], W=[ptw])
                                for gg in range(8):
                                    V(lambda e: e.tensor_copy(out=LC[v].t[:, gg, 16 * gg:16 * gg + 16], in_=ptw.t[:, 16 * gg:16 * gg + 16]), R=[ptw], W=[LC[v]])
                            dcol = vecT.t[:, V_SD + gt:V_SD + gt + 1]
                            units = [(bi, hh) for bi in range(NPT) for hh in range(2)]

                            def st1a(u):
                                bi, hh = u; t0 = bi * 128; g0 = gt * 8 + hh * 4; i = hh
                                for q in range(4):
                                    M(lambda e: e.matmul(PS1[i].t[:, q, :], lhsT=LB[0].t[:, hh * 4 + q, :], rhs=uT.t[:, t0:t0 + 128], start=True, stop=True), R=[LB[0], uT], W=[PS1[i]], inc=False)
                                    M(lambda e: e.matmul(PS2[i].t[:, q, :], lhsT=LB[1].t[:, hh * 4 + q, :], rhs=uT.t[:, t0:t0 + 128], start=True, stop=True), R=[LB[1], uT], W=[PS2[i]], inc=(q == 3))
                                V(lambda e: e.tensor_tensor(out=m1[i].t[:], in0=PS1[i].t[:], in1=EC.t[:, g0:g0 + 4, :], op=ALU.mult), R=[PS1[i], EC], W=[m1[i]])
                                V(lambda e: e.tensor_tensor(out=m2[i].t[:], in0=PS2[i].t[:], in1=ES.t[:, g0:g0 + 4, :], op=ALU.mult), R=[PS2[i], ES], W=[m2[i]])
                                G(lambda e: e.tensor_tensor(out=win[i].t[:], in0=m1[i].t[:], in1=m2[i].t[:], op=ALU.add), R=[m1[i], m2[i]], W=[win[i]])
                                for q in range(4):
                                    V(lambda e: e.tensor_tensor_scan(out=wv[i].t[:, q, :], data0=mag[:, g0 + q:g0 + q + 1].to_broadcast([128, 128]), data1=win[i].t[:, q, :],
                                                                     initial=xl.t[:, g0 + q:g0 + q + 1], op0=ALU.mult, op1=ALU.add), R=[win[i], xlb[hh], sp_], W=[wv[i]])
                                G(lambda e: e.tensor_tensor(out=a1b[i].t[:], in0=wv[i].t[:], in1=EC.t[:, g0:g0 + 4, :], op=ALU.mult), R=[wv[i], EC], W=[a1b[i]])
                                G(lambda e: e.tensor_tensor(out=a2b[i].t[:], in0=wv[i].t[:], in1=ES.t[:, g0:g0 + 4, :], op=ALU.mult), R=[wv[i], ES], W=[a2b[i]])

                            def st1b(u):
                                bi, hh = u; g0 = gt * 8 + hh * 4; i = hh
                                M(lambda e: e.matmul(psw.t[:, 0:4], lhsT=cst.t[:, C_J:C_J + 128], rhs=wv[i].t[:, :, 127], start=True, stop=True), R=[cst, wv[i]], W=[psw])
                                V(lambda e: e.tensor_tensor(out=c1[i].t[:, 0:4], in0=wv[i].t[:, :, 127], in1=cT[:, g0:g0 + 4], op=ALU.mult), R=[wv[i], EC], W=[c1[i]])
                                V(lambda e: e.tensor_tensor(out=c2[i].t[:, 0:4], in0=psw.t[:, 0:4], in1=sT[:, g0:g0 + 4], op=ALU.mult), R=[psw, ES], W=[c2[i]])
                                V(lambda e: e.tensor_tensor(out=xl.t[:, g0:g0 + 4], in0=c1[i].t[:, 0:4], in1=c2[i].t[:, 0:4], op=ALU.add), R=[c1[i], c2[i]], W=[xlb[hh]])

                            def st2(u):
                                bi, hh = u; t0 = bi * 128; i = hh; py = PY[bi % 2]
                                for q in range(4):
                                    gg = hh * 4 + q
                                    M(lambda e: e.matmul(py.t[:], lhsT=LC[0].t[:, gg, :], rhs=a1b[i].t[:, q, :], start=(gg == 0), stop=False), R=[LC[0], a1b[i]], W=[py], inc=False)
                                    M(lambda e: e.matmul(py.t[:], lhsT=LC[1].t[:, gg, :], rhs=a2b[i].t[:, q, :], start=False, stop=(gg == 7)), R=[LC[1], a2b[i]], W=[py], inc=(q == 3))
                                if hh == 1:
                                    V(lambda e: e.scalar_tensor_tensor(out=ypre.t[:, t0:t0 + 128], in0=uT.t[:, t0:t0 + 128], scalar=dcol, in1=py.t[:], op0=ALU.mult, op1=ALU.add), R=[uT, vecT, py], W=[ypre])

                            st1a(units[0])
                            for ui_ in range(1, len(units)):
                                st1a(units[ui_]); st1b(units[ui_ - 1]); st2(units[ui_ - 1])
                            st1b(units[-1]); st2(units[-1])
                            DMA("sp", rp.t[:], s5row[l][:, :, gt * 512:(gt + 1) * 512].rearrange("a b x -> b a x"), "rp", W=[rp])
                            DMA("sp", x0r.t[:], s5r0[l][:, gt * 512:(gt + 1) * 512], "rp", W=[x0r]); DMA("sp", x0i.t[:], s5i0[l][:, gt * 512:(gt + 1) * 512], "rp", W=[x0i])
                            (dtr, mgr, thr, a1r, snr, csr, q1, q2) = [rw.t[:, i_, :] for i_ in range(8)]; abrr = thr; abir = dtr
                            RWs = dict(R=[rp, rw], W=[rw])
                            A(lambda e: e.activation(out=dtr, in_=rp.t[:, 2, :], func=AF.Exp), **RWs)
                            V(lambda e: e.tensor_tensor(out=q1, in0=rp.t[:, 0, :], in1=dtr, op=ALU.mult), **RWs)
                            A(lambda e: e.activation(out=mgr, in_=q1, func=AF.Exp), **RWs)
                            V(lambda e: e.tensor_tensor(out=thr, in0=rp.t[:, 1, :], in1=dtr, op=ALU.mult), **RWs)
                            sincos(thr, snr, csr, a1r, q1, RWs)
                            V(lambda e: e.tensor_tensor(out=abrr, in0=mgr, in1=csr, op=ALU.mult), **RWs)
                            V(lambda e: e.tensor_tensor(out=abir, in0=mgr, in1=snr, op=ALU.mult), **RWs)
                            v3 = lambda ap: ap.rearrange("b (g p) -> b g p", p=64)
                            V(lambda e: e.tensor_tensor(out=q1, in0=abrr, in1=x0r.t[:], op=ALU.mult), R=[rw, x0r], W=[rw])
                            V(lambda e: e.tensor_tensor(out=q2, in0=abir, in1=x0i.t[:], op=ALU.mult), R=[rw, x0i], W=[rw])
                            V(lambda e: e.tensor_tensor(out=ax0.t[:, :, 0:64], in0=v3(q1), in1=v3(q2), op=ALU.subtract), R=[rw], W=[ax0])
                            V(lambda e: e.tensor_tensor(out=q1, in0=abrr, in1=x0i.t[:], op=ALU.mult), R=[rw, x0i], W=[rw])
                            V(lambda e: e.tensor_tensor(out=q2, in0=abir, in1=x0r.t[:], op=ALU.mult), R=[rw, x0r], W=[rw])
                            V(lambda e: e.tensor_tensor(out=ax0.t[:, :, 64:128], in0=v3(q1), in1=v3(q2), op=ALU.add), R=[rw], W=[ax0])
                            for hh in range(2):
                                for q in range(4):
                                    M(lambda e: e.matmul(PS1[hh].t[:16, q, :], lhsT=uT.t[:, NP:NT], rhs=LB[0].t[:, hh * 4 + q, :], start=True, stop=True), R=[uT, LB[0]], W=[PS1[hh]], inc=(q == 3))
                                V(lambda e: e.tensor_tensor(out=xnew.t[:, hh * 4:hh * 4 + 4, :], in0=PS1[hh].t[:16], in1=ax0.t[:, hh * 4:hh * 4 + 4, :], op=ALU.add), R=[PS1[hh], ax0], W=[xnew])
                            DMA("sp", ss5[l][:, gt * 8:(gt + 1) * 8, :], xnew.t[:], "xnw", R=[xnew])
                            V(lambda e: e.tensor_copy(out=xnb.t[:], in_=xnew.t[:]), R=[xnew], W=[xnb])
                            for gg in range(8):
                                M(lambda e: e.transpose(out=ptx.t[:, gg, :], in_=xnb.t[:, gg, :], identity=idb.t[:16, :16]), R=[xnb, idb], W=[ptx], inc=(gg == 7))
                            A(lambda e: e.activation(out=xnT.t[:], in_=ptx.t[:], func=AF.Copy), R=[ptx], W=[xnT])
                            for gg in range(8):
                                M(lambda e: e.matmul(PY[1].t[:, 0:16], lhsT=LC[0].t[:, gg, :], rhs=xnT.t[:, gg, :], start=(gg == 0), stop=(gg == 7)), R=[LC[0], xnT], W=[PY[1]], inc=(gg == 7))
                            V(lambda e: e.scalar_tensor_tensor(out=ypre.t[:, NP:NT], in0=uT.t[:, NP:NT], scalar=dcol, in1=PY[1].t[:, 0:16], op0=ALU.mult, op1=ALU.add), R=[uT, vecT, PY[1]], W=[ypre])
                            for (b0, bn) in blocks:
                                yv = ypre.t[:, b0:b0 + bn]
                                A(lambda e: e.activation(out=gz.t[:, :bn], in_=yv, func=AF.Square), R=[ypre], W=[gz])
                                V(lambda e: e.tensor_scalar(out=gz.t[:, :bn], in0=gz.t[:, :bn], scalar1=0.044715, scalar2=1.0, op0=ALU.mult, op1=ALU.add), R=[gz], W=[gz])
                                V(lambda e: e.tensor_tensor(out=gz.t[:, :bn], in0=gz.t[:, :bn], in1=yv, op=ALU.mult), R=[gz, ypre], W=[gz])
                                A(lambda e: e.activation(out=gs.t[:, :bn], in_=gz.t[:, :bn], func=AF.Sigmoid, scale=1.5957691216057308), R=[gz], W=[gs])
                                V(lambda e: e.tensor_tensor(out=y5s.t[:, b0:b0 + bn], in0=gs.t[:, :bn], in1=yv, op=ALU.mult), R=[gs, ypre], W=[y5s])
                            DMA("sp", Y5A[gt * 128:(gt + 1) * 128, :], y5s.t[:], "y5s", R=[y5s])
                        M(lambda e: e.transpose(out=ptw.t[:64, :], in_=xl.t[:], identity=idf), R=xlb + [cst], W=[ptw])
                        A(lambda e: e.activation(out=gz.t[:64, 0:128], in_=ptw.t[:64, :], func=AF.Copy), R=[ptw], W=[gz])
                        DMA("sp", ps5[l], gz.t[:64, 0:128], "xnw", R=[gz])
                        S.barrier()
                    with ExitStack() as ps1:
                        T1 = mk(ps1, "s")
                        y5a = T1("y5a", [128, 8, NT], BF16)
                        DMA("sp", y5a.t[:], Y5A.rearrange("(k p) t -> p k t", p=128), "y5s", W=[y5a])
                        wgl = T1("wgl", [128, 8, 1024], BF16); sgz = T1("sgz", [128, 512]); stb = [T1(f"gstb{i}", [128, 512], BF16) for i in range(2)]
                        pz = [mk(ps1, "p")(f"pz{i}", [128, 512]) for i in range(2)]
                        DMA("pool", wgl.t[:], w_glu[l].rearrange("(k p) e -> p k e", p=128), "w0", W=[wgl])
                        ui = 0
                        for et in range(8):
                            for (b0, bn) in blocks:
                                i = ui % 2; ui += 1
                                for kc in range(8):
                                    M(lambda e: e.matmul(pz[i].t[:, :bn], lhsT=wgl.t[:, kc, et * 128:(et + 1) * 128], rhs=y5a.t[:, kc, b0:b0 + bn], start=(kc == 0), stop=(kc == 7)),
                                      R=[wgl, y5a], W=[pz[i]], inc=(kc == 7))
                                A(lambda e: e.activation(out=sgz.t[:, :bn], in_=pz[i].t[:, :bn], func=AF.Sigmoid, bias=vecT.t[:, V_BG + et:V_BG + et + 1]), R=[pz[i], vecT], W=[sgz])
                                V(lambda e: e.tensor_tensor(out=stb[i].t[:, :bn], in0=sgz.t[:, :bn], in1=y5a.t[:, et, b0:b0 + bn], op=ALU.mult), R=[sgz, y5a], W=[stb[i]])
                                DMA("sp", A5T[et * 128:(et + 1) * 128, b0:b0 + bn], stb[i].t[:, :bn], f"sb{i}", R=[stb[i]])
                        S.barrier()

                SC = 128.0 ** -0.5
                with ExitStack() as ps:
                    TT = mk(ps, "s"); PP = mk(ps, "p")
                    wgt = TT("wgt", [16, 512]); gnr = TT("gnr", [128, 256]); nb = TT("nb", [128, 16]); glr = TT("glr", [16, NT])
                    DMA("sp", wgt.t[:], w_gate[l], "g0", W=[wgt]); DMA("sp", gnr.t[:], gnorm[l], "g0", W=[gnr]); DMA("sp", glr.t[:], GLRT, "g0", W=[glr])
                    V(lambda e: e.tensor_scalar_mul(out=nb.t[:, 0:4], in0=vecT.t[:, V_GB:V_GB + 4], scalar1=-1.0), R=[vecT], W=[nb])
                    qT = TT("gqT", [128, NT]); kT = TT("gkT", [128, NT]); l1 = TT("gl1", [128, NT]); bc = TT("gbc", [128, NT]); enb = TT("genb", [128, NT])
                    eb = [TT(f"geb{h}", [128, NT]) for h in range(4)]
                    qp = [TT(f"gqp{h}", [128, NT], BF16) for h in range(4)]; kp = [TT(f"gkp{h}", [128, NT], BF16) for h in range(4)]
                    ksT = [TT(f"gksT{h}", [16, 128]) for h in range(4)]; qs = [TT(f"gqs{h}", [128, 16]) for h in range(4)]
                    St = [TT(f"gS{h}", [128, 256]) for h in range(4)]; Sb = [TT(f"gSb{h}", [128, 256], BF16) for h in range(4)]
                    kpt = [TT(f"gkpt{h}", [128, 128], BF16) for h in range(4)]; scm = [TT(f"gscm{h}", [128, 128], BF16) for h in range(4)]
                    ss = [TT(f"gss{h}", [128, 16]) for h in range(4)]; rs = [TT(f"grs{h}", [128, 16]) for h in range(4)]
                    on = [TT(f"gon{h}", [128, 256]) for h in range(4)]; og = [TT(f"gog{h}", [128, 256], BF16) for h in range(4)]; ost = [TT(f"gost{h}", [128, 2, 128], BF16) for h in range(4)]
                    vt = [TT(f"gv{i}", [128, 1024], BF16) for i in range(2)]; grt = [TT(f"ggr{i}", [128, 1024]) for i in range(2)]
                    Vbd = TT("gVbd", [16, 16, 256]); Qbd = TT("gQbd", [128, 16, 16]); S0 = [TT(f"gS0{i}", [128, 256]) for i in range(2)]; Sn = [TT(f"gSn{i}", [128, 256]) for i in range(2)]
                    pgt = PP("pgt", [128, 512]); ptw = PP("gptw", [16, 128])
                    pab = [PP(f"gpab{i}", [128, 2, 256]) for i in range(2)]; psc = [PP(f"gpsc{i}", [128, 256]) for i in range(2)]; pbf = [PP(f"gpbf{i}", [128, 3, 128], BF16) for i in range(2)]
                    pob = [pab[i].b for i in range(2)]; pkb = pob; ptkb = [pbf[i].b for i in range(2)]; ptob = ptkb

                    def gpost(n, t0, h, po_ap, po_buf, grt_ap, grt_tl, si):
                        V(lambda e: e.memset(ss[h].t[:], 0.0), W=[ss[h]])
                        A(lambda e: e.activation(out=on[h].t[:n, :], in_=po_ap, func=AF.Square, accum_out=ss[h].t[:n, 0:1]), R=[po_buf], W=[on[h], ss[h]])
                        A(lambda e: e.activation(out=rs[h].t[:n, 0:1], in_=ss[h].t[:n, 0:1], func=AF.Sqrt, scale=1.0 / 256, bias=EPS), R=[ss[h]], W=[rs[h]])
                        V(lambda e: e.reciprocal(out=rs[h].t[:n, 0:1], in_=rs[h].t[:n, 0:1]), R=[rs[h]], W=[rs[h]])
                        V(lambda e: e.scalar_tensor_tensor(out=on[h].t[:n, :], in0=po_ap, scalar=rs[h].t[:n, 0:1], in1=gnr.t[:n, :], op0=ALU.mult, op1=ALU.mult), R=[po_buf, rs[h], gnr], W=[on[h]])
                        G(lambda e: e.tensor_tensor(out=og[h].t[:n, :], in0=on[h].t[:n, :], in1=grt_ap, op=ALU.mult), R=[on[h], grt_tl], W=[og[h]])
                        for j in range(2):
                            M(lambda e: e.transpose(out=pbf[si].t[:, 1 + j, :n], in_=og[h].t[:n, j * 128:(j + 1) * 128], identity=idb.t[:n, :n]), R=[og[h], idb], W=[ptob[si]], inc=(j == 1))
                        A(lambda e: e.activation(out=ost[h].t[:, :, :n], in_=pbf[si].t[:, 1:3, :n], func=AF.Copy), R=[ptob[si]], W=[ost[h]])
                        for j in range(2):
                            DMA("sp", AGT[h * 256 + j * 128:h * 256 + (j + 1) * 128, t0:t0 + n], ost[h].t[:, j, :n], f"g1{h}", R=[ost[h]])

                    for h in range(4):
                        DMA("sp", qT.t[:], GQT[h * 128:(h + 1) * 128, :], "g2", W=[qT]); DMA("sp", kT.t[:], GKT[h * 128:(h + 1) * 128, :], "g2", W=[kT])
                        for (b0, bn) in blocks:
                            M(lambda e: e.matmul(pgt.t[:, :bn], lhsT=wgt.t[0:16, h * 128:(h + 1) * 128], rhs=glr.t[0:16, b0:b0 + bn], start=True, stop=True), R=[wgt, glr], W=[pgt])
                            A(lambda e: e.activation(out=l1.t[:, b0:b0 + bn], in_=pgt.t[:, :bn], func=AF.Exp, scale=-1.0, bias=nb.t[:, h:h + 1]), R=[pgt, nb], W=[l1])
                            A(lambda e: e.activation(out=l1.t[:, b0:b0 + bn], in_=l1.t[:, b0:b0 + bn], func=AF.Ln, bias=1.0), R=[l1], W=[l1])
                            if b0 < NP:
                                V(lambda e: e.tensor_tensor_scan(out=bc.t[:, b0:b0 + bn], data0=cst.t[:, C_D0:C_D0 + bn], data1=l1.t[:, b0:b0 + bn], initial=0.0, op0=ALU.mult, op1=ALU.add), R=[cst, l1], W=[bc])
                            else:
                                V(lambda e: e.tensor_copy(out=bc.t[:, b0:b0 + bn], in_=l1.t[:, b0:b0 + bn]), R=[l1], W=[bc])
                        A(lambda e: e.activation(out=eb[h].t[:], in_=bc.t[:], func=AF.Exp, scale=-1.0 / 16), R=[bc], W=[eb[h]])
                        A(lambda e: e.activation(out=enb.t[:], in_=bc.t[:], func=AF.Exp, scale=1.0 / 16), R=[bc], W=[enb])
                        V(lambda e: e.scalar_tensor_tensor(out=qp[h].t[:], in0=qT.t[:], scalar=SC, in1=eb[h].t[:], op0=ALU.mult, op1=ALU.mult), R=[qT, eb[h]], W=[qp[h]])
                        G(lambda e: e.tensor_tensor(out=kp[h].t[:], in0=kT.t[:], in1=enb.t[:], op=ALU.mult), R=[kT, enb], W=[kp[h]])
                        V(lambda e: e.memset(St[h].t[:], 0.0), W=[St[h]]); V(lambda e: e.memset(Sb[h].t[:], 0.0), W=[Sb[h]])
                        M(lambda e: e.transpose(out=ptw.t[:], in_=kT.t[:, NP:NT], identity=idf), R=[kT, cst], W=[ptw])
                        A(lambda e: e.activation(out=ksT[h].t[:], in_=ptw.t[:], func=AF.Copy), R=[ptw], W=[ksT[h]])
                        V(lambda e: e.tensor_scalar_mul(out=qs[h].t[:], in0=qT.t[:, NP:NT], scalar1=SC), R=[qT], W=[qs[h]])
                    for c in range(NPT):
                        t0 = c * 128; vi = c % 2
                        DMA("sp", vt[vi].t[:], GV[t0:t0 + 128, :], f"g3{vi}", W=[vt[vi]]); DMA("sp", grt[vi].t[:], GR[t0:t0 + 128, :], f"g3{vi}", W=[grt[vi]])
                        for pr in range(2):
                            hs_ = (2 * pr, 2 * pr + 1)
                            for h in hs_:
                                si = h % 2
                                M(lambda e: e.transpose(out=pbf[si].t[:, 0, :], in_=kp[h].t[:, t0:t0 + 128], identity=idb.t[:]), R=[kp[h], idb], W=[ptkb[si]])
                                M(lambda e: e.matmul(psc[si].t[:, 0:128], lhsT=kp[h].t[:, t0:t0 + 128], rhs=qp[h].t[:, t0:t0 + 128], start=True, stop=True), R=[kp[h], qp[h]], W=[psc[si]])
                            for h in hs_:
                                si = h % 2
                                A(lambda e: e.activation(out=kpt[h].t[:], in_=pbf[si].t[:, 0, :], func=AF.Copy), R=[ptkb[si]], W=[kpt[h]])
                                V(lambda e: e.tensor_tensor(out=scm[h].t[:], in0=psc[si].t[:, 0:128], in1=cst.t[:, C_TRI:C_TRI + 128], op=ALU.mult), R=[psc[si], cst], W=[scm[h]])
                            for h in hs_:
                                si = h % 2; vh = vt[vi].t[:, h * 256:(h + 1) * 256]
                                M(lambda e: e.matmul(pab[si].t[:, 0, :], lhsT=scm[h].t[:], rhs=vh, start=True, stop=False), R=[scm[h], vt[vi]], W=[pob[si]], inc=False)
                                M(lambda e: e.matmul(pab[si].t[:, 0, :], lhsT=qp[h].t[:, t0:t0 + 128], rhs=Sb[h].t[:], start=False, stop=True), R=[qp[h], Sb[h]], W=[pob[si]])
                                M(lambda e: e.matmul(pab[si].t[:, 1, :], lhsT=kpt[h].t[:], rhs=vh, start=True, stop=True), R=[kpt[h], vt[vi]], W=[pkb[si]])
                            for h in hs_:
                                si = h % 2
                                ecol = eb[h].t[:, t0 + 127:t0 + 128]
                                V(lambda e: e.tensor_scalar_mul(out=St[h].t[:], in0=St[h].t[:], scalar1=ecol), R=[St[h], eb[h]], W=[St[h]])
                                V(lambda e: e.scalar_tensor_tensor(out=St[h].t[:], in0=pab[si].t[:, 1, :], scalar=ecol, in1=St[h].t[:], op0=ALU.mult, op1=ALU.add), R=[pkb[si], eb[h], St[h]], W=[St[h]])
                                A(lambda e: e.activation(out=Sb[h].t[:], in_=St[h].t[:], func=AF.Copy), R=[St[h]], W=[Sb[h]])
                            for h in hs_:
                                si = h % 2
                                gpost(128, t0, h, pab[si].t[:, 0, :], pob[si], grt[vi].t[:, h * 256:(h + 1) * 256], grt[vi], si)
                    for h in range(4):
                        DMA("sp", pgla[l, h], St[h].t[:], "g4", R=[St[h]])
                    DMA("sp", vt[0].t[:NS, :], GV[NP:NT, :], "g30", W=[vt[0]]); DMA("sp", grt[0].t[:NS, :], GR[NP:NT, :], "g30", W=[grt[0]])
                    for h in range(4):
                        si = h % 2
                        V(lambda e: e.tensor_tensor(out=Vbd.t[:], in0=vt[0].t[:NS, h * 256:(h + 1) * 256].unsqueeze(1).to_broadcast([NS, NS, 256]),
                                                    in1=cst.t[:NS, C_ID:C_ID + NS].unsqueeze(2).to_broadcast([NS, NS, 256]), op=ALU.mult), R=[vt[0], cst], W=[Vbd])
                        V(lambda e: e.tensor_tensor(out=Qbd.t[:], in0=qs[h].t[:].unsqueeze(1).to_broadcast([128, NS, NS]),
                                                    in1=cst.t[:, C_E16:C_E16 + 256].rearrange("p (a b) -> p a b", a=NS), op=ALU.mult), R=[qs[h], cst], W=[Qbd])
                        for b_ in range(NS):
                            bi_ = b_ % 2
                            DMA("sp", S0[bi_].t[:], gla0[l, b_, h], f"g5{bi_}", W=[S0[bi_]])
                            M(lambda e: e.matmul(psc[bi_].t[:], lhsT=ksT[h].t[:], rhs=Vbd.t[:, b_, :], start=True, stop=True), R=[ksT[h], Vbd], W=[psc[bi_]])
                            V(lambda e: e.scalar_tensor_tensor(out=Sn[bi_].t[:], in0=S0[bi_].t[:], scalar=eb[h].t[:, NP + b_:NP + b_ + 1], in1=psc[bi_].t[:], op0=ALU.mult, op1=ALU.add),
                              R=[S0[bi_], eb[h], psc[bi_]], W=[Sn[bi_]])
                            DMA("sp", sgla[l, b_, h], Sn[bi_].t[:], f"g6{bi_}", R=[Sn[bi_]])
                            M(lambda e: e.matmul(pgt.t[:NS, 0:256], lhsT=Qbd.t[:, b_, :], rhs=Sn[bi_].t[:], start=(b_ == 0), stop=(b_ == NS - 1)), R=[Qbd, Sn[bi_]], W=[pgt])
                        gpost(NS, NP, h, pgt.t[:NS, 0:256], pgt, grt[0].t[:NS, h * 256:(h + 1) * 256], grt[0], si)
                    S.barrier()

                with ExitStack() as ps:
                    TT = mk(ps, "s"); PP = mk(ps, "p")
                    rq = TT("rq", [128, 1024]); rk = TT("rk", [128, 1024]); rv = TT("rv", [128, 1024], BF16); rg = TT("rg", [128, 1024]); rt = TT("rt", [128, 128])
                    qr = TT("qr", [128, 1024]); kr = TT("kr", [128, 1024]); ta = TT("rta", [128, 8, 64]); tb = TT("rtb", [128, 8, 64])
                    qb = TT("rqb", [128, 1024], BF16); kb = TT("rkb", [128, 1024], BF16); k2 = TT("rk2", [128, 1024], BF16)
                    qTt = TT("rqT", [128, 8, 128], BF16); kTt = TT("rkT", [128, 8, 128], BF16)
                    S4 = [TT(f"rS{g_}", [128, 4, 128]) for g_ in range(2)]; Sb4 = [TT(f"rSb{g_}", [128, 4, 128], BF16) for g_ in range(2)]
                    scm4 = TT("rscm", [128, 4, 128], BF16); pos4 = TT("rpos", [128, 4, 128]); o_ = TT("ro", [128, 8, 128]); sq = TT("rsq", [128, 8, 128])
                    st1 = TT("rst1", [128, 16]); st2 = TT("rst2", [128, 16]); og = TT("rog", [128, 1024], BF16); ost = TT("rost", [128, 8, 128], BF16)
                    Vbd = TT("rVbd", [16, 16, 128]); qs = TT("rqs", [128, 16]); Qbd = TT("rQbd", [128, 16, 16]); S0 = TT("rS0", [128, 128]); Sn = TT("rSn", [128, 128])
                    ptq = [PP(f"rptq{i}", [128, 4, 128], BF16) for i in range(2)]; psc = PP("rpsc", [128, 4, 128]); po = PP("rpo", [128, 4, 128]); pi_ = PP("rpi", [128, 4, 128]); pkv = PP("rpkv", [128, 4, 128])
                    ptw = PP("rptw", [128, 16])
                    for g_ in range(2):
                        V(lambda e: e.memset(S4[g_].t[:], 0.0), W=[S4[g_]]); V(lambda e: e.memset(Sb4[g_].t[:], 0.0), W=[Sb4[g_]])

                    def rotary(src, dst, n, scale):
                        s4 = src.t[:n, :].rearrange("p (h i two) -> p h i two", h=8, two=2); d4 = dst.t[:n, :].rearrange("p (h i two) -> p h i two", h=8, two=2)
                        cb_ = rt.t[:n, 0:64].unsqueeze(1).to_broadcast([n, 8, 64]); sb_ = rt.t[:n, 64:128].unsqueeze(1).to_broadcast([n, 8, 64])
                        V(lambda e: e.tensor_tensor(out=ta.t[:n], in0=s4[:, :, :, 0], in1=cb_, op=ALU.mult), R=[src, rt], W=[ta])
                        V(lambda e: e.tensor_tensor(out=tb.t[:n], in0=s4[:, :, :, 1], in1=sb_, op=ALU.mult), R=[src, rt], W=[tb])
                        V(lambda e: e.tensor_tensor(out=d4[:, :, :, 0], in0=ta.t[:n], in1=tb.t[:n], op=ALU.subtract), R=[ta, tb], W=[dst])
                        V(lambda e: e.tensor_tensor(out=ta.t[:n], in0=s4[:, :, :, 1], in1=cb_, op=ALU.mult), R=[src, rt], W=[ta])
                        V(lambda e: e.tensor_tensor(out=tb.t[:n], in0=s4[:, :, :, 0], in1=sb_, op=ALU.mult), R=[src, rt], W=[tb])
                        V(lambda e: e.tensor_tensor(out=d4[:, :, :, 1], in0=ta.t[:n], in1=tb.t[:n], op=ALU.add), R=[ta, tb], W=[dst])
                        if scale != 1.0:
                            V(lambda e: e.tensor_scalar_mul(out=dst.t[:n, :], in0=dst.t[:n, :], scalar1=scale), R=[dst], W=[dst])

                    for ti, (r0, n) in enumerate(tiles):
                        DMA("sp", rq.t[:n, :], RQ[r0:r0 + n, :], "r0", W=[rq]); DMA("sp", rk.t[:n, :], RK[r0:r0 + n, :], "r0", W=[rk])
                        DMA("sp", rv.t[:n, :], RV[r0:r0 + n, :], "r0", W=[rv]); DMA("sp", rg.t[:n, :], RG[r0:r0 + n, :], "r0", W=[rg]); DMA("sp", rt.t[:n, :], rot[r0:r0 + n, :], "r0", W=[rt])
                        rotary(rq, qr, n, 1.0); rotary(rk, kr, n, SC)
                        if ti < NPT:
                            G(lambda e: e.tensor_copy(out=qb.t[:], in_=qr.t[:]), R=[qr], W=[qb]); G(lambda e: e.tensor_copy(out=kb.t[:], in_=kr.t[:]), R=[kr], W=[kb])
                            V(lambda e: e.tensor_tensor(out=k2.t[:].rearrange("p (h k) -> p h k", h=8), in0=kr.t[:].rearrange("p (h k) -> p h k", h=8),
                                                        in1=cst.t[:, C_KDEC:C_KDEC + 8].unsqueeze(2).to_broadcast([128, 8, 128]), op=ALU.mult), R=[kr, cst], W=[k2])
                            for (srcb, dstT) in ((qb, qTt), (kb, kTt)):
                                for hq in range(2):
                                    for j in range(4):
                                        hh_ = hq * 4 + j
                                        M(lambda e: e.transpose(out=ptq[hq].t[:, j, :], in_=srcb.t[:, hh_ * 128:(hh_ + 1) * 128], identity=idb.t[:]), R=[srcb, idb], W=[ptq[hq]], inc=(j == 3))
                                    A(lambda e: e.activation(out=dstT.t[:, hq * 4:hq * 4 + 4, :], in_=ptq[hq].t[:], func=AF.Copy), R=[ptq[hq]], W=[dstT])
                            for g_ in range(2):
                                hsl = [(j, g_ * 4 + j, slice((g_ * 4 + j) * 128, (g_ * 4 + j + 1) * 128)) for j in range(4)]
                                for (j, h, hs) in hsl:
                                    M(lambda e: e.matmul(psc.t[:, j, :], lhsT=kTt.t[:, h, :], rhs=qTt.t[:, h, :], start=True, stop=True), R=[kTt, qTt], W=[psc], inc=(j == 3))
                                V(lambda e: e.tensor_tensor(out=scm4.t[:], in0=psc.t[:], in1=cst.t[:, C_DH + g_ * 512:C_DH + (g_ + 1) * 512].rearrange("p (a b) -> p a b", a=4), op=ALU.mult), R=[psc, cst], W=[scm4])
                                for (j, h, hs) in hsl:
                                    M(lambda e: e.matmul(po.t[:, j, :], lhsT=scm4.t[:, j, :], rhs=rv.t[:, hs], start=True, stop=True), R=[scm4, rv], W=[po], inc=(j == 3))
                                for (j, h, hs) in hsl:
                                    M(lambda e: e.matmul(pi_.t[:, j, :], lhsT=qTt.t[:, h, :], rhs=Sb4[g_].t[:, j, :], start=True, stop=True), R=[qTt, Sb4[g_]], W=[pi_], inc=(j == 3))
                                for (j, h, hs) in hsl:
                                    M(lambda e: e.matmul(pkv.t[:, j, :], lhsT=k2.t[:, hs], rhs=rv.t[:, hs], start=True, stop=True), R=[k2, rv], W=[pkv], inc=(j == 3))
                                V(lambda e: e.tensor_tensor(out=S4[g_].t[:], in0=S4[g_].t[:], in1=cst.t[:, C_G128 + g_ * 4:C_G128 + g_ * 4 + 4].unsqueeze(2).to_broadcast([128, 4, 128]), op=ALU.mult), R=[S4[g_], cst], W=[S4[g_]])
                                V(lambda e: e.tensor_tensor(out=S4[g_].t[:], in0=S4[g_].t[:], in1=pkv.t[:], op=ALU.add), R=[S4[g_], pkv], W=[S4[g_]])
                                A(lambda e: e.activation(out=Sb4[g_].t[:], in_=S4[g_].t[:], func=AF.Copy), R=[S4[g_]], W=[Sb4[g_]])
                                A(lambda e: e.activation(out=pos4.t[:], in_=po.t[:], func=AF.Copy), R=[po], W=[pos4])
                                V(lambda e: e.tensor_tensor(out=o_.t[:, g_ * 4:g_ * 4 + 4, :], in0=pi_.t[:], in1=cst.t[:, C_GPOW + g_ * 4:C_GPOW + g_ * 4 + 4].unsqueeze(2).to_broadcast([128, 4, 128]), op=ALU.mult), R=[pi_, cst], W=[o_])
                                V(lambda e: e.tensor_tensor(out=o_.t[:, g_ * 4:g_ * 4 + 4, :], in0=o_.t[:, g_ * 4:g_ * 4 + 4, :], in1=pos4.t[:], op=ALU.add), R=[o_, pos4], W=[o_])
                        else:
                            for h in range(8):
                                hs = slice(h * 128, (h + 1) * 128)
                                M(lambda e: e.transpose(out=ptw.t[:], in_=qr.t[:NS, hs], identity=idf[:NS, :NS]), R=[qr, cst], W=[ptw])
                                V(lambda e: e.tensor_copy(out=qs.t[:], in_=ptw.t[:]), R=[ptw], W=[qs])
                                V(lambda e: e.tensor_tensor(out=Qbd.t[:], in0=qs.t[:].unsqueeze(1).to_broadcast([128, NS, NS]),
                                                            in1=cst.t[:, C_E16:C_E16 + 256].rearrange("p (a b) -> p a b", a=NS), op=ALU.mult), R=[qs, cst], W=[Qbd])
                                V(lambda e: e.tensor_tensor(out=Vbd.t[:], in0=rv.t[:NS, hs].unsqueeze(1).to_broadcast([NS, NS, 128]),
                                                            in1=cst.t[:NS, C_ID:C_ID + NS].unsqueeze(2).to_broadcast([NS, NS, 128]), op=ALU.mult), R=[rv, cst], W=[Vbd])
                                for b in range(NS):
                                    DMA("sp", S0.t[:], ret0[l, b, h], "r1", W=[S0])
                                    M(lambda e: e.matmul(pkv.t[:, 0, :], lhsT=kr.t[:NS, hs], rhs=Vbd.t[:, b, :], start=True, stop=True), R=[kr, Vbd], W=[pkv])
                                    V(lambda e: e.scalar_tensor_tensor(out=Sn.t[:], in0=S0.t[:], scalar=cst.t[:, C_G1 + h:C_G1 + h + 1], in1=pkv.t[:, 0, :], op0=ALU.mult, op1=ALU.add), R=[S0, cst, pkv], W=[Sn])
                                    DMA("sp", sret[l, b, h], Sn.t[:], "r2", R=[Sn])
                                    M(lambda e: e.matmul(po.t[:NS, 0, :], lhsT=Qbd.t[:, b, :], rhs=Sn.t[:], start=(b == 0), stop=(b == NS - 1)), R=[Qbd, Sn], W=[po])
                                V(lambda e: e.tensor_copy(out=o_.t[:NS, h, :], in_=po.t[:NS, 0, :]), R=[po], W=[o_])
                        V(lambda e: e.reduce_sum(out=st1.t[:n, 0:8], in_=o_.t[:n], axis=AX.X), R=[o_], W=[st1])
                        G(lambda e: e.tensor_tensor(out=sq.t[:n], in0=o_.t[:n], in1=o_.t[:n], op=ALU.mult), R=[o_], W=[sq])
                        V(lambda e: e.reduce_sum(out=st2.t[:n, 0:8], in_=sq.t[:n], axis=AX.X), R=[sq], W=[st2])
                        V(lambda e: e.tensor_scalar_mul(out=st1.t[:n, 0:8], in0=st1.t[:n, 0:8], scalar1=1.0 / 128), R=[st1], W=[st1])
                        V(lambda e: e.tensor_tensor(out=st1.t[:n, 8:16], in0=st1.t[:n, 0:8], in1=st1.t[:n, 0:8], op=ALU.mult), R=[st1], W=[st1])
                        V(lambda e: e.scalar_tensor_tensor(out=st2.t[:n, 0:8], in0=st2.t[:n, 0:8], scalar=1.0 / 128, in1=st1.t[:n, 8:16], op0=ALU.mult, op1=ALU.subtract), R=[st2, st1], W=[st2])
                        A(lambda e: e.activation(out=st2.t[:n, 0:8], in_=st2.t[:n, 0:8], func=AF.Sqrt, bias=EPS), R=[st2], W=[st2])
                        V(lambda e: e.reciprocal(out=st2.t[:n, 0:8], in_=st2.t[:n, 0:8]), R=[st2], W=[st2])
                        V(lambda e: e.tensor_tensor(out=o_.t[:n], in0=o_.t[:n], in1=st1.t[:n, 0:8].unsqueeze(2).to_broadcast([n, 8, 128]), op=ALU.subtract), R=[o_, st1], W=[o_])
                        V(lambda e: e.tensor_tensor(out=o_.t[:n], in0=o_.t[:n], in1=st2.t[:n, 0:8].unsqueeze(2).to_broadcast([n, 8, 128]), op=ALU.mult), R=[o_, st2], W=[o_])
                        G(lambda e: e.tensor_tensor(out=og.t[:n, :], in0=o_.t[:n].rearrange("p h k -> p (h k)"), in1=rg.t[:n, :], op=ALU.mult), R=[o_, rg], W=[og])
                        for hq in range(2):
                            for j in range(4):
                                hh_ = hq * 4 + j
                                M(lambda e: e.transpose(out=ptq[hq].t[:, j, :n], in_=og.t[:n, hh_ * 128:(hh_ + 1) * 128], identity=idb.t[:n, :n]), R=[og, idb], W=[ptq[hq]], inc=(j == 3))
                            A(lambda e: e.activation(out=ost.t[:, hq * 4:hq * 4 + 4, :n], in_=ptq[hq].t[:, :, :n], func=AF.Copy), R=[ptq[hq]], W=[ost])
                        DMA("sp", ART[:, r0:r0 + n].rearrange("(k p) t -> p k t", p=128), ost.t[:, :, :n], "r3", R=[ost])
                    for h in range(8):
                        DMA("sp", pret[l, h], S4[h // 4].t[:, h % 4, :], "r4", R=[S4[h // 4]])
                    S.barrier()

                with ExitStack() as ps:
                    TT = mk(ps, "s"); PP = mk(ps, "p")
                    mT = TT("mT", [128, 16, NT], BF16)
                    mTb = [Buf(f"mT{d_}") for d_ in range(16)]
                    with ExitStack() as ps1:
                        T1 = mk(ps1, "s"); P1 = mk(ps1, "p")
                        at = [T1(f"at{b}", [128, 8, NT], BF16) for b in range(3)]
                        for b, src in enumerate((A5T, AGT, ART)):
                            DMA("sp", at[b].t[:], src.rearrange("(k p) t -> p k t", p=128), f"at{b}", W=[at[b]])
                        wb = [[T1(f"wb{i}_{b}", [128, 8, 128], BF16) for b in range(3)] for i in range(2)]
                        gt = [T1(f"gt{i}", [128, 3, 512], BF16) for i in range(2)]
                        pmb = [[P1(f"pb{i}_{b}", [128, 512]) for b in range(3)] for i in range(2)]
                        t1 = [T1(f"t1_{i}", [128, 512]) for i in range(2)]; t2 = [T1(f"t2_{i}", [128, 512]) for i in range(2)]
                        ui = 0
                        for dt_ in range(16):
                            w = wb[dt_ % 2]
                            for b, wsrc in enumerate((w_s5o, w_glao, w_reto)):
                                DMA("pool", w[b].t[:], wsrc[l][:, dt_ * 128:(dt_ + 1) * 128].rearrange("(k p) e -> p k e", p=128), f"wb{dt_ % 2}", W=[w[b]])
                            for (b0, bn) in blocks:
                                i = ui % 2; ui += 1
                                for b in range(3):
                                    DMA("sp", gt[i].t[:, b, :bn], MGT[b * 2048 + dt_ * 128:b * 2048 + (dt_ + 1) * 128, b0:b0 + bn], f"gt{i}", W=[gt[i]])
                                    for kc in range(8):
                                        M(lambda e: e.matmul(pmb[i][b].t[:, :bn], lhsT=w[b].t[:, kc, :], rhs=at[b].t[:, kc, b0:b0 + bn], start=(kc == 0), stop=(kc == 7)),
                                          R=[w[b], at[b]], W=[pmb[i][b]], inc=(kc == 7))
                                V(lambda e: e.tensor_tensor(out=t1[i].t[:, :bn], in0=pmb[i][0].t[:, :bn], in1=gt[i].t[:, 0, :bn], op=ALU.mult), R=[pmb[i][0], gt[i]], W=[t1[i]])
                                V(lambda e: e.tensor_tensor(out=t2[i].t[:, :bn], in0=pmb[i][1].t[:, :bn], in1=gt[i].t[:, 1, :bn], op=ALU.mult), R=[pmb[i][1], gt[i]], W=[t2[i]])
                                G(lambda e: e.tensor_tensor(out=t1[i].t[:, :bn], in0=t1[i].t[:, :bn], in1=t2[i].t[:, :bn], op=ALU.add), R=[t1[i], t2[i]], W=[t1[i]])
                                V(lambda e: e.tensor_tensor(out=t2[i].t[:, :bn], in0=pmb[i][2].t[:, :bn], in1=gt[i].t[:, 2, :bn], op=ALU.mult), R=[pmb[i][2], gt[i]], W=[t2[i]])
                                V(lambda e: e.tensor_tensor(out=mT.t[:, dt_, b0:b0 + bn], in0=t1[i].t[:, :bn], in1=t2[i].t[:, :bn], op=ALU.add), R=[t1[i], t2[i]], W=[mTb[dt_]])
                        S.barrier()
                    wsl = [TT(f"w{i}", [128, 16, 512], BF16) for i in range(2)]
                    pm = [PP(f"pm{i}", [128, 512]) for i in range(4)]
                    xr = [TT(f"xr{i}", [128, 512]) for i in range(3)]
                    stf = [TT(f"stf{i}", [128, 512]) for i in range(3)]
                    ui = 0
                    for eb in range(4):
                        w = wsl[eb % 2]
                        DMA("pool", w.t[:], w_mix[l][:, eb * 512:(eb + 1) * 512].rearrange("(k p) e -> p k e", p=128), f"w{eb % 2}", W=[w])
                        for ti, (r0, n) in enumerate(tiles):
                            p = pm[ui % 4]; x_ = xr[ui % 3]; st = stf[ui % 3]; ci = ui % 3; ui += 1
                            DMA("sp", x_.t[:n, :], xrows(xsrc, ti)[:, eb * 512:(eb + 1) * 512], f"xr{ci}", W=[x_])
                            for kc in range(16):
                                M(lambda e: e.matmul(p.t[:n, :], lhsT=mT.t[:, kc, r0:r0 + n], rhs=w.t[:, kc, :], start=(kc == 0), stop=(kc == 15)),
                                  R=mTb + [w], W=[p], inc=(kc == 15))
                            V(lambda e: e.tensor_tensor(out=st.t[:n, :], in0=p.t[:n, :], in1=x_.t[:n, :], op=ALU.add), R=[p, x_], W=[st])
                            DMA("sp", XMID[r0:r0 + n, eb * 512:(eb + 1) * 512], st.t[:n, :], f"sf{ci}", R=[st])
                    S.barrier()

                with ExitStack() as ps:
                    TT = mk(ps, "s"); PP = mk(ps, "p")
                    hT = TT("hT2", [128, 16, NT], BF16)
                    hTb = [Buf(f"hT2{ti}") for ti in range(len(tiles))]
                    with ExitStack() as ps1:
                        norm_T(mk(ps1, "s"), mk(ps1, "p"), XMID, V_NF, hT, hTb, vecT, "n2")
                        S.barrier()
                    wg = [TT(f"wg{i}", [128, 16, 512], BF16) for i in range(2)]; wu = [TT(f"wu{i}", [128, 16, 512], BF16) for i in range(2)]
                    pg = [PP(f"pg{i}", [128, 512]) for i in range(4)]; pu = [PP(f"pu{i}", [128, 512]) for i in range(4)]
                    sg = [TT(f"sg{i}", [128, 512]) for i in range(4)]; stb = [TT(f"stb{i}", [128, 512], BF16) for i in range(6)]
                    ui = 0
                    for fb in range(DFF // 512):
                        i2 = fb % 2
                        DMA("pool", wg[i2].t[:], w_fg[l][:, fb * 512:(fb + 1) * 512].rearrange("(k p) e -> p k e", p=128), f"wg{i2}", W=[wg[i2]])
                        DMA("pool", wu[i2].t[:], w_fu[l][:, fb * 512:(fb + 1) * 512].rearrange("(k p) e -> p k e", p=128), f"wu{i2}", W=[wu[i2]])
                        for j in range(4):
                            ft = fb * 4 + j
                            for (b0, bn) in blocks:
                                i = ui % 4; si = ui % 6; ui += 1
                                bt = [hTb[t_] for t_ in blk_tiles(b0, bn)]
                                for kc in range(16):
                                    M(lambda e: e.matmul(pg[i].t[:, :bn], lhsT=wg[i2].t[:, kc, j * 128:(j + 1) * 128], rhs=hT.t[:, kc, b0:b0 + bn], start=(kc == 0), stop=(kc == 15)),
                                      R=bt + [wg[i2]], W=[pg[i]], inc=(kc == 15))
                                for kc in range(16):
                                    M(lambda e: e.matmul(pu[i].t[:, :bn], lhsT=wu[i2].t[:, kc, j * 128:(j + 1) * 128], rhs=hT.t[:, kc, b0:b0 + bn], start=(kc == 0), stop=(kc == 15)),
                                      R=bt + [wu[i2]], W=[pu[i]], inc=(kc == 15))
                                A(lambda e: e.activation(out=sg[i].t[:, :bn], in_=pg[i].t[:, :bn], func=AF.Silu), R=[pg[i]], W=[sg[i]])
                                V(lambda e: e.tensor_tensor(out=stb[si].t[:, :bn], in0=pu[i].t[:, :bn], in1=sg[i].t[:, :bn], op=ALU.mult), R=[pu[i], sg[i]], W=[stb[si]])
                                bts = blk_tiles(b0, bn)
                                if bn % 128 == 0:
                                    DMA("sp" if ui % 2 else "act", FFTL.rearrange("n p k t -> p n k t")[:, bts[0]:bts[0] + len(bts), ft, :],
                                        stb[si].t[:, :bn].rearrange("p (n t) -> p n t", t=128), f"sb{si}", R=[stb[si]])
                                else:
                                    DMA("sp", FFTL[bts[0], :, ft, :bn], stb[si].t[:, :bn], f"sb{si}", R=[stb[si]])
                    S.barrier()
                with ExitStack() as ps:
                    TT = mk(ps, "s"); PP = mk(ps, "p")
                    NF = DFF // 128
                    wd = TT("wd", [128, NF, 1024], BF16)
                    ff = [TT(f"ff{i}", [128, NF, 128], BF16) for i in range(3)]
                    pm = [PP(f"pm{i}", [128, 1024]) for i in range(3)]
                    xr = [TT(f"xr{i}", [128, 1024]) for i in range(2)]; stf = [TT(f"stf{i}", [128, 1024]) for i in range(2)]
                    ui = 0
                    for db in range(2):
                        for hf in range(2):
                            DMA("pool", wd.t[:, :, hf * 512:(hf + 1) * 512], w_fd[l][:, db * 1024 + hf * 512:db * 1024 + (hf + 1) * 512].rearrange("(k p) e -> p k e", p=128), "w0", W=[wd])
                        for ti, (r0, n) in enumerate(tiles):
                            p = pm[ui % 3]; x_ = xr[ui % 2]; st = stf[ui % 2]; ci = ui % 2; f_ = ff[ui % 3]; fi_ = ui % 3; ui += 1
                            DMA("sp", f_.t[:, :, :n], FFTL[ti][:, :, :n], f"ff{fi_}", W=[f_])
                            DMA("act", x_.t[:n, :], XMID[r0:r0 + n, db * 1024:(db + 1) * 1024], f"xr{ci}", W=[x_])
                            for hf in range(2):
                                for kc in range(NF):
                                    M(lambda e: e.matmul(p.t[:n, hf * 512:(hf + 1) * 512], lhsT=f_.t[:, kc, :n], rhs=wd.t[:, kc, hf * 512:(hf + 1) * 512], start=(kc == 0), stop=(kc == NF - 1)),
                                      R=[f_, wd], W=[p], inc=(kc == NF - 1 and hf == 1))
                            V(lambda e: e.tensor_tensor(out=st.t[:n, :], in0=p.t[:n, :], in1=x_.t[:n, :], op=ALU.add), R=[p, x_], W=[st])
                            DMA("sp", X1[r0:r0 + n, db * 1024:(db + 1) * 1024], st.t[:n, :], f"sf{ci}", R=[st])
                    S.barrier()
        with ExitStack() as ps:
            TT = mk(ps, "s")
            nf = TT("nf", [128, D]); junk = TT("fjunk", [128, D], BF16)
            DMA("sp", nf.t[:], nfin, "nf", W=[nf])
            xt = [TT(f"fxt{i}", [128, D]) for i in range(2)]; yo = [TT(f"fyo{i}", [128, D]) for i in range(2)]
            ss = [TT(f"fss{i}", [128, 16]) for i in range(2)]; rs = [TT(f"frs{i}", [128, 16]) for i in range(2)]
            for ti, (r0, n) in enumerate(tiles):
                s_ = ti % 2
                DMA("sp", xt[s_].t[:n, :], X1[r0:r0 + n, :], f"xr{s_}", W=[xt[s_]])
                V(lambda e: e.memset(ss[s_].t[:], 0.0), W=[ss[s_]])
                A(lambda e: e.activation(out=junk.t[:n, :], in_=xt[s_].t[:n, :], func=AF.Square, accum_out=ss[s_].t[:n, 0:1]), R=[xt[s_]], W=[junk, ss[s_]])
                A(lambda e: e.activation(out=rs[s_].t[:n, 0:1], in_=ss[s_].t[:n, 0:1], func=AF.Sqrt, scale=1.0 / D, bias=EPS), R=[ss[s_]], W=[rs[s_]])
                V(lambda e: e.reciprocal(out=rs[s_].t[:n, 0:1], in_=rs[s_].t[:n, 0:1]), R=[rs[s_]], W=[rs[s_]])
                V(lambda e: e.scalar_tensor_tensor(out=yo[s_].t[:n, :], in0=xt[s_].t[:n, :], scalar=rs[s_].t[:n, 0:1], in1=nf.t[:n, :], op0=ALU.mult, op1=ALU.mult),
                  R=[xt[s_], rs[s_], nf], W=[yo[s_]])
                DMA("sp", (yp[r0:r0 + n, :] if ti < NPT else ys[:, :]), yo[s_].t[:n, :], f"sf{s_}", R=[yo[s_]])
            S.barrier()
        S.barrier()
    return nc


def make_consts():
    c = np.zeros((128, C_N), np.float32)
    c[:, C_ID:C_ID + 128] = np.eye(128)
    for p in range(64):
        c[64 + p, C_J + p] = -1.0
        c[p, C_J + 64 + p] = 1.0
    s = np.arange(128)[:, None]; t = np.arange(128)[None, :]
    c[:, C_TRI:C_TRI + 128] = (t >= s)
    for gg in range(8):
        c[gg * 16:(gg + 1) * 16, C_GM + gg] = 1.0
    c[:64, C_SGN] = -1.0; c[64:, C_SGN] = 1.0
    for b in range(16):
        c[:, C_E16 + b * 16 + b] = 1.0
    c[:, C_D0:C_D0 + 512] = 1.0
    c[:, C_D0:C_D0 + 512:128] = 0.0
    lg = np.log1p(-np.power(2.0, -5.0 - np.arange(8, dtype=np.float64)))
    for h in range(8):
        c[:, C_DH + h * 128:C_DH + (h + 1) * 128] = np.where(t >= s, np.exp(lg[h] * (t - s)), 0.0)
        c[:, C_GPOW + h] = np.exp(lg[h] * (np.arange(128) + 1))
        c[:, C_KDEC + h] = np.exp(lg[h] * (127 - np.arange(128)))
        c[:, C_G128 + h] = np.exp(lg[h] * 128)
        c[:, C_G1 + h] = np.exp(lg[h])
    return c


def make_rot(NP):
    inv = (np.float32(1.0) / (np.float32(10000.0) ** np.linspace(0.0, 1.0, 64, dtype=np.float32))).astype(np.float32)
    pos = np.concatenate([np.arange(NP, dtype=np.float32), np.full(NS, 16384.0, np.float32)])
    ang = (pos[:, None] * inv[None, :]).astype(np.float32)
    return np.concatenate([np.cos(ang), np.sin(ang)], axis=1).astype(np.float32)


def prep_shared(inp, DEPTH):
    f = lambda a: np.ascontiguousarray(np.asarray(a, dtype=np.float32))
    a_re = f(inp["s5_a_re"]); a_im = f(inp["s5_a_im"]); ldt = f(inp["s5_log_dt"])
    s5dup = np.zeros((DEPTH, 3, 128, 64), np.float32); s5row = np.zeros((DEPTH, 3, NS, 4096), np.float32)
    vecs = np.zeros((DEPTH, V_N, 128), np.float32)
    for l in range(DEPTH):
        s5dup[l, 0] = np.concatenate([a_re[l].T, a_re[l].T], 0)
        s5dup[l, 1] = np.concatenate([a_im[l].T, a_im[l].T], 0)
        s5dup[l, 2] = np.broadcast_to(ldt[l][None, :], (128, 64))
        s5row[l, 0] = np.broadcast_to(a_re[l].reshape(1, 4096), (NS, 4096))
        s5row[l, 1] = np.broadcast_to(a_im[l].reshape(1, 4096), (NS, 4096))
        s5row[l, 2] = np.broadcast_to(np.repeat(ldt[l], 64)[None, :], (NS, 4096))
        vecs[l, V_NM:V_NM + 16] = f(inp["norm_mix"])[l].reshape(16, 128)
        vecs[l, V_NF:V_NF + 16] = f(inp["norm_ffn"])[l].reshape(16, 128)
        vecs[l, V_SD:V_SD + 8] = f(inp["s5_d"])[l].reshape(8, 128)
        vecs[l, V_BG:V_BG + 8] = f(inp["s5_b_glu"])[l].reshape(8, 128)
        vecs[l, V_GB:V_GB + 4] = f(inp["gla_b_gate"])[l].reshape(4, 128)
    sh = {
        "w_in": f(inp["w_in"])[:DEPTH], "s5dup": s5dup, "s5row": s5row,
        "s5b_re": f(inp["s5_b_re"])[:DEPTH], "s5b_im": f(inp["s5_b_im"])[:DEPTH],
        "s5c_re": f(inp["s5_c_re"])[:DEPTH].reshape(DEPTH, 1024, 64), "s5c_im": f(inp["s5_c_im"])[:DEPTH].reshape(DEPTH, 1024, 64),
        "w_glu": f(inp["s5_w_glu"])[:DEPTH], "w_s5o": f(inp["s5_w_out"])[:DEPTH], "w_gate": f(inp["gla_w_gate"])[:DEPTH],
        "gnorm": np.ascontiguousarray(np.broadcast_to(f(inp["gla_norm"])[:DEPTH, None, :], (DEPTH, 128, 256))),
        "w_glao": f(inp["gla_w_out"])[:DEPTH], "w_reto": f(inp["ret_w_out"])[:DEPTH], "w_mix": f(inp["w_mix_out"])[:DEPTH],
        "w_fg": f(inp["w_ffn_gate"])[:DEPTH], "w_fu": f(inp["w_ffn_up"])[:DEPTH], "w_fd": f(inp["w_ffn_down"])[:DEPTH],
        "vecs": vecs, "nfin": np.ascontiguousarray(np.broadcast_to(f(inp["norm_final"])[None, :], (128, D))),
        "consts": make_consts(),
    }
    return sh


def prep_core(inp, sh, seq, srow0, NPT, DEPTH):
    f = lambda a: np.ascontiguousarray(np.asarray(a, dtype=np.float32))
    NP = NPT * 128
    m = dict(sh)
    m["xp"] = f(inp["x_prompt"][seq, :NP])
    m["xs"] = f(inp["x_sample"][srow0:srow0 + NS, 0])
    m["s5r0"] = f(inp["state_s5_re"][:DEPTH, srow0:srow0 + NS]).reshape(DEPTH, NS, 4096)
    m["s5i0"] = f(inp["state_s5_im"][:DEPTH, srow0:srow0 + NS]).reshape(DEPTH, NS, 4096)
    m["gla0"] = f(inp["state_gla"][:DEPTH, srow0:srow0 + NS])
    m["ret0"] = f(inp["state_ret"][:DEPTH, srow0:srow0 + NS])
    m["rot"] = make_rot(NP)
    return m


_NC_CACHE = {}


def kernel(**inputs):
    NPT, DEPTH, NCORES = 16, 2, 8
    if "nc" not in _NC_CACHE:
        _NC_CACHE["nc"] = build(NPT, DEPTH)
    nc = _NC_CACHE["nc"]
    sh = prep_shared(inputs, DEPTH)
    in_maps = [prep_core(inputs, sh, c % 4, c * NS, NPT, DEPTH) for c in range(NCORES)]
    res = run_bass_kernel_spmd(nc, in_maps, core_ids=list(range(NCORES)))
    r = res.results
    g = lambda c, k: np.asarray(r[c][k], dtype=np.float32)
    y_prompt = np.stack([g(c, "yp") for c in range(4)], 0)
    y_sample = np.concatenate([g(c, "ys") for c in range(NCORES)], 0)[:, None, :]
    ps5 = np.stack([g(c, "ps5") for c in range(4)], 1)
    pgla = np.stack([g(c, "pgla") for c in range(4)], 1)
    pret = np.stack([g(c, "pret") for c in range(4)], 1)
    ss5 = np.concatenate([g(c, "ss5") for c in range(NCORES)], 1)
    sgla = np.concatenate([g(c, "sgla") for c in range(NCORES)], 1)
    sret = np.concatenate([g(c, "sret") for c in range(NCORES)], 1)
    return (y_prompt, y_sample,
            np.ascontiguousarray(ps5[..., :64]), np.ascontiguousarray(ps5[..., 64:]), pgla, pret,
            np.ascontiguousarray(ss5[..., :64]), np.ascontiguousarray(ss5[..., 64:]), sgla, sret)
```

```python
from contextlib import ExitStack
import numpy as np
import concourse.bass as bass
import concourse.mybir as mybir
from concourse.bass_utils import run_bass_kernel_spmd

F32 = mybir.dt.float32
BF16 = mybir.dt.bfloat16
AF = mybir.ActivationFunctionType
ALU = mybir.AluOpType
AX = mybir.AxisListType


class Buf:
    __slots__ = ("name", "w", "r")

    def __init__(self, name=""):
        self.name = name
        self.w = None
        self.r = {}


class Sched:
    ENG = ("pe", "act", "dve", "pool", "sp")
    NCHAN = 90

    def __init__(self, nc, es):
        self.nc = nc
        self.es = es
        self.eng = {"pe": nc.tensor, "act": nc.scalar, "dve": nc.vector, "pool": nc.gpsimd, "sp": nc.sync}
        self.sems = {}
        self.cnt = {}
        self.seen = {e: {} for e in self.ENG}
        self.pend = {e: ([], []) for e in self.ENG}
        self.pool = []
        for e in self.ENG:
            h = self.es.enter_context(self.nc.semaphore("E_" + e))
            self.sems["E_" + e] = h
            self.cnt["E_" + e] = 0
            nc.sync.sem_clear(h)
        for i in range(self.NCHAN):
            h = self.es.enter_context(self.nc.semaphore(f"DS{i}"))
            nc.sync.sem_clear(h)
            self.pool.append(h)
        nc.all_engine_barrier()

    def chan(self, name):
        key = "D_" + name
        self.sems[key] = self.pool.pop()
        self.cnt[key] = 0
        return key

    def _wait(self, e, tok):
        key, val = tok
        if key == "E_pe" and e == "pe":
            return
        if key.startswith("D_"):
            val = max(val, self.cnt[key])
        if self.seen[e].get(key, 0) >= val:
            return
        self.eng[e].wait_ge(self.sems[key], val)
        self.seen[e][key] = val

    def _deps(self, e, R, W):
        for b in R:
            if b.w is not None:
                self._wait(e, b.w)
        for b in W:
            if b.w is not None:
                self._wait(e, b.w)
            for k, v in b.r.items():
                self._wait(e, (k, v))

    def _commit(self, tok, R, W):
        k, v = tok
        for b in R:
            if b.r.get(k, 0) < v:
                b.r[k] = v
        for b in W:
            b.w = tok
            b.r = {}

    def op(self, e, fn, R=(), W=(), inc=True):
        self._deps(e, R, W)
        inst = fn(self.eng[e])
        pr, pw = self.pend[e]
        pr.extend(R)
        pw.extend(W)
        if not inc:
            return None
        own = "E_" + e
        self.cnt[own] += 1
        inst.then_inc(self.sems[own], 1)
        tok = (own, self.cnt[own])
        self._commit(tok, pr, pw)
        self.pend[e] = ([], [])
        return tok

    def dma(self, q, out, in_, ch, R=(), W=()):
        self._deps(q, R, W)
        inst = self.eng[q].dma_start(out=out, in_=in_)
        self.cnt[ch] += 16
        inst.then_inc(self.sems[ch], 16)
        tok = (ch, self.cnt[ch])
        self._commit(tok, R, W)
        return tok

    def barrier(self):
        toks = [(k, v) for k, v in self.cnt.items() if v > 0]
        for e in self.ENG:
            for t in toks:
                self._wait(e, t)


D = 2048
WIN = 14352
DFF = 5632
NS = 16
EPS = 1e-6
DBG = False
C_ID, C_J, C_TRI, C_GM, C_SGN, C_E16, C_D0, C_DH, C_GPOW, C_KDEC, C_G128, C_G1, C_N = (
    0, 128, 256, 384, 392, 400, 656, 1168, 2192, 2200, 2208, 2216, 2224)
V_NM, V_NF, V_SD, V_BG, V_GB, V_N = 0, 16, 32, 40, 48, 52


class Tl:
    __slots__ = ("t", "b")

    def __init__(self, t, name):
        self.t = t
        self.b = Buf(name)


def build(NPT, DEPTH):
    NP = NPT * 128
    NT = NP + NS
    tiles = [(i * 128, 128) for i in range(NPT)] + [(NP, NS)]
    blocks = [(b0, min(512, NP - b0)) for b0 in range(0, NP, 512)] + [(NP, NS)]
    nc = bass.Bass("TRN2", target_bir_lowering=False)

    def din(name, shape, dt=F32):
        return nc.dram_tensor(name, list(shape), dt, kind="ExternalInput").ap()

    def dout(name, shape, dt=F32):
        return nc.dram_tensor(name, list(shape), dt, kind="ExternalOutput").ap()

    def dscr(name, shape, dt=F32):
        if DBG:
            return nc.dram_tensor(name, list(shape), dt, kind="ExternalOutput").ap()
        return nc.dram_tensor(name, list(shape), dt).ap()

    xp = din("xp", [NP, D]); xs = din("xs", [NS, D])
    s5r0 = din("s5r0", [DEPTH, NS, 4096]); s5i0 = din("s5i0", [DEPTH, NS, 4096])
    gla0 = din("gla0", [DEPTH, NS, 4, 128, 256]); ret0 = din("ret0", [DEPTH, NS, 8, 128, 128])
    w_in = din("w_in", [DEPTH, D, WIN])
    s5dup = din("s5dup", [DEPTH, 3, 128, 64]); s5row = din("s5row", [DEPTH, 3, NS, 4096])
    s5b_re = din("s5b_re", [DEPTH, 64, 64, 16]); s5b_im = din("s5b_im", [DEPTH, 64, 64, 16])
    s5c_re = din("s5c_re", [DEPTH, 1024, 64]); s5c_im = din("s5c_im", [DEPTH, 1024, 64])
    w_glu = din("w_glu", [DEPTH, 1024, 1024]); w_s5o = din("w_s5o", [DEPTH, 1024, D])
    w_gate = din("w_gate", [DEPTH, 16, 512]); gnorm = din("gnorm", [DEPTH, 128, 256])
    w_glao = din("w_glao", [DEPTH, 1024, D]); w_reto = din("w_reto", [DEPTH, 1024, D])
    w_mix = din("w_mix", [DEPTH, D, D])
    w_fg = din("w_fg", [DEPTH, D, DFF]); w_fu = din("w_fu", [DEPTH, D, DFF]); w_fd = din("w_fd", [DEPTH, DFF, D])
    vecs = din("vecs", [DEPTH, V_N, 128]); nfin = din("nfin", [128, D])
    consts = din("consts", [128, C_N]); rot = din("rot", [NT, 128])

    yp = dout("yp", [NP, D]); ys = dout("ys", [NS, D])
    ps5 = dout("ps5", [DEPTH, 64, 128]); pgla = dout("pgla", [DEPTH, 4, 128, 256]); pret = dout("pret", [DEPTH, 8, 128, 128])
    ss5 = dout("ss5", [DEPTH, NS, 64, 128]); sgla = dout("sgla", [DEPTH, NS, 4, 128, 256]); sret = dout("sret", [DEPTH, NS, 8, 128, 128])

    UT = dscr("UT", [1024, NT], BF16); GQT = dscr("GQT", [512, NT]); GKT = dscr("GKT", [512, NT]); GLRT = dscr("GLRT", [16, NT])
    GV = dscr("GV", [NT, 1024], BF16); GR = dscr("GR", [NT, 1024]); RQ = dscr("RQ", [NT, 1024]); RK = dscr("RK", [NT, 1024])
    RV = dscr("RV", [NT, 1024], BF16); RG = dscr("RG", [NT, 1024]); MGT = dscr("MGT", [6144, NT], BF16)
    Y5A = dscr("Y5A", [1024, NT], BF16); A5T = dscr("A5T", [1024, NT], BF16); AGT = dscr("AGT", [1024, NT], BF16); ART = dscr("ART", [1024, NT], BF16)
    XMID = dscr("XMID", [NT, D]); X1 = dscr("X1", [NT, D]); FFTL = dscr("FFTL", [NPT + 1, 128, DFF // 128, 128], BF16)

    def xrows(src, ti):
        r0, n = tiles[ti]
        if src is None:
            return xp[r0:r0 + n, :] if ti < NPT else xs[:, :]
        return src[r0:r0 + n, :]

    with ExitStack() as es:
        S = Sched(nc, es)
        chans = {}

        def ch(name):
            if name not in chans:
                chans[name] = S.chan(name)
            return chans[name]

        def V(fn, R=(), W=(), inc=True): return S.op("dve", fn, [x.b if isinstance(x, Tl) else x for x in R], [x.b if isinstance(x, Tl) else x for x in W], inc)
        def A(fn, R=(), W=(), inc=True): return S.op("act", fn, [x.b if isinstance(x, Tl) else x for x in R], [x.b if isinstance(x, Tl) else x for x in W], inc)
        def G(fn, R=(), W=(), inc=True): return S.op("pool", fn, [x.b if isinstance(x, Tl) else x for x in R], [x.b if isinstance(x, Tl) else x for x in W], inc)
        def M(fn, R=(), W=(), inc=True): return S.op("pe", fn, [x.b if isinstance(x, Tl) else x for x in R], [x.b if isinstance(x, Tl) else x for x in W], inc)

        def DMA(q, out, in_, chn, R=(), W=()):
            if len(W) == 0 and q == "sp":
                q = "act"
            elif len(W) > 0 and q == "act":
                q = "sp"
            return S.dma(q, out, in_, ch(chn), [x.b if isinstance(x, Tl) else x for x in R], [x.b if isinstance(x, Tl) else x for x in W])

        uid = [0]

        def mk(stack, kind):
            def f(name, shape, dt=F32):
                alloc = nc.sbuf_tensor if kind == "s" else nc.psum_tensor
                uid[0] += 1
                name = f"{name}_{uid[0]}"
                return Tl(stack.enter_context(alloc(name, list(shape), dt)), name)
            return f

        T0 = mk(es, "s")
        cst = T0("cst", [128, C_N]); idb = T0("idb", [128, 128], BF16)
        DMA("sp", cst.t[:], consts, "cst", W=[cst])
        V(lambda e: e.tensor_copy(out=idb.t[:], in_=cst.t[:, C_ID:C_ID + 128]), R=[cst], W=[idb])
        idf = cst.t[:, C_ID:C_ID + 128]

        def rms_rstd(TT, src_ap, src_tl, n, width, name):
            junk = TT(name + "_j", [128, width]); ss = TT(name + "_s", [128, 1]); rs = TT(name + "_r", [128, 1])
            return junk, ss, rs

        def norm_T(TT, PP, src, gcol, hT, hTb, vecT, pfx):
            xt = [TT(pfx + f"xt{i}", [128, D]) for i in range(2)]
            junk = TT(pfx + "junk", [128, D], BF16)
            ss = [TT(pfx + f"ss{i}", [128, 16]) for i in range(2)]
            rs = [TT(pfx + f"rs{i}", [128, 16]) for i in range(2)]
            xn = [TT(pfx + f"xn{i}", [128, D], BF16) for i in range(2)]
            ptr = [PP(pfx + f"ptr{i}", [128, 4, 128], BF16) for i in range(2)]
            for ti, (r0, n) in enumerate(tiles):
                s = ti % 2
                DMA("sp", xt[s].t[:n, :], xrows(src, ti), pfx + f"xt{s}", W=[xt[s]])
                V(lambda e: e.memset(ss[s].t[:], 0.0), W=[ss[s]])
                A(lambda e: e.activation(out=junk.t[:n, :], in_=xt[s].t[:n, :], func=AF.Square, accum_out=ss[s].t[:n, 0:1]), R=[xt[s]], W=[junk, ss[s]])
                A(lambda e: e.activation(out=rs[s].t[:n, 0:1], in_=ss[s].t[:n, 0:1], func=AF.Sqrt, scale=1.0 / D, bias=EPS), R=[ss[s]], W=[rs[s]])
                V(lambda e: e.reciprocal(out=rs[s].t[:n, 0:1], in_=rs[s].t[:n, 0:1]), R=[rs[s]], W=[rs[s]])
                A(lambda e: e.activation(out=xn[s].t[:n, :], in_=xt[s].t[:n, :], func=AF.Copy, scale=rs[s].t[:n, 0:1]), R=[xt[s], rs[s]], W=[xn[s]])
                for q in range(4):
                    p = ptr[q % 2]
                    for j in range(4):
                        kc = q * 4 + j
                        M(lambda e: e.transpose(out=p.t[:, j, :n], in_=xn[s].t[:n, kc * 128:(kc + 1) * 128], identity=idb.t[:n, :n]),
                          R=[xn[s], idb], W=[p], inc=(j == 3))
                    V(lambda e: e.tensor_tensor(out=hT.t[:, q * 4:q * 4 + 4, r0:r0 + n], in0=p.t[:, :, :n],
                                                in1=vecT.t[:, gcol + q * 4:gcol + q * 4 + 4].unsqueeze(2).to_broadcast([128, 4, n]), op=ALU.mult),
                      R=[p, vecT], W=[hTb[ti]])
            if DBG and pfx == "n1" and not hasattr(nc, "_dbg1"):
                nc._dbg1 = 1
                d1 = dscr("DBG_ss1", [128, 16]); d2 = dscr("DBG_rs1", [128, 16]); d3 = dscr("DBG_xt1", [128, D]); d4 = dscr("DBG_xn1", [128, D], BF16)
                DMA("sp", d1, ss[1].t[:], "dbg", R=[ss[1]]); DMA("sp", d2, rs[1].t[:], "dbg", R=[rs[1]])
                DMA("sp", d3, xt[1].t[:], "dbg", R=[xt[1]]); DMA("sp", d4, xn[1].t[:], "dbg", R=[xn[1]])

        def sincos(th, sn, cs, t1, t2, RW):
            A(lambda e: e.activation(out=t1, in_=th, func=AF.Sin, scale=1.0 / 16), **RW)
            V(lambda e: e.tensor_tensor(out=t1, in0=t1, in1=t1, op=ALU.mult), **RW)
            V(lambda e: e.tensor_scalar(out=cs, in0=t1, scalar1=-2.0, scalar2=1.0, op0=ALU.mult, op1=ALU.add), **RW)
            A(lambda e: e.activation(out=sn, in_=th, func=AF.Sin, scale=1.0 / 8), **RW)
            for _ in range(3):
                V(lambda e: e.tensor_tensor(out=t1, in0=sn, in1=cs, op=ALU.mult), **RW)
                V(lambda e: e.tensor_tensor(out=t2, in0=sn, in1=sn, op=ALU.mult), **RW)
                V(lambda e: e.tensor_tensor(out=cs, in0=cs, in1=cs, op=ALU.mult), **RW)
                V(lambda e: e.tensor_tensor(out=cs, in0=cs, in1=t2, op=ALU.subtract), **RW)
                V(lambda e: e.tensor_scalar_mul(out=sn, in0=t1, scalar1=2.0), **RW)

        def blk_tiles(b0, bn):
            return [ti for ti, (r0, n) in enumerate(tiles) if r0 < b0 + bn and r0 + n > b0]

        for l in range(DEPTH):
            xsrc = None if l == 0 else X1
            with ExitStack() as ls:
                TL = mk(ls, "s")
                vecT = TL("vecT", [128, V_N])
                with ExitStack() as ps:
                    TT = mk(ps, "s"); PP = mk(ps, "p")
                    vr = TT("vr", [V_N, 128]); pv = PP("pv", [128, V_N])
                    DMA("sp", vr.t[:], vecs[l], "vr", W=[vr])
                    M(lambda e: e.transpose(out=pv.t[:], in_=vr.t[:], identity=idf[:V_N, :V_N]), R=[vr, cst], W=[pv])
                    V(lambda e: e.tensor_copy(out=vecT.t[:], in_=pv.t[:]), R=[pv], W=[vecT])
                    S.barrier()

                with ExitStack() as ps:
                    TT = mk(ps, "s"); PP = mk(ps, "p")
                    hT = TT("hT", [128, 16, NT], BF16)
                    hTb = [Buf(f"hT{ti}") for ti in range(len(tiles))]
                    with ExitStack() as ps1:
                        norm_T(mk(ps1, "s"), mk(ps1, "p"), xsrc, V_NM, hT, hTb, vecT, "n1")
                        S.barrier()
                    if DBG and l == 0:
                        dbg_hT = dscr("DBG_hT", [128, 16, NT], BF16)
                        DMA("sp", dbg_hT, hT.t[:], "dbg", R=hTb)
                    wsl = [TT(f"w{i}", [128, 16, 512], BF16) for i in range(3)]
                    pm = [PP(f"pm{i}", [128, 512]) for i in range(4)]
                    stf = [TT(f"stf{i}", [128, 512]) for i in range(4)]
                    stb = [TT(f"stb{i}", [128, 512], BF16) for i in range(4)]
                    segs = [(0, 1024, "F", AF.Copy, UT, True), (1024, 512, "F", AF.Copy, GQT, False), (1536, 512, "F", AF.Copy, GKT, False),
                            (2048, 1024, "T", AF.Copy, GV, True), (3072, 1024, "T", AF.Silu, GR, False), (4096, 16, "F", AF.Copy, GLRT, False),
                            (4112, 1024, "T", AF.Copy, RQ, False), (5136, 1024, "T", AF.Copy, RK, False), (6160, 1024, "T", AF.Copy, RV, True),
                            (7184, 1024, "T", AF.Silu, RG, False), (8208, 6144, "F", AF.Sigmoid, MGT, True)]
                    wi = 0; ui = 0
                    for (c0, ncol, form, func, dst, isb) in segs:
                        for cb in range(0, ncol, 512):
                            ncb = min(512, ncol - cb)
                            w = wsl[wi % 3]
                            DMA("pool", w.t[:, :, :ncb], w_in[l][:, c0 + cb:c0 + cb + ncb].rearrange("(k p) e -> p k e", p=128), f"w{wi % 3}", W=[w])
                            wi += 1
                            if form == "T":
                                for ti, (r0, n) in enumerate(tiles):
                                    p = pm[ui % 4]; st = (stb if isb else stf)[ui % 4]; ui += 1
                                    for kc in range(16):
                                        M(lambda e: e.matmul(p.t[:n, :ncb], lhsT=hT.t[:, kc, r0:r0 + n], rhs=w.t[:, kc, :ncb], start=(kc == 0), stop=(kc == 15)),
                                          R=[hTb[ti], w], W=[p], inc=(kc == 15))
                                    A(lambda e: e.activation(out=st.t[:n, :ncb], in_=p.t[:n, :ncb], func=func), R=[p], W=[st])
                                    DMA("sp", dst[r0:r0 + n, cb:cb + ncb], st.t[:n, :ncb], ("sb" if isb else "sf") + str((ui - 1) % 4), R=[st])
                            else:
                                for j0 in range(0, ncb, 128):
                                    m = min(128, ncb - j0)
                                    for (b0, bn) in blocks:
                                        p = pm[ui % 4]; st = (stb if isb else stf)[ui % 4]; ui += 1
                                        for kc in range(16):
                                            M(lambda e: e.matmul(p.t[:m, :bn], lhsT=w.t[:, kc, j0:j0 + m], rhs=hT.t[:, kc, b0:b0 + bn], start=(kc == 0), stop=(kc == 15)),
                                              R=[hTb[t_] for t_ in blk_tiles(b0, bn)] + [w], W=[p], inc=(kc == 15))
                                        A(lambda e: e.activation(out=st.t[:m, :bn], in_=p.t[:m, :bn], func=func), R=[p], W=[st])
                                        DMA("sp", dst[cb + j0:cb + j0 + m, b0:b0 + bn], st.t[:m, :bn], ("sb" if isb else "sf") + str((ui - 1) % 4), R=[st])
                    S.barrier()

                PI = float(np.pi)
                with ExitStack() as ps:
                    TT = mk(ps, "s"); PP = mk(ps, "p")
                    EC = TT("EC", [128, 64, 128]); ES = TT("ES", [128, 64, 128])
                    sp_ = TT("s5p", [128, 16, 64]); prm = TT("prm", [128, 3, 64])
                    X1_ = TT("X1_", [128, 64, 16]); X2_ = TT("X2_", [128, 64, 16])
                    CA = TT("CA", [128, 8, 128]); CB = TT("CB", [128, 8, 128])
                    xl = TT("xl", [128, 64])
                    DMA("sp", prm.t[:], s5dup[l].rearrange("a p g -> p a g"), "prm", W=[prm])
                    ar = prm.t[:, 0, :]; ai = prm.t[:, 1, :]; ldt = prm.t[:, 2, :]
                    (dtb, mag, th, a1, sn, cs, abr, abi, den, nr, fr, fi, tq, tr) = [sp_.t[:, i_, :] for i_ in range(14)]
                    RW = dict(R=[sp_, prm], W=[sp_])
                    A(lambda e: e.activation(out=dtb, in_=ldt, func=AF.Exp), **RW)
                    V(lambda e: e.tensor_tensor(out=tq, in0=ar, in1=dtb, op=ALU.mult), **RW)
                    A(lambda e: e.activation(out=mag, in_=tq, func=AF.Exp), **RW)
                    V(lambda e: e.tensor_tensor(out=th, in0=ai, in1=dtb, op=ALU.mult), **RW)
                    sincos(th, sn, cs, a1, tq, RW)
                    V(lambda e: e.tensor_tensor(out=abr, in0=mag, in1=cs, op=ALU.mult), **RW)
                    V(lambda e: e.tensor_tensor(out=abi, in0=mag, in1=sn, op=ALU.mult), **RW)
                    V(lambda e: e.tensor_tensor(out=den, in0=ar, in1=ar, op=ALU.mult), **RW)
                    V(lambda e: e.tensor_tensor(out=tq, in0=ai, in1=ai, op=ALU.mult), **RW)
                    V(lambda e: e.tensor_tensor(out=den, in0=den, in1=tq, op=ALU.add), **RW)
                    V(lambda e: e.reciprocal(out=den, in_=den), **RW)
                    V(lambda e: e.tensor_scalar_add(out=nr, in0=abr, scalar1=-1.0), **RW)
                    V(lambda e: e.tensor_tensor(out=tq, in0=nr, in1=ar, op=ALU.mult), **RW)
                    V(lambda e: e.tensor_tensor(out=tr, in0=abi, in1=ai, op=ALU.mult), **RW)
                    V(lambda e: e.tensor_tensor(out=tq, in0=tq, in1=tr, op=ALU.add), **RW)
                    V(lambda e: e.tensor_tensor(out=fr, in0=tq, in1=den, op=ALU.mult), **RW)
                    V(lambda e: e.tensor_tensor(out=tq, in0=abi, in1=ar, op=ALU.mult), **RW)
                    V(lambda e: e.tensor_tensor(out=tr, in0=nr, in1=ai, op=ALU.mult), **RW)
                    V(lambda e: e.tensor_tensor(out=tq, in0=tq, in1=tr, op=ALU.subtract), **RW)
                    V(lambda e: e.tensor_tensor(out=fi, in0=tq, in1=den, op=ALU.mult), **RW)
                    xlb = [Buf("xl0"), Buf("xl1")]
                    V(lambda e: e.memset(xl.t[:], 0.0), W=xlb)
                    with ExitStack() as ps1:
                        T1 = mk(ps1, "s")
                        t1 = T1("et1", [128, 64, 64]); t2 = T1("et2", [128, 64, 64])
                        A_ = T1("A_", [128, 64, 16]); Bm = T1("Bm", [128, 64, 16]); tb = T1("tb", [128, 64, 16])
                        V(lambda e: e.tensor_copy(out=EC.t[:, :, 0:1], in_=cs.unsqueeze(2)), R=[sp_], W=[EC])
                        V(lambda e: e.tensor_copy(out=ES.t[:, :, 0:1], in_=sn.unsqueeze(2)), R=[sp_], W=[ES])
                        for s_ in range(7):
                            n_ = 2 ** s_
                            cb_ = EC.t[:, :, n_ - 1:n_].to_broadcast([128, 64, n_]); sb_ = ES.t[:, :, n_ - 1:n_].to_broadcast([128, 64, n_])
                            V(lambda e: e.tensor_tensor(out=t1.t[:, :, :n_], in0=EC.t[:, :, 0:n_], in1=cb_, op=ALU.mult), R=[EC], W=[t1])
                            V(lambda e: e.tensor_tensor(out=t2.t[:, :, :n_], in0=ES.t[:, :, 0:n_], in1=sb_, op=ALU.mult), R=[ES], W=[t2])
                            V(lambda e: e.tensor_tensor(out=t1.t[:, :, :n_], in0=t1.t[:, :, :n_], in1=t2.t[:, :, :n_], op=ALU.subtract), R=[t1, t2], W=[t1])
                            V(lambda e: e.tensor_tensor(out=t2.t[:, :, :n_], in0=EC.t[:, :, 0:n_], in1=sb_, op=ALU.mult), R=[EC, ES], W=[t2])
                            V(lambda e: e.tensor_copy(out=EC.t[:, :, n_:2 * n_], in_=t1.t[:, :, :n_]), R=[t1], W=[EC])
                            V(lambda e: e.tensor_tensor(out=t1.t[:, :, :n_], in0=ES.t[:, :, 0:n_], in1=cb_, op=ALU.mult), R=[EC, ES], W=[t1])
                            V(lambda e: e.tensor_tensor(out=ES.t[:, :, n_:2 * n_], in0=t1.t[:, :, :n_], in1=t2.t[:, :, :n_], op=ALU.add), R=[t1, t2], W=[ES])
                        DMA("sp", A_.t[0:64], s5b_re[l].rearrange("g p n -> p g n"), "bb", W=[A_]); DMA("sp", A_.t[64:128], s5b_im[l].rearrange("g p n -> p g n"), "bb", W=[A_])
                        DMA("sp", Bm.t[0:64], s5b_im[l].rearrange("g p n -> p g n"), "bb", W=[Bm]); DMA("sp", Bm.t[64:128], s5b_re[l].rearrange("g p n -> p g n"), "bb", W=[Bm])
                        V(lambda e: e.tensor_scalar_mul(out=Bm.t[0:64], in0=Bm.t[0:64], scalar1=-1.0), R=[Bm], W=[Bm])
                        frb = fr.unsqueeze(2).to_broadcast([128, 64, 16]); fib = fi.unsqueeze(2).to_broadcast([128, 64, 16])
                        V(lambda e: e.tensor_tensor(out=X1_.t[:], in0=A_.t[:], in1=frb, op=ALU.mult), R=[A_, sp_], W=[X1_])
                        V(lambda e: e.tensor_tensor(out=tb.t[:], in0=Bm.t[:], in1=fib, op=ALU.mult), R=[Bm, sp_], W=[tb])
                        V(lambda e: e.tensor_tensor(out=X1_.t[:], in0=X1_.t[:], in1=tb.t[:], op=ALU.add), R=[X1_, tb], W=[X1_])
                        V(lambda e: e.tensor_tensor(out=X2_.t[:], in0=A_.t[:], in1=fib, op=ALU.mult), R=[A_, sp_], W=[X2_])
                        V(lambda e: e.tensor_tensor(out=tb.t[:], in0=Bm.t[:], in1=frb, op=ALU.mult), R=[Bm, sp_], W=[tb])
                        V(lambda e: e.tensor_tensor(out=X2_.t[:], in0=X2_.t[:], in1=tb.t[:], op=ALU.subtract), R=[X2_, tb], W=[X2_])
                        DMA("sp", CA.t[:, :, 0:64], s5c_re[l].rearrange("(t q) p -> q t p", q=128), "cc", W=[CA]); DMA("sp", CA.t[:, :, 64:128], s5c_im[l].rearrange("(t q) p -> q t p", q=128), "cc", W=[CA])
                        DMA("sp", CB.t[:, :, 0:64], s5c_im[l].rearrange("(t q) p -> q t p", q=128), "cc", W=[CB]); DMA("sp", CB.t[:, :, 64:128], s5c_re[l].rearrange("(t q) p -> q t p", q=128), "cc", W=[CB])
                        V(lambda e: e.tensor_scalar_mul(out=CA.t[:, :, 64:128], in0=CA.t[:, :, 64:128], scalar1=-1.0), R=[CA], W=[CA])
                        V(lambda e: e.tensor_scalar_mul(out=CB.t[:], in0=CB.t[:], scalar1=-1.0), R=[CB], W=[CB])
                        S.barrier()
                    cT = EC.t[:, :, 127]; sT = ES.t[:, :, 127]
                    with ExitStack() as ps1:
                        T1 = mk(ps1, "s")
                        LB = [T1(f"LB{v}", [128, 8, 128], BF16) for v in range(2)]; LC = [T1(f"LC{v}", [128, 8, 128], BF16) for v in range(2)]
                        for v in range(2):
                            G(lambda e: e.memset(LC[v].t[:], 0.0), W=[LC[v]])
                        uT = T1("uT", [128, NT], BF16); ypre = T1("ypre", [128, NT]); y5s = T1("y5s", [128, NT], BF16)
                        m1 = [T1(f"m1{i}", [128, 4, 128]) for i in range(2)]; m2 = [T1(f"m2{i}", [128, 4, 128]) for i in range(2)]
                        win = [T1(f"win{i}", [128, 4, 128]) for i in range(2)]; wv = [T1(f"wv{i}", [128, 4, 128]) for i in range(2)]
                        a1b = [T1(f"a1b{i}", [128, 4, 128], BF16) for i in range(2)]; a2b = [T1(f"a2b{i}", [128, 4, 128], BF16) for i in range(2)]
                        c1 = [T1(f"c1{i}", [128, 16]) for i in range(2)]; c2 = [T1(f"c2{i}", [128, 16]) for i in range(2)]
                        rp = T1("rp", [16, 3, 512]); rw = T1("rw", [16, 8, 512]); x0r = T1("x0r", [16, 512]); x0i = T1("x0i", [16, 512])
                        ax0 = T1("ax0", [16, 8, 128]); xnew = T1("xnew", [16, 8, 128]); xnb = T1("xnb", [16, 8, 128], BF16); xnT = T1("xnT", [128, 8, 16], BF16)
                        gz = T1("gz", [128, 512]); gs = T1("gs", [128, 512])
                        P1 = mk(ps1, "p")
                        PS1 = [P1(f"PS1{i}", [128, 4, 128]) for i in range(2)]; PS2 = [P1(f"PS2{i}", [128, 4, 128]) for i in range(2)]
                        PY = [P1(f"PY{i}", [128, 128]) for i in range(2)]; ptw = PY[0]
                        psw = P1("psw", [128, 16]); ptx = P1("ptx", [128, 8, 16], BF16)
                        for gt in range(8):
                            DMA("sp", uT.t[:], UT[gt * 128:(gt + 1) * 128, :], "uT", W=[uT])
                            for v, X_ in enumerate((X1_, X2_)):
                                M(lambda e: e.transpose(out=ptw.t[:], in_=X_.t[:, gt * 8:(gt + 1) * 8, :], identity=idf), R=[X_, cst], W=[ptw])
                                for gg in range(8):
                                    V(lambda e: e.tensor_scalar_mul(out=LB[v].t[:, gg, :], in0=ptw.t[:], scalar1=cst.t[:, C_GM + gg:C_GM + gg + 1]), R=[ptw, cst], W=[LB[v]])
                            for v, C_ in enumerate((CA, CB)):
                                M(lambda e: e.transpose(out=ptw.t[:], in_=C_.t[:, gt, :], identity=idf), R=[C_, cst], W=[ptw])
                                for gg in range(8):
                                    V(lambda e: e.tensor_copy(out=LC[v].t[:, gg, 16 * gg:16 * gg + 16], in_=ptw.t[:, 16 * gg:16 * gg + 16]), R=[ptw], W=[LC[v]])
                            dcol = vecT.t[:, V_SD + gt:V_SD + gt + 1]
                            units = [(bi, hh) for bi in range(NPT) for hh in range(2)]

                            def st1a(u):
                                bi, hh = u; t0 = bi * 128; g0 = gt * 8 + hh * 4; i = hh
                                for q in range(4):
                                    M(lambda e: e.matmul(PS1[i].t[:, q, :], lhsT=LB[0].t[:, hh * 4 + q, :], rhs=uT.t[:, t0:t0 + 128], start=True, stop=True), R=[LB[0], uT], W=[PS1[i]], inc=False)
                                    M(lambda e: e.matmul(PS2[i].t[:, q, :], lhsT=LB[1].t[:, hh * 4 + q, :], rhs=uT.t[:, t0:t0 + 128], start=True, stop=True), R=[LB[1], uT], W=[PS2[i]], inc=(q == 3))
                                V(lambda e: e.tensor_tensor(out=m1[i].t[:], in0=PS1[i].t[:], in1=EC.t[:, g0:g0 + 4, :], op=ALU.mult), R=[PS1[i], EC], W=[m1[i]])
                                V(lambda e: e.tensor_tensor(out=m2[i].t[:], in0=PS2[i].t[:], in1=ES.t[:, g0:g0 + 4, :], op=ALU.mult), R=[PS2[i], ES], W=[m2[i]])
                                G(lambda e: e.tensor_tensor(out=win[i].t[:], in0=m1[i].t[:], in1=m2[i].t[:], op=ALU.add), R=[m1[i], m2[i]], W=[win[i]])
                                for q in range(4):
                                    V(lambda e: e.tensor_tensor_scan(out=wv[i].t[:, q, :], data0=mag[:, g0 + q:g0 + q + 1].to_broadcast([128, 128]), data1=win[i].t[:, q, :],
                                                                     initial=xl.t[:, g0 + q:g0 + q + 1], op0=ALU.mult, op1=ALU.add), R=[win[i], xlb[hh], sp_], W=[wv[i]])
                                G(lambda e: e.tensor_tensor(out=a1b[i].t[:], in0=wv[i].t[:], in1=EC.t[:, g0:g0 + 4, :], op=ALU.mult), R=[wv[i], EC], W=[a1b[i]])
                                G(lambda e: e.tensor_tensor(out=a2b[i].t[:], in0=wv[i].t[:], in1=ES.t[:, g0:g0 + 4, :], op=ALU.mult), R=[wv[i], ES], W=[a2b[i]])

                            def st1b(u):
                                bi, hh = u; g0 = gt * 8 + hh * 4; i = hh
                                M(lambda e: e.matmul(psw.t[:, 0:4], lhsT=cst.t[:, C_J:C_J + 128], rhs=wv[i].t[:, :, 127], start=True, stop=True), R=[cst, wv[i]], W=[psw])
                                V(lambda e: e.tensor_tensor(out=c1[i].t[:, 0:4], in0=wv[i].t[:, :, 127], in1=cT[:, g0:g0 + 4], op=ALU.mult), R=[wv[i], EC], W=[c1[i]])
                                V(lambda e: e.tensor_tensor(out=c2[i].t[:, 0:4], in0=psw.t[:, 0:4], in1=sT[:, g0:g0 + 4], op=ALU.mult), R=[psw, ES], W=[c2[i]])
                                V(lambda e: e.tensor_tensor(out=xl.t[:, g0:g0 + 4], in0=c1[i].t[:, 0:4], in1=c2[i].t[:, 0:4], op=ALU.add), R=[c1[i], c2[i]], W=[xlb[hh]])

                            def st2(u):
                                bi, hh = u; t0 = bi * 128; i = hh; py = PY[bi % 2]
                                for q in range(4):
                                    gg = hh * 4 + q
                                    M(lambda e: e.matmul(py.t[:], lhsT=LC[0].t[:, gg, :], rhs=a1b[i].t[:, q, :], start=(gg == 0), stop=False), R=[LC[0], a1b[i]], W=[py], inc=False)
                                    M(lambda e: e.matmul(py.t[:], lhsT=LC[1].t[:, gg, :], rhs=a2b[i].t[:, q, :], start=False, stop=(gg == 7)), R=[LC[1], a2b[i]], W=[py], inc=(q == 3))
                                if hh == 1:
                                    V(lambda e: e.scalar_tensor_tensor(out=ypre.t[:, t0:t0 + 128], in0=uT.t[:, t0:t0 + 128], scalar=dcol, in1=py.t[:], op0=ALU.mult, op1=ALU.add), R=[uT, vecT, py], W=[ypre])

                            st1a(units[0])
                            for ui_ in range(1, len(units)):
                                st1a(units[ui_]); st1b(units[ui_ - 1]); st2(units[ui_ - 1])
                            st1b(units[-1]); st2(units[-1])
                            DMA("sp", rp.t[:], s5row[l][:, :, gt * 512:(gt + 1) * 512].rearrange("a b x -> b a x"), "rp", W=[rp])
                            DMA("sp", x0r.t[:], s5r0[l][:, gt * 512:(gt + 1) * 512], "rp", W=[x0r]); DMA("sp", x0i.t[:], s5i0[l][:, gt * 512:(gt + 1) * 512], "rp", W=[x0i])
                            (dtr, mgr, thr, a1r, snr, csr, q1, q2) = [rw.t[:, i_, :] for i_ in range(8)]; abrr = thr; abir = dtr
                            RWs = dict(R=[rp, rw], W=[rw])
                            A(lambda e: e.activation(out=dtr, in_=rp.t[:, 2, :], func=AF.Exp), **RWs)
                            V(lambda e: e.tensor_tensor(out=q1, in0=rp.t[:, 0, :], in1=dtr, op=ALU.mult), **RWs)
                            A(lambda e: e.activation(out=mgr, in_=q1, func=AF.Exp), **RWs)
                            V(lambda e: e.tensor_tensor(out=thr, in0=rp.t[:, 1, :], in1=dtr, op=ALU.mult), **RWs)
                            sincos(thr, snr, csr, a1r, q1, RWs)
                            V(lambda e: e.tensor_tensor(out=abrr, in0=mgr, in1=csr, op=ALU.mult), **RWs)
                            V(lambda e: e.tensor_tensor(out=abir, in0=mgr, in1=snr, op=ALU.mult), **RWs)
                            v3 = lambda ap: ap.rearrange("b (g p) -> b g p", p=64)
                            V(lambda e: e.tensor_tensor(out=q1, in0=abrr, in1=x0r.t[:], op=ALU.mult), R=[rw, x0r], W=[rw])
                            V(lambda e: e.tensor_tensor(out=q2, in0=abir, in1=x0i.t[:], op=ALU.mult), R=[rw, x0i], W=[rw])
                            V(lambda e: e.tensor_tensor(out=ax0.t[:, :, 0:64], in0=v3(q1), in1=v3(q2), op=ALU.subtract), R=[rw], W=[ax0])
                            V(lambda e: e.tensor_tensor(out=q1, in0=abrr, in1=x0i.t[:], op=ALU.mult), R=[rw, x0i], W=[rw])
                            V(lambda e: e.tensor_tensor(out=q2, in0=abir, in1=x0r.t[:], op=ALU.mult), R=[rw, x0r], W=[rw])
                            V(lambda e: e.tensor_tensor(out=ax0.t[:, :, 64:128], in0=v3(q1), in1=v3(q2), op=ALU.add), R=[rw], W=[ax0])
                            for hh in range(2):
                                for q in range(4):
                                    M(lambda e: e.matmul(PS1[hh].t[:16, q, :], lhsT=uT.t[:, NP:NT], rhs=LB[0].t[:, hh * 4 + q, :], start=True, stop=True), R=[uT, LB[0]], W=[PS1[hh]], inc=(q == 3))
                                V(lambda e: e.tensor_tensor(out=xnew.t[:, hh * 4:hh * 4 + 4, :], in0=PS1[hh].t[:16], in1=ax0.t[:, hh * 4:hh * 4 + 4, :], op=ALU.add), R=[PS1[hh], ax0], W=[xnew])
                            DMA("sp", ss5[l][:, gt * 8:(gt + 1) * 8, :], xnew.t[:], "xnw", R=[xnew])
                            V(lambda e: e.tensor_copy(out=xnb.t[:], in_=xnew.t[:]), R=[xnew], W=[xnb])
                            for gg in range(8):
                                M(lambda e: e.transpose(out=ptx.t[:, gg, :], in_=xnb.t[:, gg, :], identity=idb.t[:16, :16]), R=[xnb, idb], W=[ptx], inc=(gg == 7))
                            A(lambda e: e.activation(out=xnT.t[:], in_=ptx.t[:], func=AF.Copy), R=[ptx], W=[xnT])
                            for gg in range(8):
                                M(lambda e: e.matmul(PY[1].t[:, 0:16], lhsT=LC[0].t[:, gg, :], rhs=xnT.t[:, gg, :], start=(gg == 0), stop=(gg == 7)), R=[LC[0], xnT], W=[PY[1]], inc=(gg == 7))
                            V(lambda e: e.scalar_tensor_tensor(out=ypre.t[:, NP:NT], in0=uT.t[:, NP:NT], scalar=dcol, in1=PY[1].t[:, 0:16], op0=ALU.mult, op1=ALU.add), R=[uT, vecT, PY[1]], W=[ypre])
                            for (b0, bn) in blocks:
                                yv = ypre.t[:, b0:b0 + bn]
                                A(lambda e: e.activation(out=gz.t[:, :bn], in_=yv, func=AF.Square), R=[ypre], W=[gz])
                                V(lambda e: e.tensor_scalar(out=gz.t[:, :bn], in0=gz.t[:, :bn], scalar1=0.044715, scalar2=1.0, op0=ALU.mult, op1=ALU.add), R=[gz], W=[gz])
                                V(lambda e: e.tensor_tensor(out=gz.t[:, :bn], in0=gz.t[:, :bn], in1=yv, op=ALU.mult), R=[gz, ypre], W=[gz])
                                A(lambda e: e.activation(out=gs.t[:, :bn], in_=gz.t[:, :bn], func=AF.Sigmoid, scale=1.5957691216057308), R=[gz], W=[gs])
                                V(lambda e: e.tensor_tensor(out=y5s.t[:, b0:b0 + bn], in0=gs.t[:, :bn], in1=yv, op=ALU.mult), R=[gs, ypre], W=[y5s])
                            DMA("sp", Y5A[gt * 128:(gt + 1) * 128, :], y5s.t[:], "y5s", R=[y5s])
                        M(lambda e: e.transpose(out=ptw.t[:64, :], in_=xl.t[:], identity=idf), R=xlb + [cst], W=[ptw])
                        A(lambda e: e.activation(out=gz.t[:64, 0:128], in_=ptw.t[:64, :], func=AF.Copy), R=[ptw], W=[gz])
                        DMA("sp", ps5[l], gz.t[:64, 0:128], "xnw", R=[gz])
                        S.barrier()
                    with ExitStack() as ps1:
                        T1 = mk(ps1, "s")
                        y5a = T1("y5a", [128, 8, NT], BF16)
                        DMA("sp", y5a.t[:], Y5A.rearrange("(k p) t -> p k t", p=128), "y5s", W=[y5a])
                        wgl = T1("wgl", [128, 8, 1024], BF16); sgz = T1("sgz", [128, 512]); stb = [T1(f"gstb{i}", [128, 512], BF16) for i in range(2)]
                        pz = [mk(ps1, "p")(f"pz{i}", [128, 512]) for i in range(2)]
                        DMA("pool", wgl.t[:], w_glu[l].rearrange("(k p) e -> p k e", p=128), "w0", W=[wgl])
                        ui = 0
                        for et in range(8):
                            for (b0, bn) in blocks:
                                i = ui % 2; ui += 1
                                for kc in range(8):
                                    M(lambda e: e.matmul(pz[i].t[:, :bn], lhsT=wgl.t[:, kc, et * 128:(et + 1) * 128], rhs=y5a.t[:, kc, b0:b0 + bn], start=(kc == 0), stop=(kc == 7)),
                                      R=[wgl, y5a], W=[pz[i]], inc=(kc == 7))
                                A(lambda e: e.activation(out=sgz.t[:, :bn], in_=pz[i].t[:, :bn], func=AF.Sigmoid, bias=vecT.t[:, V_BG + et:V_BG + et + 1]), R=[pz[i], vecT], W=[sgz])
                                V(lambda e: e.tensor_tensor(out=stb[i].t[:, :bn], in0=sgz.t[:, :bn], in1=y5a.t[:, et, b0:b0 + bn], op=ALU.mult), R=[sgz, y5a], W=[stb[i]])
                                DMA("sp", A5T[et * 128:(et + 1) * 128, b0:b0 + bn], stb[i].t[:, :bn], f"sb{i}", R=[stb[i]])
                        S.barrier()

                SC = 128.0 ** -0.5
                with ExitStack() as ps:
                    TT = mk(ps, "s"); PP = mk(ps, "p")
                    wgt = TT("wgt", [16, 512]); gnr = TT("gnr", [128, 256]); nb = TT("nb", [128, 16]); glr = TT("glr", [16, NT])
                    DMA("sp", wgt.t[:], w_gate[l], "g0", W=[wgt]); DMA("sp", gnr.t[:], gnorm[l], "g0", W=[gnr]); DMA("sp", glr.t[:], GLRT, "g0", W=[glr])
                    V(lambda e: e.tensor_scalar_mul(out=nb.t[:, 0:4], in0=vecT.t[:, V_GB:V_GB + 4], scalar1=-1.0), R=[vecT], W=[nb])
                    qT = TT("gqT", [128, NT]); kT = TT("gkT", [128, NT]); l1 = TT("gl1", [128, NT]); bc = TT("gbc", [128, NT]); enb = TT("genb", [128, NT])
                    eb = [TT(f"geb{h}", [128, NT]) for h in range(4)]
                    qp = [TT(f"gqp{h}", [128, NT], BF16) for h in range(4)]; kp = [TT(f"gkp{h}", [128, NT], BF16) for h in range(4)]
                    ksT = [TT(f"gksT{h}", [16, 128]) for h in range(4)]; qs = [TT(f"gqs{h}", [128, 16]) for h in range(4)]
                    St = [TT(f"gS{h}", [128, 256]) for h in range(4)]; Sb = [TT(f"gSb{h}", [128, 256], BF16) for h in range(4)]
                    kpt = [TT(f"gkpt{h}", [128, 128], BF16) for h in range(4)]; scm = [TT(f"gscm{h}", [128, 128], BF16) for h in range(4)]
                    ss = [TT(f"gss{h}", [128, 16]) for h in range(4)]; rs = [TT(f"grs{h}", [128, 16]) for h in range(4)]
                    on = [TT(f"gon{h}", [128, 256]) for h in range(4)]; og = [TT(f"gog{h}", [128, 256], BF16) for h in range(4)]; ost = [TT(f"gost{h}", [128, 2, 128], BF16) for h in range(4)]
                    vt = [TT(f"gv{i}", [128, 1024], BF16) for i in range(2)]; grt = [TT(f"ggr{i}", [128, 1024]) for i in range(2)]
                    Vbd = TT("gVbd", [16, 16, 256]); Qbd = TT("gQbd", [128, 16, 16]); S0 = [TT(f"gS0{i}", [128, 256]) for i in range(2)]; Sn = [TT(f"gSn{i}", [128, 256]) for i in range(2)]
                    pgt = PP("pgt", [128, 512]); ptw = PP("gptw", [16, 128])
                    pab = [PP(f"gpab{i}", [128, 2, 256]) for i in range(2)]; psc = [PP(f"gpsc{i}", [128, 256]) for i in range(2)]; pbf = [PP(f"gpbf{i}", [128, 3, 128], BF16) for i in range(2)]
                    pob = [pab[i].b for i in range(2)]; pkb = pob; ptkb = [pbf[i].b for i in range(2)]; ptob = ptkb

                    def gpost(n, t0, h, po_ap, po_buf, grt_ap, grt_tl, si):
                        V(lambda e: e.memset(ss[h].t[:], 0.0), W=[ss[h]])
                        A(lambda e: e.activation(out=on[h].t[:n, :], in_=po_ap, func=AF.Square, accum_out=ss[h].t[:n, 0:1]), R=[po_buf], W=[on[h], ss[h]])
                        A(lambda e: e.activation(out=rs[h].t[:n, 0:1], in_=ss[h].t[:n, 0:1], func=AF.Sqrt, scale=1.0 / 256, bias=EPS), R=[ss[h]], W=[rs[h]])
                        V(lambda e: e.reciprocal(out=rs[h].t[:n, 0:1], in_=rs[h].t[:n, 0:1]), R=[rs[h]], W=[rs[h]])
                        V(lambda e: e.scalar_tensor_tensor(out=on[h].t[:n, :], in0=po_ap, scalar=rs[h].t[:n, 0:1], in1=gnr.t[:n, :], op0=ALU.mult, op1=ALU.mult), R=[po_buf, rs[h], gnr], W=[on[h]])
                        G(lambda e: e.tensor_tensor(out=og[h].t[:n, :], in0=on[h].t[:n, :], in1=grt_ap, op=ALU.mult), R=[on[h], grt_tl], W=[og[h]])
                        for j in range(2):
                            M(lambda e: e.transpose(out=pbf[si].t[:, 1 + j, :n], in_=og[h].t[:n, j * 128:(j + 1) * 128], identity=idb.t[:n, :n]), R=[og[h], idb], W=[ptob[si]], inc=(j == 1))
                        A(lambda e: e.activation(out=ost[h].t[:, :, :n], in_=pbf[si].t[:, 1:3, :n], func=AF.Copy), R=[ptob[si]], W=[ost[h]])
                        for j in range(2):
                            DMA("sp", AGT[h * 256 + j * 128:h * 256 + (j + 1) * 128, t0:t0 + n], ost[h].t[:, j, :n], f"g1{h}", R=[ost[h]])

                    for h in range(4):
                        DMA("sp", qT.t[:], GQT[h * 128:(h + 1) * 128, :], "g2", W=[qT]); DMA("sp", kT.t[:], GKT[h * 128:(h + 1) * 128, :], "g2", W=[kT])
                        for (b0, bn) in blocks:
                            M(lambda e: e.matmul(pgt.t[:, :bn], lhsT=wgt.t[0:16, h * 128:(h + 1) * 128], rhs=glr.t[0:16, b0:b0 + bn], start=True, stop=True), R=[wgt, glr], W=[pgt])
                            A(lambda e: e.activation(out=l1.t[:, b0:b0 + bn], in_=pgt.t[:, :bn], func=AF.Exp, scale=-1.0, bias=nb.t[:, h:h + 1]), R=[pgt, nb], W=[l1])
                            A(lambda e: e.activation(out=l1.t[:, b0:b0 + bn], in_=l1.t[:, b0:b0 + bn], func=AF.Ln, bias=1.0), R=[l1], W=[l1])
                            if b0 < NP:
                                V(lambda e: e.tensor_tensor_scan(out=bc.t[:, b0:b0 + bn], data0=cst.t[:, C_D0:C_D0 + bn], data1=l1.t[:, b0:b0 + bn], initial=0.0, op0=ALU.mult, op1=ALU.add), R=[cst, l1], W=[bc])
                            else:
                                V(lambda e: e.tensor_copy(out=bc.t[:, b0:b0 + bn], in_=l1.t[:, b0:b0 + bn]), R=[l1], W=[bc])
                        A(lambda e: e.activation(out=eb[h].t[:], in_=bc.t[:], func=AF.Exp, scale=-1.0 / 16), R=[bc], W=[eb[h]])
                        A(lambda e: e.activation(out=enb.t[:], in_=bc.t[:], func=AF.Exp, scale=1.0 / 16), R=[bc], W=[enb])
                        V(lambda e: e.scalar_tensor_tensor(out=qp[h].t[:], in0=qT.t[:], scalar=SC, in1=eb[h].t[:], op0=ALU.mult, op1=ALU.mult), R=[qT, eb[h]], W=[qp[h]])
                        G(lambda e: e.tensor_tensor(out=kp[h].t[:], in0=kT.t[:], in1=enb.t[:], op=ALU.mult), R=[kT, enb], W=[kp[h]])
                        V(lambda e: e.memset(St[h].t[:], 0.0), W=[St[h]]); V(lambda e: e.memset(Sb[h].t[:], 0.0), W=[Sb[h]])
                        M(lambda e: e.transpose(out=ptw.t[:], in_=kT.t[:, NP:NT], identity=idf), R=[kT, cst], W=[ptw])
                        A(lambda e: e.activation(out=ksT[h].t[:], in_=ptw.t[:], func=AF.Copy), R=[ptw], W=[ksT[h]])
                        V(lambda e: e.tensor_scalar_mul(out=qs[h].t[:], in0=qT.t[:, NP:NT], scalar1=SC), R=[qT], W=[qs[h]])
                    for c in range(NPT):
                        t0 = c * 128; vi = c % 2
                        DMA("sp", vt[vi].t[:], GV[t0:t0 + 128, :], f"g3{vi}", W=[vt[vi]]); DMA("sp", grt[vi].t[:], GR[t0:t0 + 128, :], f"g3{vi}", W=[grt[vi]])
                        for pr in range(2):
                            hs_ = (2 * pr, 2 * pr + 1)
                            for h in hs_:
                                si = h % 2
                                M(lambda e: e.transpose(out=pbf[si].t[:, 0, :], in_=kp[h].t[:, t0:t0 + 128], identity=idb.t[:]), R=[kp[h], idb], W=[ptkb[si]])
                                M(lambda e: e.matmul(psc[si].t[:, 0:128], lhsT=kp[h].t[:, t0:t0 + 128], rhs=qp[h].t[:, t0:t0 + 128], start=True, stop=True), R=[kp[h], qp[h]], W=[psc[si]])
                            for h in hs_:
                                si = h % 2
                                A(lambda e: e.activation(out=kpt[h].t[:], in_=pbf[si].t[:, 0, :], func=AF.Copy), R=[ptkb[si]], W=[kpt[h]])
                                V(lambda e: e.tensor_tensor(out=scm[h].t[:], in0=psc[si].t[:, 0:128], in1=cst.t[:, C_TRI:C_TRI + 128], op=ALU.mult), R=[psc[si], cst], W=[scm[h]])
                            for h in hs_:
                                si = h % 2; vh = vt[vi].t[:, h * 256:(h + 1) * 256]
                                M(lambda e: e.matmul(pab[si].t[:, 0, :], lhsT=scm[h].t[:], rhs=vh, start=True, stop=False), R=[scm[h], vt[vi]], W=[pob[si]], inc=False)
                                M(lambda e: e.matmul(pab[si].t[:, 0, :], lhsT=qp[h].t[:, t0:t0 + 128], rhs=Sb[h].t[:], start=False, stop=True), R=[qp[h], Sb[h]], W=[pob[si]])
                                M(lambda e: e.matmul(pab[si].t[:, 1, :], lhsT=kpt[h].t[:], rhs=vh, start=True, stop=True), R=[kpt[h], vt[vi]], W=[pkb[si]])
                            for h in hs_:
                                si = h % 2
                                ecol = eb[h].t[:, t0 + 127:t0 + 128]
                                V(lambda e: e.tensor_scalar_mul(out=St[h].t[:], in0=St[h].t[:], scalar1=ecol), R=[St[h], eb[h]], W=[St[h]])
                                V(lambda e: e.scalar_tensor_tensor(out=St[h].t[:], in0=pab[si].t[:, 1, :], scalar=ecol, in1=St[h].t[:], op0=ALU.mult, op1=ALU.add), R=[pkb[si], eb[h], St[h]], W=[St[h]])
                                A(lambda e: e.activation(out=Sb[h].t[:], in_=St[h].t[:], func=AF.Copy), R=[St[h]], W=[Sb[h]])
                            for h in hs_:
                                si = h % 2
                                gpost(128, t0, h, pab[si].t[:, 0, :], pob[si], grt[vi].t[:, h * 256:(h + 1) * 256], grt[vi], si)
                    for h in range(4):
                        DMA("sp", pgla[l, h], St[h].t[:], "g4", R=[St[h]])
                    DMA("sp", vt[0].t[:NS, :], GV[NP:NT, :], "g30", W=[vt[0]]); DMA("sp", grt[0].t[:NS, :], GR[NP:NT, :], "g30", W=[grt[0]])
                    for h in range(4):
                        si = h % 2
                        V(lambda e: e.tensor_tensor(out=Vbd.t[:], in0=vt[0].t[:NS, h * 256:(h + 1) * 256].unsqueeze(1).to_broadcast([NS, NS, 256]),
                                                    in1=cst.t[:NS, C_ID:C_ID + NS].unsqueeze(2).to_broadcast([NS, NS, 256]), op=ALU.mult), R=[vt[0], cst], W=[Vbd])
                        V(lambda e: e.tensor_tensor(out=Qbd.t[:], in0=qs[h].t[:].unsqueeze(1).to_broadcast([128, NS, NS]),
                                                    in1=cst.t[:, C_E16:C_E16 + 256].rearrange("p (a b) -> p a b", a=NS), op=ALU.mult), R=[qs[h], cst], W=[Qbd])
                        for b_ in range(NS):
                            bi_ = b_ % 2
                            DMA("sp", S0[bi_].t[:], gla0[l, b_, h], f"g5{bi_}", W=[S0[bi_]])
                            M(lambda e: e.matmul(psc[bi_].t[:], lhsT=ksT[h].t[:], rhs=Vbd.t[:, b_, :], start=True, stop=True), R=[ksT[h], Vbd], W=[psc[bi_]])
                            V(lambda e: e.scalar_tensor_tensor(out=Sn[bi_].t[:], in0=S0[bi_].t[:], scalar=eb[h].t[:, NP + b_:NP + b_ + 1], in1=psc[bi_].t[:], op0=ALU.mult, op1=ALU.add),
                              R=[S0[bi_], eb[h], psc[bi_]], W=[Sn[bi_]])
                            DMA("sp", sgla[l, b_, h], Sn[bi_].t[:], f"g6{bi_}", R=[Sn[bi_]])
                            M(lambda e: e.matmul(pgt.t[:NS, 0:256], lhsT=Qbd.t[:, b_, :], rhs=Sn[bi_].t[:], start=(b_ == 0), stop=(b_ == NS - 1)), R=[Qbd, Sn[bi_]], W=[pgt])
                        gpost(NS, NP, h, pgt.t[:NS, 0:256], pgt, grt[0].t[:NS, h * 256:(h + 1) * 256], grt[0], si)
                    S.barrier()

                with ExitStack() as ps:
                    TT = mk(ps, "s"); PP = mk(ps, "p")
                    rq = TT("rq", [128, 1024]); rk = TT("rk", [128, 1024]); rv = TT("rv", [128, 1024], BF16); rg = TT("rg", [128, 1024]); rt = TT("rt", [128, 128])
                    qr = TT("qr", [128, 1024]); kr = TT("kr", [128, 1024]); ta = TT("rta", [128, 8, 64]); tb = TT("rtb", [128, 8, 64])
                    qb = TT("rqb", [128, 1024], BF16); kb = TT("rkb", [128, 1024], BF16); k2 = TT("rk2", [128, 1024], BF16)
                    qTt = TT("rqT", [128, 8, 128], BF16); kTt = TT("rkT", [128, 8, 128], BF16)
                    S4 = [TT(f"rS{g_}", [128, 4, 128]) for g_ in range(2)]; Sb4 = [TT(f"rSb{g_}", [128, 4, 128], BF16) for g_ in range(2)]
                    scm4 = TT("rscm", [128, 4, 128], BF16); pos4 = TT("rpos", [128, 4, 128]); o_ = TT("ro", [128, 8, 128]); sq = TT("rsq", [128, 8, 128])
                    st1 = TT("rst1", [128, 16]); st2 = TT("rst2", [128, 16]); og = TT("rog", [128, 1024], BF16); ost = TT("rost", [128, 8, 128], BF16)
                    Vbd = TT("rVbd", [16, 16, 128]); qs = TT("rqs", [128, 16]); Qbd = TT("rQbd", [128, 16, 16]); S0 = TT("rS0", [128, 128]); Sn = TT("rSn", [128, 128])
                    ptq = [PP(f"rptq{i}", [128, 4, 128], BF16) for i in range(2)]; psc = PP("rpsc", [128, 4, 128]); po = PP("rpo", [128, 4, 128]); pi_ = PP("rpi", [128, 4, 128]); pkv = PP("rpkv", [128, 4, 128])
                    ptw = PP("rptw", [128, 16])
                    for g_ in range(2):
                        V(lambda e: e.memset(S4[g_].t[:], 0.0), W=[S4[g_]]); V(lambda e: e.memset(Sb4[g_].t[:], 0.0), W=[Sb4[g_]])

                    def rotary(src, dst, n, scale):
                        s4 = src.t[:n, :].rearrange("p (h i two) -> p h i two", h=8, two=2); d4 = dst.t[:n, :].rearrange("p (h i two) -> p h i two", h=8, two=2)
                        cb_ = rt.t[:n, 0:64].unsqueeze(1).to_broadcast([n, 8, 64]); sb_ = rt.t[:n, 64:128].unsqueeze(1).to_broadcast([n, 8, 64])
                        V(lambda e: e.tensor_tensor(out=ta.t[:n], in0=s4[:, :, :, 0], in1=cb_, op=ALU.mult), R=[src, rt], W=[ta])
                        V(lambda e: e.tensor_tensor(out=tb.t[:n], in0=s4[:, :, :, 1], in1=sb_, op=ALU.mult), R=[src, rt], W=[tb])
                        V(lambda e: e.tensor_tensor(out=d4[:, :, :, 0], in0=ta.t[:n], in1=tb.t[:n], op=ALU.subtract), R=[ta, tb], W=[dst])
                        V(lambda e: e.tensor_tensor(out=ta.t[:n], in0=s4[:, :, :, 1], in1=cb_, op=ALU.mult), R=[src, rt], W=[ta])
                        V(lambda e: e.tensor_tensor(out=tb.t[:n], in0=s4[:, :, :, 0], in1=sb_, op=ALU.mult), R=[src, rt], W=[tb])
                        V(lambda e: e.tensor_tensor(out=d4[:, :, :, 1], in0=ta.t[:n], in1=tb.t[:n], op=ALU.add), R=[ta, tb], W=[dst])
                        if scale != 1.0:
                            V(lambda e: e.tensor_scalar_mul(out=dst.t[:n, :], in0=dst.t[:n, :], scalar1=scale), R=[dst], W=[dst])

                    for ti, (r0, n) in enumerate(tiles):
                        DMA("sp", rq.t[:n, :], RQ[r0:r0 + n, :], "r0", W=[rq]); DMA("sp", rk.t[:n, :], RK[r0:r0 + n, :], "r0", W=[rk])
                        DMA("sp", rv.t[:n, :], RV[r0:r0 + n, :], "r0", W=[rv]); DMA("sp", rg.t[:n, :], RG[r0:r0 + n, :], "r0", W=[rg]); DMA("sp", rt.t[:n, :], rot[r0:r0 + n, :], "r0", W=[rt])
                        rotary(rq, qr, n, 1.0); rotary(rk, kr, n, SC)
                        if ti < NPT:
                            G(lambda e: e.tensor_copy(out=qb.t[:], in_=qr.t[:]), R=[qr], W=[qb]); G(lambda e: e.tensor_copy(out=kb.t[:], in_=kr.t[:]), R=[kr], W=[kb])
                            V(lambda e: e.tensor_tensor(out=k2.t[:].rearrange("p (h k) -> p h k", h=8), in0=kr.t[:].rearrange("p (h k) -> p h k", h=8),
                                                        in1=cst.t[:, C_KDEC:C_KDEC + 8].unsqueeze(2).to_broadcast([128, 8, 128]), op=ALU.mult), R=[kr, cst], W=[k2])
                            for (srcb, dstT) in ((qb, qTt), (kb, kTt)):
                                for hq in range(2):
                                    for j in range(4):
                                        hh_ = hq * 4 + j
                                        M(lambda e: e.transpose(out=ptq[hq].t[:, j, :], in_=srcb.t[:, hh_ * 128:(hh_ + 1) * 128], identity=idb.t[:]), R=[srcb, idb], W=[ptq[hq]], inc=(j == 3))
                                    A(lambda e: e.activation(out=dstT.t[:, hq * 4:hq * 4 + 4, :], in_=ptq[hq].t[:], func=AF.Copy), R=[ptq[hq]], W=[dstT])
                            for g_ in range(2):
                                hsl = [(j, g_ * 4 + j, slice((g_ * 4 + j) * 128, (g_ * 4 + j + 1) * 128)) for j in range(4)]
                                for (j, h, hs) in hsl:
                                    M(lambda e: e.matmul(psc.t[:, j, :], lhsT=kTt.t[:, h, :], rhs=qTt.t[:, h, :], start=True, stop=True), R=[kTt, qTt], W=[psc], inc=(j == 3))
                                V(lambda e: e.tensor_tensor(out=scm4.t[:], in0=psc.t[:], in1=cst.t[:, C_DH + g_ * 512:C_DH + (g_ + 1) * 512].rearrange("p (a b) -> p a b", a=4), op=ALU.mult), R=[psc, cst], W=[scm4])
                                for (j, h, hs) in hsl:
                                    M(lambda e: e.matmul(po.t[:, j, :], lhsT=scm4.t[:, j, :], rhs=rv.t[:, hs], start=True, stop=True), R=[scm4, rv], W=[po], inc=(j == 3))
                                for (j, h, hs) in hsl:
                                    M(lambda e: e.matmul(pi_.t[:, j, :], lhsT=qTt.t[:, h, :], rhs=Sb4[g_].t[:, j, :], start=True, stop=True), R=[qTt, Sb4[g_]], W=[pi_], inc=(j == 3))
                                for (j, h, hs) in hsl:
                                    M(lambda e: e.matmul(pkv.t[:, j, :], lhsT=k2.t[:, hs], rhs=rv.t[:, hs], start=True, stop=True), R=[k2, rv], W=[pkv], inc=(j == 3))
                                V(lambda e: e.tensor_tensor(out=S4[g_].t[:], in0=S4[g_].t[:], in1=cst.t[:, C_G128 + g_ * 4:C_G128 + g_ * 4 + 4].unsqueeze(2).to_broadcast([128, 4, 128]), op=ALU.mult), R=[S4[g_], cst], W=[S4[g_]])
                                V(lambda e: e.tensor_tensor(out=S4[g_].t[:], in0=S4[g_].t[:], in1=pkv.t[:], op=ALU.add), R=[S4[g_], pkv], W=[S4[g_]])
                                A(lambda e: e.activation(out=Sb4[g_].t[:], in_=S4[g_].t[:], func=AF.Copy), R=[S4[g_]], W=[Sb4[g_]])
                                A(lambda e: e.activation(out=pos4.t[:], in_=po.t[:], func=AF.Copy), R=[po], W=[pos4])
                                V(lambda e: e.tensor_tensor(out=o_.t[:, g_ * 4:g_ * 4 + 4, :], in0=pi_.t[:], in1=cst.t[:, C_GPOW + g_ * 4:C_GPOW + g_ * 4 + 4].unsqueeze(2).to_broadcast([128, 4, 128]), op=ALU.mult), R=[pi_, cst], W=[o_])
                                V(lambda e: e.tensor_tensor(out=o_.t[:, g_ * 4:g_ * 4 + 4, :], in0=o_.t[:, g_ * 4:g_ * 4 + 4, :], in1=pos4.t[:], op=ALU.add), R=[o_, pos4], W=[o_])
                        else:
                            for h in range(8):
                                hs = slice(h * 128, (h + 1) * 128)
                                M(lambda e: e.transpose(out=ptw.t[:], in_=qr.t[:NS, hs], identity=idf[:NS, :NS]), R=[qr, cst], W=[ptw])
                                V(lambda e: e.tensor_copy(out=qs.t[:], in_=ptw.t[:]), R=[ptw], W=[qs])
                                V(lambda e: e.tensor_tensor(out=Qbd.t[:], in0=qs.t[:].unsqueeze(1).to_broadcast([128, NS, NS]),
                                                            in1=cst.t[:, C_E16:C_E16 + 256].rearrange("p (a b) -> p a b", a=NS), op=ALU.mult), R=[qs, cst], W=[Qbd])
                                V(lambda e: e.tensor_tensor(out=Vbd.t[:], in0=rv.t[:NS, hs].unsqueeze(1).to_broadcast([NS, NS, 128]),
                                                            in1=cst.t[:NS, C_ID:C_ID + NS].unsqueeze(2).to_broadcast([NS, NS, 128]), op=ALU.mult), R=[rv, cst], W=[Vbd])
                                for b in range(NS):
                                    DMA("sp", S0.t[:], ret0[l, b, h], "r1", W=[S0])
                                    M(lambda e: e.matmul(pkv.t[:, 0, :], lhsT=kr.t[:NS, hs], rhs=Vbd.t[:, b, :], start=True, stop=True), R=[kr, Vbd], W=[pkv])
                                    V(lambda e: e.scalar_tensor_tensor(out=Sn.t[:], in0=S0.t[:], scalar=cst.t[:, C_G1 + h:C_G1 + h + 1], in1=pkv.t[:, 0, :], op0=ALU.mult, op1=ALU.add), R=[S0, cst, pkv], W=[Sn])
                                    DMA("sp", sret[l, b, h], Sn.t[:], "r2", R=[Sn])
                                    M(lambda e: e.matmul(po.t[:NS, 0, :], lhsT=Qbd.t[:, b, :], rhs=Sn.t[:], start=(b == 0), stop=(b == NS - 1)), R=[Qbd, Sn], W=[po])
                                V(lambda e: e.tensor_copy(out=o_.t[:NS, h, :], in_=po.t[:NS, 0, :]), R=[po], W=[o_])
                        V(lambda e: e.reduce_sum(out=st1.t[:n, 0:8], in_=o_.t[:n], axis=AX.X), R=[o_], W=[st1])
                        G(lambda e: e.tensor_tensor(out=sq.t[:n], in0=o_.t[:n], in1=o_.t[:n], op=ALU.mult), R=[o_], W=[sq])
                        V(lambda e: e.reduce_sum(out=st2.t[:n, 0:8], in_=sq.t[:n], axis=AX.X), R=[sq], W=[st2])
                        V(lambda e: e.tensor_scalar_mul(out=st1.t[:n, 0:8], in0=st1.t[:n, 0:8], scalar1=1.0 / 128), R=[st1], W=[st1])
                        V(lambda e: e.tensor_tensor(out=st1.t[:n, 8:16], in0=st1.t[:n, 0:8], in1=st1.t[:n, 0:8], op=ALU.mult), R=[st1], W=[st1])
                        V(lambda e: e.scalar_tensor_tensor(out=st2.t[:n, 0:8], in0=st2.t[:n, 0:8], scalar=1.0 / 128, in1=st1.t[:n, 8:16], op0=ALU.mult, op1=ALU.subtract), R=[st2, st1], W=[st2])
                        A(lambda e: e.activation(out=st2.t[:n, 0:8], in_=st2.t[:n, 0:8], func=AF.Sqrt, bias=EPS), R=[st2], W=[st2])
                        V(lambda e: e.reciprocal(out=st2.t[:n, 0:8], in_=st2.t[:n, 0:8]), R=[st2], W=[st2])
                        V(lambda e: e.tensor_tensor(out=o_.t[:n], in0=o_.t[:n], in1=st1.t[:n, 0:8].unsqueeze(2).to_broadcast([n, 8, 128]), op=ALU.subtract), R=[o_, st1], W=[o_])
                        V(lambda e: e.tensor_tensor(out=o_.t[:n], in0=o_.t[:n], in1=st2.t[:n, 0:8].unsqueeze(2).to_broadcast([n, 8, 128]), op=ALU.mult), R=[o_, st2], W=[o_])
                        G(lambda e: e.tensor_tensor(out=og.t[:n, :], in0=o_.t[:n].rearrange("p h k -> p (h k)"), in1=rg.t[:n, :], op=ALU.mult), R=[o_, rg], W=[og])
                        for hq in range(2):
                            for j in range(4):
                                hh_ = hq * 4 + j
                                M(lambda e: e.transpose(out=ptq[hq].t[:, j, :n], in_=og.t[:n, hh_ * 128:(hh_ + 1) * 128], identity=idb.t[:n, :n]), R=[og, idb], W=[ptq[hq]], inc=(j == 3))
                            A(lambda e: e.activation(out=ost.t[:, hq * 4:hq * 4 + 4, :n], in_=ptq[hq].t[:, :, :n], func=AF.Copy), R=[ptq[hq]], W=[ost])
                        DMA("sp", ART[:, r0:r0 + n].rearrange("(k p) t -> p k t", p=128), ost.t[:, :, :n], "r3", R=[ost])
                    for h in range(8):
                        DMA("sp", pret[l, h], S4[h // 4].t[:, h % 4, :], "r4", R=[S4[h // 4]])
                    S.barrier()

                with ExitStack() as ps:
                    TT = mk(ps, "s"); PP = mk(ps, "p")
                    mT = TT("mT", [128, 16, NT], BF16)
                    mTb = [Buf(f"mT{d_}") for d_ in range(16)]
                    with ExitStack() as ps1:
                        T1 = mk(ps1, "s"); P1 = mk(ps1, "p")
                        at = [T1(f"at{b}", [128, 8, NT], BF16) for b in range(3)]
                        for b, src in enumerate((A5T, AGT, ART)):
                            DMA("sp", at[b].t[:], src.rearrange("(k p) t -> p k t", p=128), f"at{b}", W=[at[b]])
                        wb = [[T1(f"wb{i}_{b}", [128, 8, 128], BF16) for b in range(3)] for i in range(2)]
                        gt = [T1(f"gt{i}", [128, 3, 512], BF16) for i in range(2)]
                        pmb = [[P1(f"pb{i}_{b}", [128, 512]) for b in range(3)] for i in range(2)]
                        t1 = [T1(f"t1_{i}", [128, 512]) for i in range(2)]; t2 = [T1(f"t2_{i}", [128, 512]) for i in range(2)]
                        ui = 0
                        for dt_ in range(16):
                            w = wb[dt_ % 2]
                            for b, wsrc in enumerate((w_s5o, w_glao, w_reto)):
                                DMA("pool", w[b].t[:], wsrc[l][:, dt_ * 128:(dt_ + 1) * 128].rearrange("(k p) e -> p k e", p=128), f"wb{dt_ % 2}", W=[w[b]])
                            for (b0, bn) in blocks:
                                i = ui % 2; ui += 1
                                for b in range(3):
                                    DMA("sp", gt[i].t[:, b, :bn], MGT[b * 2048 + dt_ * 128:b * 2048 + (dt_ + 1) * 128, b0:b0 + bn], f"gt{i}", W=[gt[i]])
                                    for kc in range(8):
                                        M(lambda e: e.matmul(pmb[i][b].t[:, :bn], lhsT=w[b].t[:, kc, :], rhs=at[b].t[:, kc, b0:b0 + bn], start=(kc == 0), stop=(kc == 7)),
                                          R=[w[b], at[b]], W=[pmb[i][b]], inc=(kc == 7))
                                V(lambda e: e.tensor_tensor(out=t1[i].t[:, :bn], in0=pmb[i][0].t[:, :bn], in1=gt[i].t[:, 0, :bn], op=ALU.mult), R=[pmb[i][0], gt[i]], W=[t1[i]])
                                V(lambda e: e.tensor_tensor(out=t2[i].t[:, :bn], in0=pmb[i][1].t[:, :bn], in1=gt[i].t[:, 1, :bn], op=ALU.mult), R=[pmb[i][1], gt[i]], W=[t2[i]])
                                V(lambda e: e.tensor_tensor(out=t1[i].t[:, :bn], in0=t1[i].t[:, :bn], in1=t2[i].t[:, :bn], op=ALU.add), R=[t1[i], t2[i]], W=[t1[i]])
                                V(lambda e: e.tensor_tensor(out=t2[i].t[:, :bn], in0=pmb[i][2].t[:, :bn], in1=gt[i].t[:, 2, :bn], op=ALU.mult), R=[pmb[i][2], gt[i]], W=[t2[i]])
                                V(lambda e: e.tensor_tensor(out=mT.t[:, dt_, b0:b0 + bn], in0=t1[i].t[:, :bn], in1=t2[i].t[:, :bn], op=ALU.add), R=[t1[i], t2[i]], W=[mTb[dt_]])
                        S.barrier()
                    wsl = [TT(f"w{i}", [128, 16, 512], BF16) for i in range(2)]
                    pm = [PP(f"pm{i}", [128, 512]) for i in range(4)]
                    xr = [TT(f"xr{i}", [128, 512]) for i in range(3)]
                    stf = [TT(f"stf{i}", [128, 512]) for i in range(3)]
                    ui = 0
                    for eb in range(4):
                        w = wsl[eb % 2]
                        DMA("pool", w.t[:], w_mix[l][:, eb * 512:(eb + 1) * 512].rearrange("(k p) e -> p k e", p=128), f"w{eb % 2}", W=[w])
                        for ti, (r0, n) in enumerate(tiles):
                            p = pm[ui % 4]; x_ = xr[ui % 3]; st = stf[ui % 3]; ci = ui % 3; ui += 1
                            DMA("sp", x_.t[:n, :], xrows(xsrc, ti)[:, eb * 512:(eb + 1) * 512], f"xr{ci}", W=[x_])
                            for kc in range(16):
                                M(lambda e: e.matmul(p.t[:n, :], lhsT=mT.t[:, kc, r0:r0 + n], rhs=w.t[:, kc, :], start=(kc == 0), stop=(kc == 15)),
                                  R=mTb + [w], W=[p], inc=(kc == 15))
                            V(lambda e: e.tensor_tensor(out=st.t[:n, :], in0=p.t[:n, :], in1=x_.t[:n, :], op=ALU.add), R=[p, x_], W=[st])
                            DMA("sp", XMID[r0:r0 + n, eb * 512:(eb + 1) * 512], st.t[:n, :], f"sf{ci}", R=[st])
                    S.barrier()

                with ExitStack() as ps:
                    TT = mk(ps, "s"); PP = mk(ps, "p")
                    hT = TT("hT2", [128, 16, NT], BF16)
                    hTb = [Buf(f"hT2{ti}") for ti in range(len(tiles))]
                    with ExitStack() as ps1:
                        norm_T(mk(ps1, "s"), mk(ps1, "p"), XMID, V_NF, hT, hTb, vecT, "n2")
                        S.barrier()
                    wg = [TT(f"wg{i}", [128, 16, 512], BF16) for i in range(2)]; wu = [TT(f"wu{i}", [128, 16, 512], BF16) for i in range(2)]
                    pg = [PP(f"pg{i}", [128, 512]) for i in range(4)]; pu = [PP(f"pu{i}", [128, 512]) for i in range(4)]
                    sg = [TT(f"sg{i}", [128, 512]) for i in range(4)]; stb = [TT(f"stb{i}", [128, 512], BF16) for i in range(6)]
                    ui = 0
                    for fb in range(DFF // 512):
                        i2 = fb % 2
                        DMA("pool", wg[i2].t[:], w_fg[l][:, fb * 512:(fb + 1) * 512].rearrange("(k p) e -> p k e", p=128), f"wg{i2}", W=[wg[i2]])
                        DMA("pool", wu[i2].t[:], w_fu[l][:, fb * 512:(fb + 1) * 512].rearrange("(k p) e -> p k e", p=128), f"wu{i2}", W=[wu[i2]])
                        for j in range(4):
                            ft = fb * 4 + j
                            for (b0, bn) in blocks:
                                i = ui % 4; si = ui % 6; ui += 1
                                bt = [hTb[t_] for t_ in blk_tiles(b0, bn)]
                                for kc in range(16):
                                    M(lambda e: e.matmul(pg[i].t[:, :bn], lhsT=wg[i2].t[:, kc, j * 128:(j + 1) * 128], rhs=hT.t[:, kc, b0:b0 + bn], start=(kc == 0), stop=(kc == 15)),
                                      R=bt + [wg[i2]], W=[pg[i]], inc=(kc == 15))
                                for kc in range(16):
                                    M(lambda e: e.matmul(pu[i].t[:, :bn], lhsT=wu[i2].t[:, kc, j * 128:(j + 1) * 128], rhs=hT.t[:, kc, b0:b0 + bn], start=(kc == 0), stop=(kc == 15)),
                                      R=bt + [wu[i2]], W=[pu[i]], inc=(kc == 15))
                                A(lambda e: e.activation(out=sg[i].t[:, :bn], in_=pg[i].t[:, :bn], func=AF.Silu), R=[pg[i]], W=[sg[i]])
                                V(lambda e: e.tensor_tensor(out=stb[si].t[:, :bn], in0=pu[i].t[:, :bn], in1=sg[i].t[:, :bn], op=ALU.mult), R=[pu[i], sg[i]], W=[stb[si]])
                                bts = blk_tiles(b0, bn)
                                if bn % 128 == 0:
                                    DMA("sp" if ui % 2 else "act", FFTL.rearrange("n p k t -> p n k t")[:, bts[0]:bts[0] + len(bts), ft, :],
                                        stb[si].t[:, :bn].rearrange("p (n t) -> p n t", t=128), f"sb{si}", R=[stb[si]])
                                else:
                                    DMA("sp", FFTL[bts[0], :, ft, :bn], stb[si].t[:, :bn], f"sb{si}", R=[stb[si]])
                    S.barrier()
                with ExitStack() as ps:
                    TT = mk(ps, "s"); PP = mk(ps, "p")
                    NF = DFF // 128
                    wd = TT("wd", [128, NF, 1024], BF16)
                    ff = [TT(f"ff{i}", [128, NF, 128], BF16) for i in range(3)]
                    pm = [PP(f"pm{i}", [128, 1024]) for i in range(3)]
                    xr = [TT(f"xr{i}", [128, 1024]) for i in range(2)]; stf = [TT(f"stf{i}", [128, 1024]) for i in range(2)]
                    ui = 0
                    for db in range(2):
                        for hf in range(2):
                            DMA("pool", wd.t[:, :, hf * 512:(hf + 1) * 512], w_fd[l][:, db * 1024 + hf * 512:db * 1024 + (hf + 1) * 512].rearrange("(k p) e -> p k e", p=128), "w0", W=[wd])
                        for ti, (r0, n) in enumerate(tiles):
                            p = pm[ui % 3]; x_ = xr[ui % 2]; st = stf[ui % 2]; ci = ui % 2; f_ = ff[ui % 3]; fi_ = ui % 3; ui += 1
                            DMA("sp", f_.t[:, :, :n], FFTL[ti][:, :, :n], f"ff{fi_}", W=[f_])
                            DMA("act", x_.t[:n, :], XMID[r0:r0 + n, db * 1024:(db + 1) * 1024], f"xr{ci}", W=[x_])
                            for hf in range(2):
                                for kc in range(NF):
                                    M(lambda e: e.matmul(p.t[:n, hf * 512:(hf + 1) * 512], lhsT=f_.t[:, kc, :n], rhs=wd.t[:, kc, hf * 512:(hf + 1) * 512], start=(kc == 0), stop=(kc == NF - 1)),
                                      R=[f_, wd], W=[p], inc=(kc == NF - 1 and hf == 1))
                            V(lambda e: e.tensor_tensor(out=st.t[:n, :], in0=p.t[:n, :], in1=x_.t[:n, :], op=ALU.add), R=[p, x_], W=[st])
                            DMA("sp", X1[r0:r0 + n, db * 1024:(db + 1) * 1024], st.t[:n, :], f"sf{ci}", R=[st])
                    S.barrier()
        with ExitStack() as ps:
            TT = mk(ps, "s")
            nf = TT("nf", [128, D]); junk = TT("fjunk", [128, D], BF16)
            DMA("sp", nf.t[:], nfin, "nf", W=[nf])
            xt = [TT(f"fxt{i}", [128, D]) for i in range(2)]; yo = [TT(f"fyo{i}", [128, D]) for i in range(2)]
            ss = [TT(f"fss{i}", [128, 16]) for i in range(2)]; rs = [TT(f"frs{i}", [128, 16]) for i in range(2)]
            for ti, (r0, n) in enumerate(tiles):
                s_ = ti % 2
                DMA("sp", xt[s_].t[:n, :], X1[r0:r0 + n, :], f"xr{s_}", W=[xt[s_]])
                V(lambda e: e.memset(ss[s_].t[:], 0.0), W=[ss[s_]])
                A(lambda e: e.activation(out=junk.t[:n, :], in_=xt[s_].t[:n, :], func=AF.Square, accum_out=ss[s_].t[:n, 0:1]), R=[xt[s_]], W=[junk, ss[s_]])
                A(lambda e: e.activation(out=rs[s_].t[:n, 0:1], in_=ss[s_].t[:n, 0:1], func=AF.Sqrt, scale=1.0 / D, bias=EPS), R=[ss[s_]], W=[rs[s_]])
                V(lambda e: e.reciprocal(out=rs[s_].t[:n, 0:1], in_=rs[s_].t[:n, 0:1]), R=[rs[s_]], W=[rs[s_]])
                V(lambda e: e.scalar_tensor_tensor(out=yo[s_].t[:n, :], in0=xt[s_].t[:n, :], scalar=rs[s_].t[:n, 0:1], in1=nf.t[:n, :], op0=ALU.mult, op1=ALU.mult),
                  R=[xt[s_], rs[s_], nf], W=[yo[s_]])
                DMA("sp", (yp[r0:r0 + n, :] if ti < NPT else ys[:, :]), yo[s_].t[:n, :], f"sf{s_}", R=[yo[s_]])
            S.barrier()
        S.barrier()
    return nc


def make_consts():
    c = np.zeros((128, C_N), np.float32)
    c[:, C_ID:C_ID + 128] = np.eye(128)
    for p in range(64):
        c[64 + p, C_J + p] = -1.0
        c[p, C_J + 64 + p] = 1.0
    s = np.arange(128)[:, None]; t = np.arange(128)[None, :]
    c[:, C_TRI:C_TRI + 128] = (t >= s)
    for gg in range(8):
        c[gg * 16:(gg + 1) * 16, C_GM + gg] = 1.0
    c[:64, C_SGN] = -1.0; c[64:, C_SGN] = 1.0
    for b in range(16):
        c[:, C_E16 + b * 16 + b] = 1.0
    c[:, C_D0:C_D0 + 512] = 1.0
    c[:, C_D0:C_D0 + 512:128] = 0.0
    lg = np.log1p(-np.power(2.0, -5.0 - np.arange(8, dtype=np.float64)))
    for h in range(8):
        c[:, C_DH + h * 128:C_DH + (h + 1) * 128] = np.where(t >= s, np.exp(lg[h] * (t - s)), 0.0)
        c[:, C_GPOW + h] = np.exp(lg[h] * (np.arange(128) + 1))
        c[:, C_KDEC + h] = np.exp(lg[h] * (127 - np.arange(128)))
        c[:, C_G128 + h] = np.exp(lg[h] * 128)
        c[:, C_G1 + h] = np.exp(lg[h])
    return c


def make_rot(NP):
    inv = (np.float32(1.0) / (np.float32(10000.0) ** np.linspace(0.0, 1.0, 64, dtype=np.float32))).astype(np.float32)
    pos = np.concatenate([np.arange(NP, dtype=np.float32), np.full(NS, 16384.0, np.float32)])
    ang = (pos[:, None] * inv[None, :]).astype(np.float32)
    return np.concatenate([np.cos(ang), np.sin(ang)], axis=1).astype(np.float32)


def prep_shared(inp, DEPTH):
    f = lambda a: np.ascontiguousarray(np.asarray(a, dtype=np.float32))
    a_re = f(inp["s5_a_re"]); a_im = f(inp["s5_a_im"]); ldt = f(inp["s5_log_dt"])
    s5dup = np.zeros((DEPTH, 3, 128, 64), np.float32); s5row = np.zeros((DEPTH, 3, NS, 4096), np.float32)
    vecs = np.zeros((DEPTH, V_N, 128), np.float32)
    for l in range(DEPTH):
        s5dup[l, 0] = np.concatenate([a_re[l].T, a_re[l].T], 0)
        s5dup[l, 1] = np.concatenate([a_im[l].T, a_im[l].T], 0)
        s5dup[l, 2] = np.broadcast_to(ldt[l][None, :], (128, 64))
        s5row[l, 0] = np.broadcast_to(a_re[l].reshape(1, 4096), (NS, 4096))
        s5row[l, 1] = np.broadcast_to(a_im[l].reshape(1, 4096), (NS, 4096))
        s5row[l, 2] = np.broadcast_to(np.repeat(ldt[l], 64)[None, :], (NS, 4096))
        vecs[l, V_NM:V_NM + 16] = f(inp["norm_mix"])[l].reshape(16, 128)
        vecs[l, V_NF:V_NF + 16] = f(inp["norm_ffn"])[l].reshape(16, 128)
        vecs[l, V_SD:V_SD + 8] = f(inp["s5_d"])[l].reshape(8, 128)
        vecs[l, V_BG:V_BG + 8] = f(inp["s5_b_glu"])[l].reshape(8, 128)
        vecs[l, V_GB:V_GB + 4] = f(inp["gla_b_gate"])[l].reshape(4, 128)
    sh = {
        "w_in": f(inp["w_in"])[:DEPTH], "s5dup": s5dup, "s5row": s5row,
        "s5b_re": f(inp["s5_b_re"])[:DEPTH], "s5b_im": f(inp["s5_b_im"])[:DEPTH],
        "s5c_re": f(inp["s5_c_re"])[:DEPTH].reshape(DEPTH, 1024, 64), "s5c_im": f(inp["s5_c_im"])[:DEPTH].reshape(DEPTH, 1024, 64),
        "w_glu": f(inp["s5_w_glu"])[:DEPTH], "w_s5o": f(inp["s5_w_out"])[:DEPTH], "w_gate": f(inp["gla_w_gate"])[:DEPTH],
        "gnorm": np.ascontiguousarray(np.broadcast_to(f(inp["gla_norm"])[:DEPTH, None, :], (DEPTH, 128, 256))),
        "w_glao": f(inp["gla_w_out"])[:DEPTH], "w_reto": f(inp["ret_w_out"])[:DEPTH], "w_mix": f(inp["w_mix_out"])[:DEPTH],
        "w_fg": f(inp["w_ffn_gate"])[:DEPTH], "w_fu": f(inp["w_ffn_up"])[:DEPTH], "w_fd": f(inp["w_ffn_down"])[:DEPTH],
        "vecs": vecs, "nfin": np.ascontiguousarray(np.broadcast_to(f(inp["norm_final"])[None, :], (128, D))),
        "consts": make_consts(),
    }
    return sh


def prep_core(inp, sh, seq, srow0, NPT, DEPTH):
    f = lambda a: np.ascontiguousarray(np.asarray(a, dtype=np.float32))
    NP = NPT * 128
    m = dict(sh)
    m["xp"] = f(inp["x_prompt"][seq, :NP])
    m["xs"] = f(inp["x_sample"][srow0:srow0 + NS, 0])
    m["s5r0"] = f(inp["state_s5_re"][:DEPTH, srow0:srow0 + NS]).reshape(DEPTH, NS, 4096)
    m["s5i0"] = f(inp["state_s5_im"][:DEPTH, srow0:srow0 + NS]).reshape(DEPTH, NS, 4096)
    m["gla0"] = f(inp["state_gla"][:DEPTH, srow0:srow0 + NS])
    m["ret0"] = f(inp["state_ret"][:DEPTH, srow0:srow0 + NS])
    m["rot"] = make_rot(NP)
    return m


_NC_CACHE = {}


def kernel(**inputs):
    NPT, DEPTH, NCORES = 16, 2, 8
    if "nc" not in _NC_CACHE:
        _NC_CACHE["nc"] = build(NPT, DEPTH)
    nc = _NC_CACHE["nc"]
    sh = prep_shared(inputs, DEPTH)
    in_maps = [prep_core(inputs, sh, c % 4, c * NS, NPT, DEPTH) for c in range(NCORES)]
    res = run_bass_kernel_spmd(nc, in_maps, core_ids=list(range(NCORES)))
    r = res.results
    g = lambda c, k: np.asarray(r[c][k], dtype=np.float32)
    y_prompt = np.stack([g(c, "yp") for c in range(4)], 0)
    y_sample = np.concatenate([g(c, "ys") for c in range(NCORES)], 0)[:, None, :]
    ps5 = np.stack([g(c, "ps5") for c in range(4)], 1)
    pgla = np.stack([g(c, "pgla") for c in range(4)], 1)
    pret = np.stack([g(c, "pret") for c in range(4)], 1)
    ss5 = np.concatenate([g(c, "ss5") for c in range(NCORES)], 1)
    sgla = np.concatenate([g(c, "sgla") for c in range(NCORES)], 1)
    sret = np.concatenate([g(c, "sret") for c in range(NCORES)], 1)
    return (y_prompt, y_sample,
            np.ascontiguousarray(ps5[..., :64]), np.ascontiguousarray(ps5[..., 64:]), pgla, pret,
            np.ascontiguousarray(ss5[..., :64]), np.ascontiguousarray(ss5[..., 64:]), sgla, sret)
```
